# Optimizing a Trainium2 kernel written in Bass

```python
import jax, jax.numpy as jnp
from jax import lax
import numpy as np

D_MODEL = 1024
BATCH = 8
SEQ = 2048
DEPTH = 2

D_MIX = 2 * D_MODEL
CONV_W = D_MIX // 4
CONV_K = 31
RET_W = D_MIX // 4
RET_HEADS = 4
RET_HD = RET_W // RET_HEADS
RET_CHUNK = 128
MLA_W = D_MIX // 2
MLA_HEADS = 8
MLA_V_HD = MLA_W // MLA_HEADS
MLA_NOPE = 128
MLA_ROPE = 64
MLA_Q_RANK = 384
MLA_KV_RANK = 256
ATTN_BLOCK = 128
ROPE_BASE = 10000.0
EPS = 1e-6

OFF_RET = 2 * CONV_W
OFF_QLAT = OFF_RET + 3 * RET_W
OFF_KVLAT = OFF_QLAT + MLA_Q_RANK
OFF_KROPE = OFF_KVLAT + MLA_KV_RANK
OFF_GATE = OFF_KROPE + MLA_ROPE
N_IN = OFF_GATE + D_MIX
SPLIT_POINTS = (OFF_RET, OFF_QLAT, OFF_KVLAT, OFF_KROPE, OFF_GATE)

kernel_name = "hybrid_conv_retention_mla_encoder"


def rmsnorm(x, g):
    xf = x.astype(jnp.float32)
    y = xf * lax.rsqrt(jnp.mean(xf * xf, axis=-1, keepdims=True) + EPS)
    return (y * g.astype(jnp.float32)).astype(x.dtype)


def rope_tables(seq, dim, dtype):
    inv = 1.0 / (ROPE_BASE ** (jnp.arange(0, dim, 2, dtype=jnp.float32) / dim))
    ang = jnp.arange(seq, dtype=jnp.float32)[:, None] * inv[None, :]
    ang = jnp.concatenate([ang, ang], axis=-1)
    return jnp.cos(ang).astype(dtype), jnp.sin(ang).astype(dtype)


def apply_rope(x, cos, sin):
    x1, x2 = jnp.split(x, 2, axis=-1)
    return x * cos + jnp.concatenate([-x2, x1], axis=-1) * sin


def conv_module(u, dw_w, dw_b, ln_g, ln_b):
    a, b = jnp.split(u, 2, axis=-1)
    h = a * jax.nn.sigmoid(b)
    h = lax.conv_general_dilated(
        h, dw_w[:, None, :].astype(h.dtype), window_strides=(1,),
        padding=[(CONV_K // 2, CONV_K // 2)],
        dimension_numbers=('NWC', 'WIO', 'NWC'),
        feature_group_count=CONV_W) + dw_b.astype(h.dtype)
    hf = h.astype(jnp.float32)
    mu = jnp.mean(hf, axis=-1, keepdims=True)
    var = jnp.mean(jnp.square(hf - mu), axis=-1, keepdims=True)
    hn = (hf - mu) * lax.rsqrt(var + EPS) * ln_g.astype(jnp.float32) + ln_b.astype(jnp.float32)
    return jax.nn.silu(hn).astype(u.dtype)


def retention_scan(q, k, v, log_g, strict):
    B, H, S, D = q.shape
    C = RET_CHUNK
    N = S // C
    qc = q.reshape(B, H, N, C, D)
    kc = k.reshape(B, H, N, C, D)
    vc = v.reshape(B, H, N, C, D)
    idx = jnp.arange(C, dtype=jnp.float32)
    diff = idx[:, None] - idx[None, :]
    mask = (diff > 0) if strict else (diff >= 0)
    decay_in = jnp.where(mask[None], jnp.exp(log_g[:, None, None] * jnp.maximum(diff, 0.0)[None]), 0.0)
    s = jnp.einsum('bhnid,bhnjd->bhnij', qc, kc) * decay_in[None, :, None]
    inner = jnp.einsum('bhnij,bhnjd->bhnid', s, vc)
    k_dec = jnp.exp(log_g[:, None] * (C - 1 - idx)[None, :])
    kv = jnp.einsum('bhnjd,bhnje->nbhde', kc * k_dec[None, :, None, :, None], vc)
    chunk_dec = jnp.exp(log_g * C)[None, :, None, None]

    def step(state, kv_n):
        return chunk_dec * state + kv_n, state

    _, prev = lax.scan(step, jnp.zeros((B, H, D, D), q.dtype), kv)
    q_dec = jnp.exp(log_g[:, None] * (idx + 1.0)[None, :])
    cross = jnp.einsum('bhnid,nbhde->bhnie', qc * q_dec[None, :, None, :, None], prev)
    return (inner + cross).reshape(B, H, S, D)


def retention_branch(u_qkv, decay_logit, cos, sin):
    B, S, _ = u_qkv.shape
    q, k, v = jnp.split(u_qkv, 3, axis=-1)
    q = q.reshape(B, S, RET_HEADS, RET_HD)
    k = k.reshape(B, S, RET_HEADS, RET_HD)
    v = v.reshape(B, S, RET_HEADS, RET_HD)
    q = apply_rope(q, cos[:, None, :], sin[:, None, :])
    k = apply_rope(k, cos[:, None, :], sin[:, None, :]) * (RET_HD ** -0.5)
    q, k, v = [t.transpose(0, 2, 1, 3).astype(jnp.float32) for t in (q, k, v)]
    log_g = jax.nn.log_sigmoid(decay_logit.astype(jnp.float32))
    flip = lambda t: jnp.flip(t, axis=2)
    o = retention_scan(q, k, v, log_g[0], False) + flip(
        retention_scan(flip(q), flip(k), flip(v), log_g[1], True))
    mu = jnp.mean(o, axis=-1, keepdims=True)
    var = jnp.mean(jnp.square(o - mu), axis=-1, keepdims=True)
    o = (o - mu) * lax.rsqrt(var + EPS)
    return o.transpose(0, 2, 1, 3).reshape(B, S, RET_W).astype(u_qkv.dtype)


def mla_branch(q_lat, kv_lat, k_rope_raw, qa_g, w_uq, kva_g, w_ukv, cos, sin):
    B, S, _ = q_lat.shape
    q = (rmsnorm(q_lat, qa_g) @ w_uq).reshape(B, S, MLA_HEADS, MLA_NOPE + MLA_ROPE)
    q_nope = q[..., :MLA_NOPE]
    q_rope = apply_rope(q[..., MLA_NOPE:], cos[:, None, :], sin[:, None, :])
    kv = (rmsnorm(kv_lat, kva_g) @ w_ukv).reshape(B, S, MLA_HEADS, MLA_NOPE + MLA_V_HD)
    k_nope = kv[..., :MLA_NOPE]
    v = kv[..., MLA_NOPE:]
    k_rope = apply_rope(k_rope_raw, cos, sin)
    scale = (MLA_NOPE + MLA_ROPE) ** -0.5
    nb = S // ATTN_BLOCK

    def to_blocks(t):
        return t.reshape(B, nb, ATTN_BLOCK, *t.shape[2:]).swapaxes(0, 1)

    def attend(blk):
        qn, qr = blk
        s = (jnp.einsum('bqhd,bkhd->bhqk', qn, k_nope)
             + jnp.einsum('bqhr,bkr->bhqk', qr, k_rope))
        p = jax.nn.softmax(s.astype(jnp.float32) * scale, axis=-1).astype(v.dtype)
        return jnp.einsum('bhqk,bkhd->bqhd', p, v)

    o = lax.map(attend, (to_blocks(q_nope), to_blocks(q_rope)))
    return o.swapaxes(0, 1).reshape(B, S, MLA_W)


def setup_inputs(seed: int = 0) -> dict:
    key = jax.random.key(seed)
    ks = jax.random.split(key, 16)
    f32 = jnp.float32
    nrm = lambda k, shape, s: jax.random.normal(k, shape, f32) * s
    base_logit = jnp.log(2.0 ** (5.0 + jnp.arange(RET_HEADS, dtype=f32)) - 1.0)
    return {
        "x": nrm(ks[0], (BATCH, SEQ, D_MODEL), 1.0),
        "norm_g": 1.0 + nrm(ks[1], (DEPTH, D_MODEL), 0.02),
        "w_in": nrm(ks[2], (DEPTH, D_MODEL, N_IN), D_MODEL ** -0.5),
        "conv_dw_w": nrm(ks[3], (DEPTH, CONV_K, CONV_W), CONV_K ** -0.5),
        "conv_dw_b": nrm(ks[4], (DEPTH, CONV_W), 0.02),
        "conv_ln_g": 1.0 + nrm(ks[5], (DEPTH, CONV_W), 0.02),
        "conv_ln_b": nrm(ks[6], (DEPTH, CONV_W), 0.02),
        "ret_decay_logit": base_logit[None, None, :] + nrm(ks[7], (DEPTH, 2, RET_HEADS), 0.1),
        "mla_qa_g": 1.0 + nrm(ks[8], (DEPTH, MLA_Q_RANK), 0.02),
        "mla_w_uq": nrm(ks[9], (DEPTH, MLA_Q_RANK, MLA_HEADS * (MLA_NOPE + MLA_ROPE)), MLA_Q_RANK ** -0.5),
        "mla_kva_g": 1.0 + nrm(ks[10], (DEPTH, MLA_KV_RANK), 0.02),
        "mla_w_ukv": nrm(ks[11], (DEPTH, MLA_KV_RANK, MLA_HEADS * (MLA_NOPE + MLA_V_HD)), MLA_KV_RANK ** -0.5),
        "w_out": nrm(ks[12], (DEPTH, D_MIX, D_MODEL), D_MIX ** -0.5),
        "final_g": 1.0 + nrm(ks[13], (D_MODEL,), 0.02),
    }


def reference(x, norm_g, w_in, conv_dw_w, conv_dw_b, conv_ln_g, conv_ln_b, ret_decay_logit,
              mla_qa_g, mla_w_uq, mla_kva_g, mla_w_ukv, w_out, final_g):
    S = x.shape[1]
    cos_r, sin_r = rope_tables(S, RET_HD, x.dtype)
    cos_m, sin_m = rope_tables(S, MLA_ROPE, x.dtype)
    for l in range(DEPTH):
        h = rmsnorm(x, norm_g[l])
        u = h @ w_in[l]
        u_conv, u_ret, u_q, u_kv, u_kr, u_gate = jnp.split(u, SPLIT_POINTS, axis=-1)
        y_conv = conv_module(u_conv, conv_dw_w[l], conv_dw_b[l], conv_ln_g[l], conv_ln_b[l])
        y_ret = retention_branch(u_ret, ret_decay_logit[l], cos_r, sin_r)
        y_mla = mla_branch(u_q, u_kv, u_kr, mla_qa_g[l], mla_w_uq[l], mla_kva_g[l], mla_w_ukv[l], cos_m, sin_m)
        y = jnp.concatenate([y_conv, y_ret, y_mla], axis=-1) * jax.nn.silu(u_gate)
        x = x + y @ w_out[l]
    return rmsnorm(x, final_g)
```

```python
import numpy as np
import concourse.bass as bass
import concourse.mybir as mybir
from concourse.bass_utils import run_bass_kernel_spmd

F32 = mybir.dt.float32
BF16 = mybir.dt.bfloat16
AF = mybir.ActivationFunctionType
ALU = mybir.AluOpType

D_MODEL = 1024
SEQ = 2048
NT = 16
NTB = 4
DEPTH = 2
CONV_K = 31
OFF_RET = 1024
OFF_QLAT = 2560
OFF_KVLAT = 2944
OFF_KROPE = 3200
OFF_GATE = 3264
N_IN = 5312
EPS = 1e-6
RET_SCALE = 128 ** -0.5
MLA_SCALE = 192 ** -0.5
NPT = 5

P_NG, P_CW, P_CB, P_LG, P_LB, P_QG, P_KG, P_DL = 0, 8, 132, 136, 140, 144, 147, 149
PL = 160
P_FG = 2 * PL
NP = P_FG + 8
C_ID, C_P1, C_P2, C_I1, C_I2, C_KJ = 0, 128, 256, 384, 512, 640
NCST = 642


class Buf:
    __slots__ = ("name", "w", "r")

    def __init__(self, name):
        self.name = name
        self.w = None
        self.r = []


class Op:
    __slots__ = ("eng", "fn", "deps", "marked", "lane", "real")

    def __init__(self, eng, fn, deps):
        self.eng = eng
        self.fn = fn
        self.deps = deps
        self.marked = False
        self.lane = None
        self.real = fn is not None


class Sched:
    ENGS = ("pe", "act", "dve", "pool", "sp")

    def __init__(self, nc):
        self.nc = nc
        self.ops = {e: [] for e in self.ENGS}
        self.lanes = {}
        self.sems = {e: nc.alloc_semaphore("done_" + e) for e in self.ENGS}
        self.last_real = {e: None for e in self.ENGS}

    def _collect(self, eng, reads, writes):
        deps = set()
        for b in reads:
            if b.w is not None:
                deps.add(b.w)
        for b in writes:
            if b.w is not None:
                deps.add(b.w)
            for ev in b.r:
                if ev[0] == "e" and ev[1] == eng:
                    continue
                deps.add(ev)
        if eng == "pe":
            deps = {d for d in deps if not (d[0] == "e" and d[1] == "pe")}
        return deps

    @staticmethod
    def _register(ev, reads, writes):
        for b in reads:
            b.r.append(ev)
        for b in writes:
            b.w = ev
            b.r = []

    def op(self, eng, fn, reads=(), writes=()):
        deps = self._collect(eng, reads, writes)
        o = Op(eng, fn, deps)
        self.ops[eng].append(o)
        ev = ("e", eng, len(self.ops[eng]) - 1)
        self.last_real[eng] = ev
        self._register(ev, reads, writes)
        return ev

    def dma(self, queue, lane, fn, n, reads=(), writes=()):
        if lane not in self.lanes:
            self.lanes[lane] = {"sem": self.nc.alloc_semaphore("ln_" + lane), "val": 0}
        L = self.lanes[lane]
        deps = self._collect(queue, reads, writes)
        if L["val"] > 0:
            deps.add(("l", lane, L["val"]))
        L["val"] += 16 * n
        o = Op(queue, fn, deps)
        o.lane = lane
        o.real = False
        self.ops[queue].append(o)
        ev = ("l", lane, L["val"])
        self._register(ev, reads, writes)
        return ev

    def wait(self, eng, events):
        o = Op(eng, None, set(events))
        self.ops[eng].append(o)

    def barrier(self):
        evs = [ev for ev in self.last_real.values() if ev is not None]
        for ln, L in self.lanes.items():
            if L["val"]:
                evs.append(("l", ln, L["val"]))
        for e in self.ENGS:
            self.wait(e, [ev for ev in evs if not (ev[0] == "e" and ev[1] == e)])

    def emit(self):
        nc = self.nc
        for e in self.ENGS:
            for o in self.ops[e]:
                for d in o.deps:
                    if d[0] == "e":
                        tgt = self.ops[d[1]][d[2]]
                        assert tgt.real
                        tgt.marked = True
        cnt = {}
        for e in self.ENGS:
            c = 0
            arr = []
            for o in self.ops[e]:
                if o.marked:
                    c += 1
                arr.append(c)
            cnt[e] = arr
        self.stats = {}

        def run(e):
            def body(eng):
                waited = {}
                nwait = 0
                for o in self.ops[e]:
                    need = {}
                    for d in o.deps:
                        if d[0] == "e":
                            key = ("e", d[1])
                            val = cnt[d[1]][d[2]]
                        else:
                            key = ("l", d[1])
                            val = d[2]
                        if val > need.get(key, 0):
                            need[key] = val
                    for key, val in need.items():
                        if waited.get(key, 0) < val:
                            sem = self.sems[key[1]] if key[0] == "e" else self.lanes[key[1]]["sem"]
                            eng.wait_ge(sem, val)
                            waited[key] = val
                            nwait += 1
                    if o.lane is not None:
                        o.fn(eng, self.lanes[o.lane]["sem"])
                    elif o.fn is not None:
                        ins = o.fn(eng)
                        if o.marked:
                            ins.then_inc(self.sems[e], 1)
                self.stats[e] = (len(self.ops[e]), nwait, cnt[e][-1] if cnt[e] else 0)
            return body

        with nc.Block() as block:
            block.tensor(run("pe"))
            block.scalar(run("act"))
            block.vector(run("dve"))
            block.gpsimd(run("pool"))
            block.sync(run("sp"))


def bc(ap, pos, count):
    dims = [list(d) for d in ap.ap]
    dims.insert(1 + pos, [0, count])
    return bass.AP(ap.tensor, ap.offset, dims)


class Prog:
    def __init__(self, n_layers=DEPTH, stages=("conv", "ret", "mla"), dbg=False, final=True):
        self.n_layers = n_layers
        self.stages = stages
        self.dbg = dbg
        self.final = final
        nc = self.nc = bass.Bass("TRN2", target_bir_lowering=False)
        self.s = Sched(nc)
        dr = nc.dram_tensor
        self.x_d = dr("x", [SEQ, D_MODEL], F32, kind="ExternalInput").ap()
        self.w_in_d = dr("w_in", [DEPTH, D_MODEL, N_IN], F32, kind="ExternalInput").ap()
        self.w_uq_d = dr("w_uq", [DEPTH, 384, 1536], F32, kind="ExternalInput").ap()
        self.w_ukv_d = dr("w_ukv", [DEPTH, 256, 2048], F32, kind="ExternalInput").ap()
        self.w_out_d = dr("w_out", [DEPTH, 2048, D_MODEL], F32, kind="ExternalInput").ap()
        self.par_d = dr("params", [128, NP], F32, kind="ExternalInput").ap()
        self.cst_d = dr("consts", [128, NCST], F32, kind="ExternalInput").ap()
        self.fgb_d = dr("fgb", [128, D_MODEL], F32, kind="ExternalInput").ap()
        self.ropeR_d = dr("ropeR", [2, 128, SEQ], F32, kind="ExternalInput").ap()
        self.ropeM_d = dr("ropeM", [2, 64, SEQ], F32, kind="ExternalInput").ap()
        self.out_d = dr("out", [SEQ, D_MODEL], F32, kind="ExternalOutput").ap()
        if dbg:
            self.dbg_d = dr("dbgY", [2 * DEPTH, 128, 4, SEQ], BF16, kind="ExternalOutput").ap()
        self.dbg_evs = []
        self.interleave = False
        self.nqb = 0
        self.sctr = 0

        self.off = 16384
        sb = self.sb
        self.X = sb("X", [128, NT, D_MODEL], F32)
        self.bX = [Buf("X%d" % t) for t in range(NT)]
        self.y_off = self.off
        self.Y = sb("Y", [128, 4, SEQ], BF16)
        self.bY = [[Buf("Y%d_%d" % (c, tb)) for tb in range(NTB)] for c in range(4)]
        self.HT = sb("HT", [128, NTB, 8, 512], BF16)
        self.bHT = [Buf("HT%d" % tb) for tb in range(NTB)]
        self.ring = [self.off + i * 4096 for i in range(4)]
        self.off += 4 * 4096
        self.bring = [Buf("ring%d" % i) for i in range(4)]
        self.ring_i = 0
        self.par = sb("par", [128, NP], F32)
        self.cst = sb("cst", [128, NCST], F32)
        self.identb = sb("identb", [128, 128], BF16)
        self.onesb = sb("onesb", [128, 128], BF16)
        self.onesf = sb("onesf", [128, 128], F32)
        self.ss = sb("ss", [128, NT], F32)
        self.bconst = Buf("const")
        self.bss = [Buf("ss%d" % i) for i in range(NTB)]
        self.scr0 = self.off
        self.scr_end = 16384 + 212992 - 64
        self.nview = 0
        self.ps = [nc.alloc_psum_tensor("ps%d" % i, [128, 512], F32) for i in range(7)]
        self.bps = [Buf("ps%d" % i) for i in range(7)]
        self.psb = nc.alloc_psum_tensor("psb", [128, 1024], BF16)
        self.bpsb = [Buf("psb0"), Buf("psb1")]
        self.psb_i = 0
        self.ps_rr = {}
        self.witems = []
        self.wi = 0
        self.issued = 0

    def sb(self, name, shape, dt):
        n = int(np.prod(shape[1:])) * (4 if dt == F32 else 2)
        t = self.nc.alloc_sbuf_tensor_at(name, list(shape), dt, offset=self.off)
        self.off += (n + 31) // 32 * 32
        return t

    def scr_reset(self):
        self.off = self.scr0

    def scr(self, name, shape, dt):
        self.nview += 1
        t = self.sb("%s_%d" % (name, self.nview), shape, dt)
        assert self.off <= self.scr_end, (name, self.off, self.scr_end)
        return t

    def view(self, off, name, shape, dt):
        self.nview += 1
        return self.nc.alloc_sbuf_tensor_at("%s_%d" % (name, self.nview), list(shape), dt, offset=off)

    def bank(self, group):
        i = self.ps_rr.get(group, 0)
        self.ps_rr[group] = i + 1
        b = group[i % len(group)]
        return self.ps[b], self.bps[b]

    def bbank(self):
        i = self.psb_i
        self.psb_i += 1
        h = i % 2
        return self.psb[:, h * 512:(h + 1) * 512], self.bpsb[h]

    def wplan(self, specs):
        self.witems.extend(specs)

    def _issue(self, idx):
        if idx >= len(self.witems):
            return
        slot = idx % 4
        off = self.ring[slot]
        parts = self.witems[idx](off)

        def fn(e, sem, parts=parts):
            for dst, src in parts:
                e.dma_start(out=dst, in_=src).then_inc(sem, 16)
        self.s.dma("pool", "w%d" % slot, fn, len(parts), writes=[self.bring[slot]])

    def wpop(self, k=1):
        while self.issued < min(self.wi + 4, len(self.witems)):
            self._issue(self.issued)
            self.issued += 1
        out = []
        for _ in range(k):
            slot = self.wi % 4
            self.wi += 1
            out.append((self.ring[slot], self.bring[slot]))
        return out[0] if k == 1 else out

    def wpop_keep(self):
        while self.issued < min(self.wi + 3, len(self.witems)):
            self._issue(self.issued)
            self.issued += 1
        slot = self.wi % 4
        self.wi += 1
        return self.ring[slot], self.bring[slot]

    def w_in_cols(self, l, col_ranges):
        tot = sum(n for _, n in col_ranges)

        def mk(off):
            t = self.view(off, "wv", [128, 8, tot], BF16)
            wv = self.w_in_d[l].rearrange("(kc p) n -> p kc n", p=128)
            parts = []
            o = 0
            for c0, n in col_ranges:
                parts.append((t[:, :, o:o + n], wv[:, :, c0:c0 + n]))
                o += n
            return parts
        return mk

    def w_out_item(self, l, g, half):
        def mk(off):
            t = self.view(off, "wo", [128, 4, 512], BF16)
            wv = self.w_out_d[l].rearrange("(c p) n -> p c n", p=128)
            return [(t[:], wv[:, g * 4:(g + 1) * 4, half * 512:(half + 1) * 512])]
        return mk

    def w_head_item(self, l, h):
        def mk(off):
            a = self.view(off, "wuq", [128, 3, 192], BF16)
            c = self.view(off + 1536, "wukv", [128, 2, 256], BF16)
            uq = self.w_uq_d[l].rearrange("(kc p) n -> p kc n", p=128)
            ukv = self.w_ukv_d[l].rearrange("(kc p) n -> p kc n", p=128)
            return [(a[:], uq[:, :, h * 192:(h + 1) * 192]), (c[:], ukv[:, :, h * 256:(h + 1) * 256])]
        return mk

    def plan_layer(self, l):
        items = []
        if "conv" in self.stages:
            for c in range(4):
                items.append(self.w_in_cols(l, [(c * 128, 128), (512 + c * 128, 128)]))
            for j in range(2):
                items.append(self.w_in_cols(l, [(OFF_GATE + j * 256, 256)]))
            for half in range(2):
                items.append(self.w_out_item(l, 0, half))
        if "ret" in self.stages:
            for j in range(2):
                items.append(self.w_in_cols(l, [(OFF_GATE + 512 + j * 256, 256)]))
            for h in range(4):
                items.append(self.w_in_cols(l, [(OFF_RET + 1024 + h * 128, 128)]))
                items.append(self.w_in_cols(l, [(OFF_RET + h * 128, 128), (OFF_RET + 512 + h * 128, 128)]))
            for half in range(2):
                items.append(self.w_out_item(l, 1, half))
        if "mla" in self.stages:
            items.append(self.w_in_cols(l, [(OFF_QLAT, 256)]))
            items.append(self.w_in_cols(l, [(OFF_QLAT + 256, 128)]))
            items.append(self.w_in_cols(l, [(OFF_KVLAT, 256)]))
            items.append(self.w_in_cols(l, [(OFF_KROPE - 64, 128)]))
            def gate_items(g):
                return [self.w_in_cols(l, [(OFF_GATE + 1024 + g * 512 + j * 256, 256)]) for j in range(2)]
            items += gate_items(0)
            items += [self.w_head_item(l, h) for h in range(5)]
            items += [self.w_out_item(l, 2, half) for half in range(2)]
            items += gate_items(1)
            items += [self.w_head_item(l, h) for h in range(5, 8)]
            items += [self.w_out_item(l, 3, half) for half in range(2)]
        self.wplan(items)

    def pcol(self, l, base, i, n=1):
        c = l * PL + base + i
        return self.par[:, c:c + n]

    def mm_group(self, out, pairs, reads, writes, first=True, last=True):
        def fn(e):
            ins = None
            n = len(pairs)
            for i, (lt, rh) in enumerate(pairs):
                ins = e.matmul(out, lhsT=lt, rhs=rh, start=(first and i == 0), stop=(last and i == n - 1))
            return ins
        return self.s.op("pe", fn, reads=reads, writes=writes)

    def setup(self):
        s = self.s
        for t in range(NT):
            s.dma("sp", "x%d" % (t % 4),
                  lambda e, sem, t=t: e.dma_start(out=self.X[:, t, :], in_=self.x_d[t * 128:(t + 1) * 128, :]).then_inc(sem, 16),
                  1, writes=[self.bX[t]])
            if t == 0:
                s.dma("sp", "cst", lambda e, sem: (e.dma_start(out=self.par[:], in_=self.par_d[:, :]).then_inc(sem, 16),
                                                   e.dma_start(out=self.cst[:], in_=self.cst_d[:, :]).then_inc(sem, 16)),
                      2, writes=[self.bconst])
        s.op("dve", lambda e: e.tensor_copy(out=self.identb[:], in_=self.cst[:, C_ID:C_ID + 128]),
             reads=[self.bconst], writes=[self.bconst])
        s.op("dve", lambda e: e.memset(self.onesb[:], 1.0), writes=[self.bconst])
        s.op("dve", lambda e: e.memset(self.onesf[:], 1.0), writes=[self.bconst])

    def row_rstd(self, tiles, junk, bjunk, extra_w=()):
        s = self.s
        bss = self.bss[tiles[0] // 4]
        for t in tiles:
            s.op("act", lambda e, t=t: e.activation(out=junk[:], in_=self.X[:, t, :], func=AF.Square,
                                                    accum_out=self.ss[:, t:t + 1]),
                 reads=[self.bX[t]], writes=[bss] + list(extra_w))
        t0, t1 = tiles[0], tiles[-1] + 1
        s.op("dve", lambda e: e.tensor_scalar(out=self.ss[:, t0:t1], in0=self.ss[:, t0:t1], scalar1=1.0 / D_MODEL,
                                              scalar2=EPS, op0=ALU.mult, op1=ALU.add),
             reads=[bss], writes=[bss])
        s.op("act", lambda e: e.activation(out=self.ss[:, t0:t1], in_=self.ss[:, t0:t1], func=AF.Ln),
             reads=[bss], writes=[bss])
        s.op("act", lambda e: e.activation(out=self.ss[:, t0:t1], in_=self.ss[:, t0:t1], func=AF.Exp, scale=-0.5),
             reads=[bss], writes=[bss])

    def stage_norm(self, l):
        s = self.s
        self.nview += 1
        yoff = self.y_off
        xs = [self.nc.alloc_sbuf_tensor_at("xs%d_%d" % (i, self.nview), [128, D_MODEL], F32, offset=yoff + i * 4096) for i in range(2)]
        bxs = [self.bY[0], self.bY[1]]
        junk = self.nc.alloc_sbuf_tensor_at("junk_%d" % self.nview, [128, D_MODEL], BF16, offset=yoff + 8192)
        bjunk = self.bY[2][0]
        idf = self.cst[:, C_ID:C_ID + 128]

        def scale(t):
            k, tb = t % 2, t // 4
            s.op("dve", lambda e: e.tensor_scalar_mul(out=xs[k][:], in0=self.X[:, t, :], scalar1=self.ss[:, t:t + 1]),
                 reads=[self.bX[t], self.bss[tb]], writes=bxs[k])

        def transp(t):
            k, tb, tt = t % 2, t // 4, t % 4
            banks = []
            for hb in range(2):
                pt, bpt = self.bank((3, 4, 5, 6))

                def tr(e, hb=hb, pt=pt):
                    ins = None
                    for c in range(4):
                        cc = hb * 4 + c
                        ins = e.transpose(out=pt[:, c * 128:(c + 1) * 128], in_=xs[k][:, cc * 128:(cc + 1) * 128], identity=idf)
                    return ins
                s.op("pe", tr, reads=bxs[k] + [self.bconst], writes=[bpt])
                banks.append((pt, bpt))
            return banks

        def evac(t, banks):
            tb, tt = t // 4, t % 4
            for hb, (pt, bpt) in enumerate(banks):
                gcol = self.pcol(l, P_NG, hb * 4, 4)
                s.op("dve", lambda e, pt=pt, hb=hb, gcol=gcol: e.tensor_tensor(
                    out=self.HT[:, tb, hb * 4:(hb + 1) * 4, tt * 128:(tt + 1) * 128],
                    in0=pt[:].rearrange("p (c t) -> p c t", c=4), in1=bc(gcol, 1, 128), op=ALU.mult),
                    reads=[bpt, self.bconst], writes=[self.bHT[tb]])

        for tb in range(NTB):
            self.row_rstd(list(range(tb * 4, tb * 4 + 4)), junk, bjunk, extra_w=self.bY[2][0:2])
        scale(0)
        for t in range(NT):
            banks = transp(t)
            if t + 1 < NT:
                scale(t + 1)
            evac(t, banks)

    def inproj_fm(self, wt, bw, col, M, tb, banks=(0, 1, 2, 3)):
        pt, bpt = self.bank(banks)
        pairs = [(wt[:, kc, col:col + M], self.HT[:, tb, kc, :]) for kc in range(8)]
        self.mm_group(pt[:M, :], pairs, reads=[bw, self.bHT[tb]], writes=[bpt])
        return pt, bpt

    def gates(self, l, nitems=2):
        s = self.s
        for j in range(nitems):
            off, bw = self.wpop()
            wt = self.view(off, "wg", [128, 8, 256], BF16)
            for cc in range(2):
                c = j * 2 + cc
                for tb in range(NTB):
                    pt, bpt = self.inproj_fm(wt, bw, cc * 128, 128, tb)
                    s.op("act", lambda e, pt=pt, c=c, tb=tb: e.activation(out=self.Y[:, c, tb * 512:(tb + 1) * 512], in_=pt[:], func=AF.Silu),
                         reads=[bpt], writes=[self.bY[c][tb]])

    def out_proj(self, l, g):
        s = self.s
        for half in range(2):
            off, bw = self.wpop()
            wo = self.view(off, "wo", [128, 4, 512], BF16)
            for t in range(NT):
                pt, bpt = self.bank((0, 1, 2, 3))
                tb = t // 4
                pairs = [(self.Y[:, c, t * 128:(t + 1) * 128], wo[:, c, :]) for c in range(4)]
                self.mm_group(pt[:], pairs, reads=[bw] + [self.bY[c][tb] for c in range(4)], writes=[bpt])
                s.op("dve", lambda e, pt=pt, t=t, half=half: e.tensor_tensor(
                    out=self.X[:, t, half * 512:(half + 1) * 512], in0=pt[:], in1=self.X[:, t, half * 512:(half + 1) * 512], op=ALU.add),
                    reads=[bpt, self.bX[t]], writes=[self.bX[t]])

    def dump(self, name, t, reads):
        if not self.dbg:
            return
        d = self.nc.dram_tensor("dbg_" + name, list(t.shape), t.dtype, kind="ExternalOutput").ap()
        ev = self.s.dma("sp", "dbg", lambda e, sem: e.dma_start(out=d, in_=t[:]).then_inc(sem, 16), 1, reads=reads)
        self.dbg_evs.append(ev)

    def dump_y(self, idx):
        if not self.dbg:
            return
        ev = self.s.dma("sp", "dbg", lambda e, sem: e.dma_start(out=self.dbg_d[idx], in_=self.Y[:]).then_inc(sem, 16), 1,
                        reads=[b for r in self.bY for b in r])
        self.dbg_evs.append(ev)

    def stage_conv(self, l):
        s = self.s
        self.scr_reset()
        GW = SEQ + 30
        glu_off = self.off
        glu = self.scr("glu", [128, 4, GW], BF16)
        bglu = [Buf("glu%d" % c) for c in range(4)]
        diag0 = self.scr("diag", [128, CONV_K, 128], BF16)
        cvo = self.scr("cvo", [128, 4, SEQ], F32)
        bcvo = [[Buf("cvo%d_%d" % (c, tb)) for tb in range(NTB)] for c in range(4)]
        sig = [self.scr("sig%d" % i, [128, 512], F32) for i in range(2)]
        bsig = [Buf("sig0"), Buf("sig1")]
        reg_off = self.off
        cvb = self.scr("cvb", [128, 4, 512], BF16)
        sq = self.scr("sq", [128, 4, 512], BF16)
        diag1 = self.view(reg_off, "diag1", [128, CONV_K, 128], BF16)
        diags = [diag0, diag1]
        bdiags = [Buf("diag0"), Buf("diag1")]
        bcvb = bsq = bdiags[1]
        mean = self.scr("mean", [128, 512], F32)
        msq = self.scr("msq", [128, 512], F32)
        rstd = self.scr("rstd", [128, 512], F32)
        bmean, bmsq, brstd = Buf("mean"), Buf("msq"), Buf("rstd")
        tmp, btmp = sig, bsig

        for c in range(4):
            s.op("pool", lambda e, c=c: e.memset(glu[:, c, 0:15], 0.0), writes=[bglu[c]])
            s.op("pool", lambda e, c=c: e.memset(glu[:, c, GW - 15:GW], 0.0), writes=[bglu[c]])
            off, bw = self.wpop()
            wt = self.view(off, "wc", [128, 8, 256], BF16)
            for tb in range(NTB):
                pa, bpa = self.inproj_fm(wt, bw, 0, 128, tb)
                pb, bpb = self.inproj_fm(wt, bw, 128, 128, tb)
                k = tb % 2
                s.op("act", lambda e, pb=pb, k=k: e.activation(out=sig[k][:], in_=pb[:], func=AF.Sigmoid),
                     reads=[bpb], writes=[bsig[k]])
                s.op("dve", lambda e, pa=pa, k=k, c=c, tb=tb: e.tensor_tensor(
                    out=glu[:, c, 15 + tb * 512:15 + (tb + 1) * 512], in0=pa[:], in1=sig[k][:], op=ALU.mult),
                    reads=[bpa, bsig[k]], writes=[bglu[c]])
        self.gates(l)
        for c in range(4):
            diag, bdiag = diags[c % 2], bdiags[c % 2]
            wc0 = l * PL + P_CW + c
            wb = self.par[:, wc0:wc0 + 1]
            wcols = bass.AP(wb.tensor, wb.offset, [list(wb.ap[0]), [4, CONV_K], [0, 128]])
            s.op("dve", lambda e, diag=diag, wcols=wcols: e.tensor_tensor(out=diag[:], in0=bc(self.identb[:], 0, CONV_K), in1=wcols, op=ALU.mult),
                 reads=[self.bconst], writes=[bdiag])
            for tb in range(NTB):
                pt, bpt = self.bank((0, 1, 2, 3))
                pairs = [(diag[:, k, :], glu[:, c, tb * 512 + k:tb * 512 + k + 512]) for k in range(CONV_K)]
                self.mm_group(pt[:], pairs, reads=[bdiag, bglu[c]], writes=[bpt])
                s.op("act", lambda e, pt=pt, c=c, tb=tb: e.activation(
                    out=cvo[:, c, tb * 512:(tb + 1) * 512], in_=pt[:], func=AF.Identity, bias=self.pcol(l, P_CB, c)),
                    reads=[bpt, self.bconst], writes=[bcvo[c][tb]])
        self.nview += 1
        nv = self.nview
        at = self.nc.alloc_sbuf_tensor_at
        ln_sets = [
            dict(cvb=cvb, sq=sq, mean=mean, msq=msq, rstd=rstd,
                 bcvb=Buf("cvb0"), bsq=Buf("sq0"), bmean=bmean, bmsq=bmsq, brstd=brstd, first=[bdiags[1]]),
            dict(cvb=at("cvb1_%d" % nv, [128, 4, 512], BF16, offset=glu_off), sq=at("sq1_%d" % nv, [128, 4, 512], BF16, offset=glu_off + 4096),
                 mean=at("mean1_%d" % nv, [128, 512], F32, offset=glu_off + 8320), msq=at("msq1_%d" % nv, [128, 512], F32, offset=glu_off + 10368),
                 rstd=at("rstd1_%d" % nv, [128, 512], F32, offset=glu_off + 12416),
                 bcvb=Buf("cvb1"), bsq=Buf("sq1"), bmean=Buf("mean1"), bmsq=Buf("msq1"), brstd=Buf("rstd1"), first=list(bglu)),
        ]
        def ln_stats(tb):
            sl = slice(tb * 512, (tb + 1) * 512)
            L = ln_sets[tb % 2]
            xw = L["first"] if tb < 2 else []
            cvb_, sq_, mean_, msq_, rstd_ = L["cvb"], L["sq"], L["mean"], L["msq"], L["rstd"]
            bcvb_, bsq_, bmean_, bmsq_, brstd_ = L["bcvb"], L["bsq"], L["bmean"], L["bmsq"], L["brstd"]
            s.op("act", lambda e: e.activation(out=cvb_[:], in_=cvo[:, :, sl], func=AF.Copy),
                 reads=[bcvo[c][tb] for c in range(4)], writes=[bcvb_] + xw)
            s.op("act", lambda e: e.activation(out=sq_[:], in_=cvo[:, :, sl], func=AF.Square),
                 reads=[bcvo[c][tb] for c in range(4)], writes=[bsq_] + xw)
            pm, bpm = self.bank((4, 5, 6))
            self.mm_group(pm[:], [(self.onesb[:], cvb_[:, c, :]) for c in range(4)], reads=[bcvb_, self.bconst], writes=[bpm])
            pq, bpq = self.bank((4, 5, 6))
            self.mm_group(pq[:], [(self.onesb[:], sq_[:, c, :]) for c in range(4)], reads=[bsq_, self.bconst], writes=[bpq])
            s.op("dve", lambda e: e.tensor_scalar_mul(out=mean_[:], in0=pm[:], scalar1=1.0 / 512), reads=[bpm], writes=[bmean_] + xw)
            s.op("dve", lambda e: e.tensor_tensor(out=msq_[:], in0=mean_[:], in1=mean_[:], op=ALU.mult), reads=[bmean_], writes=[bmsq_] + xw)
            s.op("dve", lambda e: e.scalar_tensor_tensor(out=rstd_[:], in0=pq[:], scalar=1.0 / 512, in1=msq_[:], op0=ALU.mult, op1=ALU.subtract),
                 reads=[bpq, bmsq_], writes=[brstd_] + xw)
            s.op("dve", lambda e: e.tensor_scalar_add(out=rstd_[:], in0=rstd_[:], scalar1=EPS), reads=[brstd_], writes=[brstd_])
            s.op("act", lambda e: e.activation(out=rstd_[:], in_=rstd_[:], func=AF.Ln), reads=[brstd_], writes=[brstd_])
            s.op("act", lambda e: e.activation(out=rstd_[:], in_=rstd_[:], func=AF.Exp, scale=-0.5), reads=[brstd_], writes=[brstd_])

        def ln_norm(tb):
            sl = slice(tb * 512, (tb + 1) * 512)
            L = ln_sets[tb % 2]
            mean_, rstd_, bmean_, brstd_ = L["mean"], L["rstd"], L["bmean"], L["brstd"]
            def gate(c):
                k = c % 2
                s.op("dve", lambda e: e.tensor_tensor(out=self.Y[:, c, sl], in0=tmp[k][:], in1=self.Y[:, c, sl], op=ALU.mult),
                     reads=[btmp[k], self.bY[c][tb]], writes=[self.bY[c][tb]])
            for c in range(4):
                k = c % 2
                s.op("dve", lambda e, c=c, k=k: e.tensor_tensor(out=tmp[k][:], in0=cvo[:, c, sl], in1=mean_[:], op=ALU.subtract),
                     reads=[bcvo[c][tb], bmean_], writes=[btmp[k]])
                s.op("dve", lambda e, k=k: e.tensor_tensor(out=tmp[k][:], in0=tmp[k][:], in1=rstd_[:], op=ALU.mult),
                     reads=[btmp[k], brstd_], writes=[btmp[k]])
                s.op("act", lambda e, c=c, k=k: e.activation(out=tmp[k][:], in_=tmp[k][:], func=AF.Silu,
                                                             scale=self.pcol(l, P_LG, c), bias=self.pcol(l, P_LB, c)),
                     reads=[btmp[k], self.bconst], writes=[btmp[k]])
                if c > 0:
                    gate(c - 1)
            gate(3)

        ln_stats(0)
        for tb in range(NTB):
            if tb + 1 < NTB:
                ln_stats(tb + 1)
            ln_norm(tb)
        self.dump_y(l * 4 + 0)
        self.s.barrier()
        self.out_proj(l, 0)

    def stage_ret(self, l):
        s = self.s
        self.scr_reset()
        scr = self.scr
        cstv = self.cst
        IB = (0, 1, 2, 3)
        RB = (4, 5, 6)
        tbl = [[scr("tbl", [128, 512], F32) for _ in range(2)] for _ in range(2)]
        btbl = [Buf("tbl0"), Buf("tbl1")]
        qT = [scr("qT", [128, SEQ], BF16) for _ in range(2)]
        kT = [scr("kT", [128, SEQ], BF16) for _ in range(2)]
        ktok = [scr("ktok", [128, NT, 128], BF16) for _ in range(2)]
        vtok = [scr("vtok", [128, NT, 128], BF16) for _ in range(2)]
        bqT = [[Buf("qT%d_%d" % (k, i)) for i in range(4)] for k in range(2)]
        bkT = [[Buf("kT%d_%d" % (k, i)) for i in range(4)] for k in range(2)]
        bktok = [[Buf("ktok%d_%d" % (k, i)) for i in range(4)] for k in range(2)]
        bvtok = [[Buf("vtok%d_%d" % (k, i)) for i in range(4)] for k in range(2)]
        kdec = scr("kdec", [128, NT, 128], BF16)
        bkdec = Buf("kdec")
        qf = scr("qf", [128, SEQ], BF16)
        qb = scr("qb", [128, SEQ], BF16)
        bqf = [Buf("qf%d" % i) for i in range(4)]
        bqb = [Buf("qb%d" % i) for i in range(4)]
        srun = [scr("srun", [128, 128], F32) for _ in range(2)]
        bsrun = [Buf("srun0"), Buf("srun1")]
        Sf = scr("Sf", [128, NT, 128], BF16)
        Sb = scr("Sb", [128, NT, 128], BF16)
        bSf, bSb = Buf("Sf"), Buf("Sb")
        ATb = [scr("ATb", [128, 4, 128], BF16) for _ in range(2)]
        onb = [scr("onb", [128, 4, 128], BF16) for _ in range(2)]
        st = [scr("st", [128, 4, 6], F32) for _ in range(2)]
        mv = [scr("mv", [128, 4, 2], F32) for _ in range(2)]
        rs = [scr("rs", [128, 4], F32) for _ in range(2)]
        bATb = [Buf("ATb0"), Buf("ATb1")]
        bonb = [Buf("onb0"), Buf("onb1")]
        bst = [Buf("st0"), Buf("st1")]
        lg = scr("lg", [128, 8], F32)
        gC = scr("gC", [128, 8], F32)
        KD = scr("KD", [128, 8], F32)
        DT = [scr("DT", [128, 128], F32) for _ in range(2)]
        AFm = scr("AFm", [128, 128], F32)
        ABm = scr("ABm", [128, 128], F32)
        dtmp = scr("dtmp", [128, 128], F32)
        bdec = Buf("dec")
        bhd = [Buf("hd0"), Buf("hd1")]
        t1 = scr("t1", [128, 512], F32)
        t2 = scr("t2", [128, 512], F32)
        bt1, bt2 = Buf("t1"), Buf("t2")
        wBk = scr("wBk", [128, 8, 128], BF16)
        bwBk = Buf("wBk")

        dl = self.par[:, l * PL + P_DL:l * PL + P_DL + 8]
        rd, wr = [self.bconst, bdec], [bdec]
        s.op("act", lambda e: e.activation(out=lg[:], in_=dl, func=AF.Exp, scale=-1.0), reads=rd, writes=wr)
        s.op("dve", lambda e: e.tensor_scalar_add(out=lg[:], in0=lg[:], scalar1=1.0), reads=rd, writes=wr)
        s.op("act", lambda e: e.activation(out=lg[:], in_=lg[:], func=AF.Ln), reads=rd, writes=wr)
        s.op("dve", lambda e: e.tensor_scalar_mul(out=lg[:], in0=lg[:], scalar1=-1.0), reads=rd, writes=wr)
        s.op("act", lambda e: e.activation(out=gC[:], in_=lg[:], func=AF.Exp, scale=128.0), reads=rd, writes=wr)
        for h in range(4):
            lf, lb = lg[:, h:h + 1], lg[:, 4 + h:5 + h]
            s.op("act", lambda e, h=h, lf=lf: e.activation(out=KD[:, h:h + 1], in_=cstv[:, C_KJ:C_KJ + 1], func=AF.Exp, scale=lf), reads=rd, writes=wr)
            s.op("act", lambda e, h=h, lb=lb: e.activation(out=KD[:, 4 + h:5 + h], in_=cstv[:, C_KJ + 1:C_KJ + 2], func=AF.Exp, scale=lb), reads=rd, writes=wr)

        self.gates(l)
        ntab = [0]

        def inproj_steps(h):
            k = h % 2
            st_ = {}
            steps = []

            def svpop():
                off, bw = self.wpop()
                st_["bwv"] = bw
                st_["wv"] = self.view(off, "wv", [128, 8, 128], BF16)
                st_["wBq"] = self.view(off + 2048, "wBq", [128, 8, 128], BF16)

            def svmm():
                bw, wv = st_["bwv"], st_["wv"]
                for tg in range(4):
                    pt, bpt = self.bank(IB)

                    def fv(e, tg=tg, pt=pt):
                        ins = None
                        for i in range(4):
                            for kc in range(8):
                                ins = e.matmul(pt[:, i * 128:(i + 1) * 128], lhsT=self.HT[:, tg, kc, i * 128:(i + 1) * 128], rhs=wv[:, kc, :],
                                               start=(kc == 0), stop=(kc == 7))
                        return ins
                    s.op("pe", fv, reads=[bw, self.bHT[tg]], writes=[bpt])
                    s.op("act", lambda e, tg=tg, pt=pt: e.activation(out=vtok[k][:, tg * 4:(tg + 1) * 4, :],
                                                                    in_=pt[:].rearrange("p (a b) -> p a b", a=4), func=AF.Copy),
                         reads=[bpt], writes=[bvtok[k][tg]])
            steps.append(svpop)

            def sw():
                off, bw = self.wpop_keep()
                st_["bw"] = bw
                st_["wA"] = wA = self.view(off, "wqk", [128, 8, 256], BF16)
                wBq, bwv = st_["wBq"], st_["bwv"]
                s.op("act", lambda e: e.activation(out=wBq[:, :, 0:64], in_=wA[:, :, 64:128], func=AF.Copy, scale=-1.0), reads=[bw], writes=[bwv])
                s.op("act", lambda e: e.activation(out=wBq[:, :, 64:128], in_=wA[:, :, 0:64], func=AF.Copy), reads=[bw], writes=[bwv])
                s.op("act", lambda e: e.activation(out=wBk[:, :, 0:64], in_=wA[:, :, 192:256], func=AF.Copy, scale=-1.0), reads=[bw], writes=[bwBk])
                s.op("act", lambda e: e.activation(out=wBk[:, :, 64:128], in_=wA[:, :, 128:192], func=AF.Copy), reads=[bw], writes=[bwBk])
            steps.append(sw)
            for tb in range(NTB):
                def stb(tb=tb):
                    bw, wA, wBq, bwv = st_["bw"], st_["wA"], st_["wBq"], st_["bwv"]
                    sl = slice(tb * 512, (tb + 1) * 512)
                    kk = ntab[0] % 2
                    ntab[0] += 1
                    s.dma("sp", "tb%d" % kk, lambda e, sem: (
                        e.dma_start(out=tbl[kk][0][:], in_=self.ropeR_d[0][:, sl]).then_inc(sem, 16),
                        e.dma_start(out=tbl[kk][1][:], in_=self.ropeR_d[1][:, sl]).then_inc(sem, 16)), 2, writes=[btbl[kk]])
                    cos, sin = tbl[kk][0], tbl[kk][1]
                    p1, bp1 = self.inproj_fm(wA, bw, 0, 128, tb, banks=IB)
                    p2, bp2 = self.inproj_fm(wBq, bwv, 0, 128, tb, banks=IB)
                    s.op("dve", lambda e: e.tensor_tensor(out=t1[:], in0=p1[:], in1=cos[:], op=ALU.mult), reads=[bp1, btbl[kk]], writes=[bt1])
                    s.op("dve", lambda e: e.tensor_tensor(out=t2[:], in0=p2[:], in1=sin[:], op=ALU.mult), reads=[bp2, btbl[kk]], writes=[bt2])
                    s.op("pool", lambda e: e.tensor_tensor(out=qT[k][:, sl], in0=t1[:], in1=t2[:], op=ALU.add), reads=[bt1, bt2], writes=[bqT[k][tb]])
                    p3, bp3 = self.inproj_fm(wA, bw, 128, 128, tb, banks=IB)
                    p4, bp4 = self.inproj_fm(wBk, bwBk, 0, 128, tb, banks=IB)
                    s.op("dve", lambda e: e.tensor_tensor(out=t1[:], in0=p3[:], in1=cos[:], op=ALU.mult), reads=[bp3, btbl[kk]], writes=[bt1])
                    s.op("dve", lambda e: e.scalar_tensor_tensor(out=t2[:], in0=p4[:], scalar=RET_SCALE, in1=sin[:], op0=ALU.mult, op1=ALU.mult),
                         reads=[bp4, btbl[kk]], writes=[bt2])
                    s.op("dve", lambda e: e.scalar_tensor_tensor(out=kT[k][:, sl], in0=t1[:], scalar=RET_SCALE, in1=t2[:], op0=ALU.mult, op1=ALU.add),
                         reads=[bt1, bt2], writes=[bkT[k][tb]])
                    pb, bpb = self.bbank()

                    def ftr(e):
                        ins = None
                        for i in range(4):
                            n = tb * 4 + i
                            ins = e.transpose(out=pb[:, i * 128:(i + 1) * 128], in_=kT[k][:, n * 128:(n + 1) * 128], identity=self.identb[:])
                        return ins
                    s.op("pe", ftr, reads=[bkT[k][tb], self.bconst], writes=[bpb])
                    s.op("act", lambda e: e.activation(out=ktok[k][:, tb * 4:(tb + 1) * 4, :], in_=pb.rearrange("p (a b) -> p a b", a=4), func=AF.Copy),
                         reads=[bpb], writes=[bktok[k][tb]])
                steps.append(stb)
            steps.append(svmm)
            return steps

        def rest_steps(h):
            k = h % 2
            steps = []
            lf, lb = lg[:, h:h + 1], lg[:, 4 + h:5 + h]

            def sconst():
                rdh, wrh = [self.bconst, bdec, bhd[k]], [bhd[k]]
                s.op("dve", lambda e: e.tensor_scalar_mul(out=dtmp[:], in0=cstv[:, C_P1:C_P1 + 128], scalar1=lf), reads=rdh + [bhd[1 - k]], writes=wrh + [bhd[1 - k]])
                s.op("dve", lambda e: e.scalar_tensor_tensor(out=dtmp[:], in0=cstv[:, C_P2:C_P2 + 128], scalar=lb, in1=dtmp[:], op0=ALU.mult, op1=ALU.add),
                     reads=rdh + [bhd[1 - k]], writes=wrh + [bhd[1 - k]])
                s.op("act", lambda e: e.activation(out=DT[k][:], in_=dtmp[:], func=AF.Exp), reads=rdh + [bhd[1 - k]], writes=wrh)

            def make_chain(direction):
                tg_order = [0, 1, 2, 3] if direction == 0 else [3, 2, 1, 0]
                kvb = {}

                def emit_kv(tg):
                    pt, bpt = self.bank(RB)

                    def fkv(e, tg=tg, pt=pt):
                        ins = None
                        for i in range(4):
                            n = tg * 4 + i
                            ins = e.matmul(pt[:, i * 128:(i + 1) * 128], lhsT=kdec[:, n, :], rhs=vtok[k][:, n, :], start=True, stop=True)
                        return ins
                    s.op("pe", fkv, reads=[bkdec, bvtok[k][tg]], writes=[bpt])
                    kvb[tg] = (pt, bpt)

                def skv_k():
                    kdcol = KD[:, direction * 4 + h:direction * 4 + h + 1]
                    s.op("act", lambda e: e.activation(out=kdec[:], in_=ktok[k][:], func=AF.Copy, scale=kdcol), reads=bktok[k] + [bdec], writes=[bkdec])

                def skv_m():
                    for tg in tg_order[:3]:
                        emit_kv(tg)

                def schain():
                    gcol = gC[:, direction * 4 + h:direction * 4 + h + 1]
                    Sst, bSst = (Sf, bSf) if direction == 0 else (Sb, bSb)
                    order = list(range(0, 15)) if direction == 0 else list(range(15, 0, -1))
                    for step, n in enumerate(order):
                        if n // 4 not in kvb:
                            emit_kv(n // 4)
                        pt, bpt = kvb[n // 4]
                        kvn = pt[:, (n % 4) * 128:(n % 4 + 1) * 128]
                        dst = n + 1 if direction == 0 else n - 1
                        cur, nxt = step % 2, (step + 1) % 2
                        if step == 0:
                            s.op("dve", lambda e, kvn=kvn, nxt=nxt: e.tensor_copy(out=srun[nxt][:], in_=kvn), reads=[bpt], writes=[bsrun[nxt]])
                        else:
                            s.op("dve", lambda e, kvn=kvn, cur=cur, nxt=nxt: e.scalar_tensor_tensor(
                                out=srun[nxt][:], in0=srun[cur][:], scalar=gcol, in1=kvn, op0=ALU.mult, op1=ALU.add),
                                reads=[bpt, bsrun[cur], bdec], writes=[bsrun[nxt]])
                        s.op("act", lambda e, nxt=nxt, dst=dst: e.activation(out=Sst[:, dst, :], in_=srun[nxt][:], func=AF.Copy),
                             reads=[bsrun[nxt]], writes=[bSst])
                return skv_k, skv_m, schain
            make_chain.kvb = {}
            skvk_f, skvm_f, schain_f = make_chain(0)
            skvk_b, skvm_b, schain_b = make_chain(1)

            def sqfb():
                rdh = [self.bconst, bdec, bhd[0], bhd[1]]
                s.op("act", lambda e: e.activation(out=AFm[:], in_=cstv[:, C_I1:C_I1 + 128], func=AF.Exp, scale=lf), reads=rdh, writes=[bhd[0], bhd[1]])
                s.op("act", lambda e: e.activation(out=ABm[:], in_=cstv[:, C_I2:C_I2 + 128], func=AF.Exp, scale=lb), reads=rdh, writes=[bhd[0], bhd[1]])
                for tb in range(NTB):
                    sl = slice(tb * 512, (tb + 1) * 512)
                    s.op("dve", lambda e, sl=sl: e.tensor_tensor(out=qf[:, sl].rearrange("p (a b) -> p a b", a=4),
                                                                 in0=qT[k][:, sl].rearrange("p (a b) -> p a b", a=4),
                                                                 in1=bc(AFm[:], 0, 4), op=ALU.mult),
                         reads=[bqT[k][tb], bhd[0], bhd[1]], writes=[bqf[tb]])
                    s.op("dve", lambda e, sl=sl: e.tensor_tensor(out=qb[:, sl].rearrange("p (a b) -> p a b", a=4),
                                                                 in0=qT[k][:, sl].rearrange("p (a b) -> p a b", a=4),
                                                                 in1=bc(ABm[:], 0, 4), op=ALU.mult),
                         reads=[bqT[k][tb], bhd[0], bhd[1]], writes=[bqb[tb]])

            def make_out(tg):
                kk = tg % 2
                box = {}

                def souts():
                    pa, bpa = self.bank(RB)

                    def fa(e):
                        ins = None
                        for i in range(4):
                            n = tg * 4 + i
                            ins = e.matmul(pa[:, i * 128:(i + 1) * 128], lhsT=kT[k][:, n * 128:(n + 1) * 128], rhs=qT[k][:, n * 128:(n + 1) * 128],
                                           start=True, stop=True)
                        return ins
                    s.op("pe", fa, reads=[bkT[k][tg], bqT[k][tg]], writes=[bpa])
                    s.op("dve", lambda e: e.tensor_tensor(out=ATb[kk][:], in0=pa[:].rearrange("p (a b) -> p a b", a=4),
                                                          in1=bc(DT[k][:], 0, 4), op=ALU.mult),
                         reads=[bpa, bhd[k]], writes=[bATb[kk]])

                def souta():
                    po, bpo = self.bank(RB)
                    box["po"] = (po, bpo)

                    def fo(e):
                        ins = None
                        for i in range(4):
                            n = tg * 4 + i
                            terms = [(ATb[kk][:, i, :], vtok[k][:, n, :])]
                            if n > 0:
                                terms.append((qf[:, n * 128:(n + 1) * 128], Sf[:, n, :]))
                            if n < 15:
                                terms.append((qb[:, n * 128:(n + 1) * 128], Sb[:, n, :]))
                            for j, (lt, rh) in enumerate(terms):
                                ins = e.matmul(po[:, i * 128:(i + 1) * 128], lhsT=lt, rhs=rh, start=(j == 0), stop=(j == len(terms) - 1))
                        return ins
                    s.op("pe", fo, reads=[bATb[kk], bvtok[k][tg], bqf[tg], bqb[tg], bSf, bSb], writes=[bpo])
                    for i in range(4):
                        s.op("dve", lambda e, i=i: e.bn_stats(out=st[kk][:, i, :], in_=po[:, i * 128:(i + 1) * 128]), reads=[bpo], writes=[bst[kk]])
                    for i in range(4):
                        s.op("dve", lambda e, i=i: e.bn_aggr(out=mv[kk][:, i, :], in_=st[kk][:, i, :]), reads=[bst[kk]], writes=[bst[kk]])
                    s.op("dve", lambda e: e.tensor_scalar_add(out=rs[kk][:], in0=mv[kk][:, :, 1], scalar1=EPS), reads=[bst[kk]], writes=[bst[kk]])
                    s.op("act", lambda e: e.activation(out=rs[kk][:], in_=rs[kk][:], func=AF.Ln), reads=[bst[kk]], writes=[bst[kk]])
                    s.op("act", lambda e: e.activation(out=rs[kk][:], in_=rs[kk][:], func=AF.Exp, scale=-0.5), reads=[bst[kk]], writes=[bst[kk]])

                def soutb():
                    po, bpo = box["po"]
                    for i in range(4):
                        s.op("dve", lambda e, i=i: e.tensor_scalar(out=onb[kk][:, i, :], in0=po[:, i * 128:(i + 1) * 128],
                                                                   scalar1=mv[kk][:, i, 0:1], scalar2=rs[kk][:, i:i + 1],
                                                                   op0=ALU.subtract, op1=ALU.mult),
                             reads=[bpo, bst[kk]], writes=[bonb[kk]])
                    pb, bpb = self.bbank()

                    def ftr2(e):
                        ins = None
                        for i in range(4):
                            ins = e.transpose(out=pb[:, i * 128:(i + 1) * 128], in_=onb[kk][:, i, :], identity=self.identb[:])
                        return ins
                    s.op("pe", ftr2, reads=[bonb[kk], self.bconst], writes=[bpb])
                    s.op("dve", lambda e: e.tensor_tensor(out=self.Y[:, h, tg * 512:(tg + 1) * 512], in0=pb,
                                                          in1=self.Y[:, h, tg * 512:(tg + 1) * 512], op=ALU.mult),
                         reads=[bpb, self.bY[h][tg]], writes=[self.bY[h][tg]])
                return souts, souta, soutb
            outs = [make_out(tg) for tg in range(4)]
            return dict(sconst=sconst, skvk_f=skvk_f, skvm_f=skvm_f, schain_f=schain_f, skvk_b=skvk_b, skvm_b=skvm_b, schain_b=schain_b,
                        sqfb=sqfb, outs=outs)

        p0 = inproj_steps(0)
        for i in (0, 1, 6, 2, 3, 4, 5):
            p0[i]()
        for h in range(4):
            nxt = inproj_steps(h + 1) if h + 1 < 4 else [lambda: None] * 7
            r = rest_steps(h)
            r["sconst"]()
            r["skvk_f"]()
            r["skvm_f"]()
            nxt[0]()
            nxt[1]()
            r["schain_f"]()
            r["skvk_b"]()
            nxt[2]()
            r["skvm_b"]()
            r["schain_b"]()
            nxt[3]()
            r["sqfb"]()
            fill = [nxt[4], nxt[5], nxt[6], (lambda: None)]
            r["outs"][0][0]()
            for tg in range(4):
                sc, a, b = r["outs"][tg]
                if tg + 1 < 4:
                    r["outs"][tg + 1][0]()
                a()
                fill[tg]()
                b()
        self.dump_y(l * 4 + 1)
        self.s.barrier()
        self.out_proj(l, 1)

    def stage_mla(self, l):
        s = self.s
        self.scr_reset()
        scr = self.scr
        ALLB = (0, 1, 2, 3, 4, 5, 6)
        qlatn = scr("qlatn", [128, 3, SEQ], BF16)
        kvlatn = scr("kvlatn", [128, 2, SEQ], BF16)
        kr = scr("kr", [128, SEQ], BF16)
        bqlat = [Buf("qlat%d" % i) for i in range(4)]
        bkvlat = [Buf("kvlat%d" % i) for i in range(4)]
        bkr = Buf("kr")
        tblm_off = self.off
        tblm = [[scr("tblm", [128, 512], F32) for _ in range(2)] for _ in range(2)]
        btblm = [Buf("tblm0"), Buf("tblm1")]
        Vaug = [scr("Vaug", [128, NT, 128], BF16) for _ in range(2)]
        bV = [Buf("V0"), Buf("V1")]
        self.ptoff = self.off
        PT = scr("PT", [128, NPT, 512], BF16)
        bPT = [Buf("PT%d" % i) for i in range(NPT)]
        rsum_off = self.off
        rsum = scr("rsum", [128, 512], F32)
        brsum = Buf("rsum")
        rstdL_off = self.off
        rstdL = scr("rstdL", [128, 512], F32)
        brstdL = Buf("rstdL")
        rinv, brinv = rstdL, brstdL
        t2v = self.nc.alloc_sbuf_tensor_at("mt2v_%d" % l, [128, 512], F32, offset=rstdL_off)
        bt2 = brstdL
        hilo = self.nc.alloc_sbuf_tensor_at("hilo_%d" % l, [128, 2, 512], BF16, offset=tblm_off)
        bhilo = btblm[0]
        qn = [scr("qn", [128, SEQ], BF16) for _ in range(2)]
        qr = [scr("qr", [128, SEQ], BF16) for _ in range(2)]
        kn = [scr("kn", [128, SEQ], BF16) for _ in range(2)]
        bqn = [[Buf("qn%d_%d" % (k, i)) for i in range(4)] for k in range(2)]
        bqr = [[Buf("qr%d_%d" % (k, i)) for i in range(4)] for k in range(2)]
        bkn = [Buf("kn0"), Buf("kn1")]
        t1 = scr("mt1", [128, 512], F32)
        bt1 = Buf("mt1")
        wkrot = self.view(rsum_off, "wkrot", [128, 8, 128], BF16)
        bwkrot = brsum
        ntab = [0]

        s.op("pool", lambda e: e.memset(kr[0:64, :], 0.0), writes=[bkr])
        for hs in range(2):
            s.op("pool", lambda e, hs=hs: e.memset(qr[hs][0:64, :], 0.0), writes=bqr[hs])

        def load_tables(tb):
            k = ntab[0] % 2
            ntab[0] += 1
            sl = slice(tb * 512, (tb + 1) * 512)
            s.dma("sp", "tm%d" % k, lambda e, sem, k=k, sl=sl: (
                e.dma_start(out=tblm[k][0][64:128, :], in_=self.ropeM_d[0][:, sl]).then_inc(sem, 16),
                e.dma_start(out=tblm[k][1][64:128, :], in_=self.ropeM_d[1][:, sl]).then_inc(sem, 16)), 2, writes=[btblm[k]])
            return tblm[k][0], tblm[k][1], btblm[k]

        def rope64(p1, bp1, p2, bp2, tb, dst, bdst):
            cos, sin, btb = load_tables(tb)
            sl = slice(tb * 512, (tb + 1) * 512)
            s.op("dve", lambda e: e.tensor_tensor(out=t1[64:128, :], in0=p1[64:128, :], in1=cos[64:128, :], op=ALU.mult), reads=[bp1, btb], writes=[bt1])
            s.op("dve", lambda e: e.tensor_tensor(out=t2v[64:128, :], in0=p2[64:128, :], in1=sin[64:128, :], op=ALU.mult), reads=[bp2, btb], writes=[bt2])
            s.op("dve", lambda e: e.tensor_tensor(out=dst[64:128, sl], in0=t1[64:128, :], in1=t2v[64:128, :], op=ALU.add), reads=[bt1, bt2], writes=[bdst])

        def latent_norm(ws, nch, gbase, dstT, bdst):
            for tb in range(NTB):
                sl = slice(tb * 512, (tb + 1) * 512)
                raws = []
                for c in range(nch):
                    wt, bw, col = ws[c]
                    raws.append(self.inproj_fm(wt, bw, col, 128, tb, banks=(0, 1, 2, 4, 5, 6)))
                for c, (pt, bpt) in enumerate(raws):
                    s.op("act", lambda e, c=c, pt=pt: e.activation(out=PT[:, c, :], in_=pt[:], func=AF.Square), reads=[bpt], writes=[bPT[c]])
                psum, bpsum = self.bank((3,))
                self.mm_group(psum[:], [(self.onesb[:], PT[:, c, :]) for c in range(nch)], reads=[bPT[c] for c in range(nch)] + [self.bconst],
                              writes=[bpsum])
                s.op("dve", lambda e, psum=psum: e.tensor_scalar(out=rstdL[:], in0=psum[:], scalar1=1.0 / (128 * nch), scalar2=EPS,
                                                               op0=ALU.mult, op1=ALU.add), reads=[bpsum], writes=[brstdL])
                s.op("act", lambda e: e.activation(out=rstdL[:], in_=rstdL[:], func=AF.Ln), reads=[brstdL], writes=[brstdL])
                s.op("act", lambda e: e.activation(out=rstdL[:], in_=rstdL[:], func=AF.Exp, scale=-0.5), reads=[brstdL], writes=[brstdL])
                for c, (pt, bpt) in enumerate(raws):
                    s.op("dve", lambda e, c=c, pt=pt, sl=sl: e.scalar_tensor_tensor(out=dstT[:, c, sl], in0=pt[:], scalar=self.pcol(l, gbase, c),
                                                                                    in1=rstdL[:], op0=ALU.mult, op1=ALU.mult),
                         reads=[bpt, brstdL, self.bconst], writes=[bdst[tb]])

        (off1, bw1), (off2, bw2) = self.wpop(2)
        wq1 = self.view(off1, "wq1", [128, 8, 256], BF16)
        wq2 = self.view(off2, "wq2", [128, 8, 128], BF16)
        latent_norm([(wq1, bw1, 0), (wq1, bw1, 128), (wq2, bw2, 0)], 3, P_QG, qlatn, bqlat)
        off, bw = self.wpop()
        wkv = self.view(off, "wkv", [128, 8, 256], BF16)
        latent_norm([(wkv, bw, 0), (wkv, bw, 128)], 2, P_KG, kvlatn, bkvlat)
        off, bw = self.wpop()
        wkr = self.view(off, "wkr", [128, 8, 128], BF16)
        s.op("pool", lambda e: e.tensor_copy(out=wkrot[:, :, 0:64], in_=wkr[:, :, 0:64]), reads=[bw], writes=[bwkrot])
        s.op("pool", lambda e: e.tensor_scalar_mul(out=wkrot[:, :, 64:96], in0=wkr[:, :, 96:128], scalar1=-1.0), reads=[bw], writes=[bwkrot])
        s.op("pool", lambda e: e.tensor_copy(out=wkrot[:, :, 96:128], in_=wkr[:, :, 64:96]), reads=[bw], writes=[bwkrot])
        for tb in range(NTB):
            p1, bp1 = self.inproj_fm(wkr, bw, 0, 128, tb, banks=ALLB)
            p2, bp2 = self.inproj_fm(wkrot, bwkrot, 0, 128, tb, banks=ALLB)
            rope64(p1, bp1, p2, bp2, tb, kr, bkr)

        PREPB = (5, 6) if self.interleave else ALLB

        def prep_steps(h):
            hs = h % 2
            st_ = {}
            steps = []

            def s0():
                off, bw = self.wpop()
                st_["bw"] = bw
                st_["wa"] = wa = self.view(off, "wa", [128, 3, 192], BF16)
                st_["wc"] = self.view(off + 1536, "wc", [128, 2, 256], BF16)
                st_["wr"] = wr = self.view(off + 2560, "wr", [128, 3, 128], BF16)
                s.op("pool", lambda e: e.tensor_copy(out=wr[:, :, 0:64], in_=wa[:, :, 64:128]), reads=[bw], writes=[bw])
                s.op("pool", lambda e: e.tensor_scalar_mul(out=wr[:, :, 64:96], in0=wa[:, :, 160:192], scalar1=-1.0), reads=[bw], writes=[bw])
                s.op("pool", lambda e: e.tensor_copy(out=wr[:, :, 96:128], in_=wa[:, :, 128:160]), reads=[bw], writes=[bw])
            steps.append(s0)
            ksteps, qsteps, vsteps = [], [], []
            for tb in range(NTB):
                sl = slice(tb * 512, (tb + 1) * 512)

                def sqn(tb=tb, sl=sl):
                    bw, wa = st_["bw"], st_["wa"]
                    pt, bpt = self.bank(PREPB)
                    self.mm_group(pt[:], [(wa[:, kc, 0:128], qlatn[:, kc, sl]) for kc in range(3)], reads=[bw, bqlat[tb]], writes=[bpt])
                    s.op("act", lambda e: e.activation(out=qn[hs][:, sl], in_=pt[:], func=AF.Copy), reads=[bpt], writes=[bqn[hs][tb]])

                def sqr(tb=tb, sl=sl):
                    bw, wa, wr = st_["bw"], st_["wa"], st_["wr"]
                    p1, bp1 = self.bank(PREPB)
                    self.mm_group(p1[:], [(wa[:, kc, 64:192], qlatn[:, kc, sl]) for kc in range(3)], reads=[bw, bqlat[tb]], writes=[bp1])
                    p2, bp2 = self.bank(PREPB)
                    self.mm_group(p2[:], [(wr[:, kc, :], qlatn[:, kc, sl]) for kc in range(3)], reads=[bw, bqlat[tb]], writes=[bp2])
                    rope64(p1, bp1, p2, bp2, tb, qr[hs], bqr[hs][tb])

                def skn(tb=tb, sl=sl):
                    bw, wc = st_["bw"], st_["wc"]
                    pt, bpt = self.bank(PREPB)
                    self.mm_group(pt[:], [(wc[:, kc, 0:128], kvlatn[:, kc, sl]) for kc in range(2)], reads=[bw, bkvlat[tb]], writes=[bpt])
                    s.op("act", lambda e: e.activation(out=kn[hs][:, sl], in_=pt[:], func=AF.Copy), reads=[bpt], writes=[bkn[hs]])
                ksteps.append(skn)
                qsteps += [sqn, sqr]
            for tg in range(4):
                def sv(tg=tg):
                    bw, wc = st_["bw"], st_["wc"]
                    pt, bpt = self.bank(PREPB)

                    def fv(e):
                        ins = None
                        for i in range(4):
                            t = tg * 4 + i
                            for kc in range(2):
                                ins = e.matmul(pt[:, i * 128:(i + 1) * 128], lhsT=kvlatn[:, kc, t * 128:(t + 1) * 128], rhs=wc[:, kc, 128:256],
                                               start=(kc == 0), stop=(kc == 1))
                        return ins
                    s.op("pe", fv, reads=[bw, bkvlat[tg]], writes=[bpt])
                    s.op("act", lambda e: e.activation(out=Vaug[hs][:, tg * 4:(tg + 1) * 4, :],
                                                       in_=pt[:].rearrange("p (a b) -> p a b", a=4), func=AF.Copy),
                         reads=[bpt], writes=[bV[hs]])
                vsteps.append(sv)
            return steps + ksteps + vsteps + qsteps

        psbF = self.psb[:].bitcast(F32)
        sbanks = [(self.ps[0], self.bps[0]), (self.ps[1], self.bps[1]), (self.ps[6], self.bps[6]), (psbF, self.bpsb[0])]

        def attention(h, pending):
            hs = h % 2
            hh = h % 4
            fin = {"pe": None, "tail": None}
            for qb in range(4):
                qsl = slice(qb * 512, (qb + 1) * 512)
                sbank = 4 + (self.nqb % 2)
                self.nqb += 1
                pss, bpss = self.ps[sbank], self.bps[sbank]

                def emit_s(kt, qsl=qsl):
                    ps_, bps_ = sbanks[self.sctr % 4]
                    self.sctr += 1

                    def fs(e, kt=kt, ps_=ps_, qsl=qsl):
                        e.matmul(ps_[:], lhsT=kn[hs][:, kt * 128:(kt + 1) * 128], rhs=qn[hs][:, qsl], start=True, stop=False)
                        return e.matmul(ps_[:], lhsT=kr[:, kt * 128:(kt + 1) * 128], rhs=qr[hs][:, qsl], start=False, stop=True)
                    s.op("pe", fs, reads=[bkn[hs], bqn[hs][qb], bkr, bqr[hs][qb]], writes=[bps_])
                    return ps_, bps_
                acc, bacc = self.bank((2, 3))
                sq_ = [emit_s(0), emit_s(1), emit_s(2), emit_s(3)]
                if fin["pe"]:
                    fin["pe"]()
                    fin["pe"] = None
                for kt in range(NT):
                    ps_, bps_ = sq_[kt]
                    pi = (self.nqb * NT + kt) % NPT
                    s.op("act", lambda e, ps_=ps_, pi=pi: e.activation(out=PT[:, pi, :], in_=ps_[:], func=AF.Exp, scale=MLA_SCALE),
                         reads=[bps_], writes=[bPT[pi]])
                    if kt + 4 < NT:
                        sq_.append(emit_s(kt + 4))
                    s.op("pe", lambda e, kt=kt, pi=pi, acc=acc: e.matmul(acc[:], lhsT=Vaug[hs][:, kt, :], rhs=PT[:, pi, :],
                                                                         start=(kt == 0), stop=(kt == NT - 1)),
                         reads=[bPT[pi], bV[hs]], writes=[bacc])
                    if kt % 2 == 0:
                        if kt == 0:
                            s.op("dve", lambda e, pi=pi: e.tensor_copy(out=rsum[:], in_=PT[:, pi, :]), reads=[bPT[pi]], writes=[brsum])
                        else:
                            s.op("dve", lambda e, pi=pi: e.tensor_tensor(out=rsum[:], in0=rsum[:], in1=PT[:, pi, :], op=ALU.add),
                                 reads=[bPT[pi], brsum], writes=[brsum])
                    else:
                        s.op("pe", lambda e, kt=kt, pi=pi, pss=pss: e.matmul(pss[:], lhsT=self.onesb[:], rhs=PT[:, pi, :], start=(kt == 1), stop=False),
                             reads=[bPT[pi], self.bconst], writes=[bpss])
                    if kt == 1 and fin["tail"]:
                        fin["tail"]()
                        fin["tail"] = None
                    if pending:
                        pending.pop(0)()
                s.op("dve", lambda e: e.tensor_copy(out=hilo[:, 0, :], in_=rsum[:]), reads=[brsum], writes=[bhilo])
                s.op("dve", lambda e: e.tensor_tensor(out=hilo[:, 1, :], in0=rsum[:], in1=hilo[:, 0, :], op=ALU.subtract),
                     reads=[brsum, bhilo], writes=[bhilo])

                def fin_pe(pss=pss, bpss=bpss):
                    self.mm_group(pss[:], [(self.onesb[:], hilo[:, 0, :]), (self.onesb[:], hilo[:, 1, :])], reads=[bhilo, self.bconst],
                                  writes=[bpss], first=False, last=True)

                def fin_tail(pss=pss, bpss=bpss, acc=acc, bacc=bacc, qsl=qsl, qb=qb):
                    s.op("act", lambda e: e.activation(out=rinv[:], in_=pss[:], func=AF.Ln), reads=[bpss], writes=[brinv])
                    s.op("act", lambda e: e.activation(out=rinv[:], in_=rinv[:], func=AF.Exp, scale=-1.0), reads=[brinv], writes=[brinv])
                    s.op("dve", lambda e: e.tensor_tensor(out=rinv[:], in0=rinv[:], in1=self.Y[:, hh, qsl], op=ALU.mult),
                         reads=[brinv, self.bY[hh][qb]], writes=[brinv])
                    s.op("dve", lambda e: e.tensor_tensor(out=self.Y[:, hh, qsl], in0=acc[:], in1=rinv[:], op=ALU.mult),
                         reads=[bacc, brinv, self.bY[hh][qb]], writes=[self.bY[hh][qb]])
                fin["pe"], fin["tail"] = fin_pe, fin_tail
            fin["pe"]()
            fin["tail"]()

        self.gates(l)
        for st in prep_steps(0):
            st()
        for h in range(8):
            pending = prep_steps(h + 1) if h + 1 < 8 else []
            attention(h, pending if self.interleave else [])
            for st in pending:
                st()
            if h % 4 == 3:
                g = h // 4
                self.dump_y(l * 4 + 2 + g)
                if g == 1:
                    self.s.barrier()
                self.out_proj(l, 2 + g)
                if g == 0:
                    self.gates(l)
        if l == 0:
            self.dump("qlatn", qlatn, bqlat); self.dump("kvlatn", kvlatn, bkvlat); self.dump("kr", kr, [bkr])
            self.dump("qn1", qn[1], bqn[1]); self.dump("qr1", qr[1], bqr[1]); self.dump("kn1", kn[1], [bkn[1]])
            self.dump("V1", Vaug[1], [bV[1]]); self.dump("PT", PT, bPT)

    def build(self):
        self.setup()
        for l in range(self.n_layers):
            self.plan_layer(l)
        for l in range(self.n_layers):
            self.stage_norm(l)
            if "conv" in self.stages:
                self.stage_conv(l)
            if "ret" in self.stages:
                self.stage_ret(l)
            if "mla" in self.stages:
                self.stage_mla(l)
        self.stage_final()
        self.s.emit()
        return self.nc

    def stage_final(self):
        s = self.s
        self.scr_reset()
        evs = list(self.dbg_evs)
        if not self.final:
            for t in range(NT):
                evs.append(s.dma("sp", "o%d" % (t % 4), lambda e, sem, t=t: e.dma_start(
                    out=self.out_d[t * 128:(t + 1) * 128, :], in_=self.X[:, t, :]).then_inc(sem, 16), 1, reads=[self.bX[t]]))
            s.wait("sp", evs)
            return
        self.nview += 1
        junk = self.nc.alloc_sbuf_tensor_at("junkf_%d" % self.nview, [128, D_MODEL], BF16, offset=self.y_off + 8192)
        bjunk = None
        jw = self.bY[2][0:2]
        fg = self.nc.alloc_sbuf_tensor_at("fg_%d" % self.nview, [128, D_MODEL], F32, offset=self.y_off)
        bfg = Buf("fg")
        s.dma("sp", "cst", lambda e, sem: e.dma_start(out=fg[:], in_=self.fgb_d[:, :]).then_inc(sem, 16), 1, writes=[bfg] + self.bY[0])
        for tb in range(NTB):
            tiles = list(range(tb * 4, tb * 4 + 4))
            self.row_rstd(tiles, junk, bjunk, extra_w=jw)
            for t in tiles:
                s.op("dve", lambda e, t=t: e.scalar_tensor_tensor(out=self.X[:, t, :], in0=self.X[:, t, :], scalar=self.ss[:, t:t + 1],
                                                                  in1=fg[:], op0=ALU.mult, op1=ALU.mult),
                     reads=[self.bX[t], self.bss[tb], bfg], writes=[self.bX[t]])
                evs.append(s.dma("sp", "o%d" % (t % 4), lambda e, sem, t=t: e.dma_start(
                    out=self.out_d[t * 128:(t + 1) * 128, :], in_=self.X[:, t, :]).then_inc(sem, 16), 1, reads=[self.bX[t]]))
        s.wait("sp", evs)


def host_consts():
    f32 = np.float32
    cst = np.zeros((128, NCST), f32)
    i = np.arange(128)
    cst[:, C_ID:C_ID + 128] = np.eye(128)
    cst[:, C_P1:C_P1 + 128] = np.maximum(i[None, :] - i[:, None], 0)
    cst[:, C_P2:C_P2 + 128] = np.maximum(i[:, None] - i[None, :], 0)
    cst[:, C_I1:C_I1 + 128] = (i + 1)[None, :]
    cst[:, C_I2:C_I2 + 128] = (128 - i)[None, :]
    cst[:, C_KJ] = 127 - i
    cst[:, C_KJ + 1] = i
    pos = np.arange(SEQ, dtype=f32)

    def tables(dim, rows):
        inv = (1.0 / (10000.0 ** (np.arange(0, dim, 2, dtype=f32) / f32(dim)))).astype(f32)
        ang = (pos[:, None] * inv[None, :]).astype(f32)
        ang = np.concatenate([ang, ang], axis=-1)
        return np.stack([np.cos(ang).astype(f32).T, np.sin(ang).astype(f32).T])[:, :rows, :].copy()
    return cst, tables(128, 128), tables(64, 64)


def host_params(inp):
    p = np.zeros((128, NP), np.float32)
    for l in range(DEPTH):
        b = l * PL
        p[:, b + P_NG:b + P_NG + 8] = inp["norm_g"][l].reshape(8, 128).T
        p[:, b + P_CW:b + P_CW + 124] = inp["conv_dw_w"][l].reshape(CONV_K, 4, 128).transpose(2, 0, 1).reshape(128, 124)
        p[:, b + P_CB:b + P_CB + 4] = inp["conv_dw_b"][l].reshape(4, 128).T
        p[:, b + P_LG:b + P_LG + 4] = inp["conv_ln_g"][l].reshape(4, 128).T
        p[:, b + P_LB:b + P_LB + 4] = inp["conv_ln_b"][l].reshape(4, 128).T
        p[:, b + P_QG:b + P_QG + 3] = inp["mla_qa_g"][l].reshape(3, 128).T
        p[:, b + P_KG:b + P_KG + 2] = inp["mla_kva_g"][l].reshape(2, 128).T
        p[:, b + P_DL:b + P_DL + 8] = inp["ret_decay_logit"][l].reshape(1, 8)
    p[:, P_FG:P_FG + 8] = inp["final_g"].reshape(8, 128).T
    return p


_CACHE = {}


def run(inputs, n_cores=8, **kw):
    key = tuple(sorted(kw.items()))
    if key not in _CACHE:
        _CACHE[key] = Prog(**kw)
        _CACHE[key].build()
    prog = _CACHE[key]
    inp = {k: np.ascontiguousarray(np.asarray(v, dtype=np.float32)) for k, v in inputs.items()}
    cst, ropeR, ropeM = host_consts()
    par = host_params(inp)
    fgb = np.ascontiguousarray(np.broadcast_to(inp["final_g"][None, :], (128, D_MODEL)))
    common = {"w_in": inp["w_in"], "w_uq": inp["mla_w_uq"], "w_ukv": inp["mla_w_ukv"], "w_out": inp["w_out"],
              "params": par, "consts": cst, "fgb": fgb, "ropeR": ropeR, "ropeM": ropeM}
    in_maps = [dict(common, x=inp["x"][b]) for b in range(n_cores)]
    res = run_bass_kernel_spmd(prog.nc, in_maps, core_ids=list(range(n_cores)))
    return res


def kernel(**inputs):
    res = run(inputs)
    return np.stack([r["out"] for r in res.results], axis=0).astype(np.float32)
```

```python
import numpy as np
import concourse.bass as bass
import concourse.mybir as mybir
from concourse.bass_utils import run_bass_kernel_spmd

F32 = mybir.dt.float32
BF16 = mybir.dt.bfloat16
AF = mybir.ActivationFunctionType
ALU = mybir.AluOpType

D_MODEL = 1024
SEQ = 2048
NT = 16
NTB = 4
DEPTH = 2
CONV_K = 31
OFF_RET = 1024
OFF_QLAT = 2560
OFF_KVLAT = 2944
OFF_KROPE = 3200
OFF_GATE = 3264
N_IN = 5312
EPS = 1e-6
RET_SCALE = 128 ** -0.5
MLA_SCALE = 192 ** -0.5
NPT = 5

P_NG, P_CW, P_CB, P_LG, P_LB, P_QG, P_KG, P_DL = 0, 8, 132, 136, 140, 144, 147, 149
PL = 160
P_FG = 2 * PL
NP = P_FG + 8
C_ID, C_P1, C_P2, C_I1, C_I2, C_KJ = 0, 128, 256, 384, 512, 640
NCST = 642


class Buf:
    __slots__ = ("name", "w", "r")

    def __init__(self, name):
        self.name = name
        self.w = None
        self.r = []


class Op:
    __slots__ = ("eng", "fn", "deps", "marked", "lane", "real")

    def __init__(self, eng, fn, deps):
        self.eng = eng
        self.fn = fn
        self.deps = deps
        self.marked = False
        self.lane = None
        self.real = fn is not None


class Sched:
    ENGS = ("pe", "act", "dve", "pool", "sp")

    def __init__(self, nc):
        self.nc = nc
        self.ops = {e: [] for e in self.ENGS}
        self.lanes = {}
        self.sems = {e: nc.alloc_semaphore("done_" + e) for e in self.ENGS}
        self.last_real = {e: None for e in self.ENGS}

    def _collect(self, eng, reads, writes):
        deps = set()
        for b in reads:
            if b.w is not None:
                deps.add(b.w)
        for b in writes:
            if b.w is not None:
                deps.add(b.w)
            for ev in b.r:
                if ev[0] == "e" and ev[1] == eng:
                    continue
                deps.add(ev)
        if eng == "pe":
            deps = {d for d in deps if not (d[0] == "e" and d[1] == "pe")}
        return deps

    @staticmethod
    def _register(ev, reads, writes):
        for b in reads:
            b.r.append(ev)
        for b in writes:
            b.w = ev
            b.r = []

    def op(self, eng, fn, reads=(), writes=()):
        deps = self._collect(eng, reads, writes)
        o = Op(eng, fn, deps)
        self.ops[eng].append(o)
        ev = ("e", eng, len(self.ops[eng]) - 1)
        self.last_real[eng] = ev
        self._register(ev, reads, writes)
        return ev

    def dma(self, queue, lane, fn, n, reads=(), writes=()):
        if lane not in self.lanes:
            self.lanes[lane] = {"sem": self.nc.alloc_semaphore("ln_" + lane), "val": 0}
        L = self.lanes[lane]
        deps = self._collect(queue, reads, writes)
        if L["val"] > 0:
            deps.add(("l", lane, L["val"]))
        L["val"] += 16 * n
        o = Op(queue, fn, deps)
        o.lane = lane
        o.real = False
        self.ops[queue].append(o)
        ev = ("l", lane, L["val"])
        self._register(ev, reads, writes)
        return ev

    def wait(self, eng, events):
        o = Op(eng, None, set(events))
        self.ops[eng].append(o)

    def barrier(self):
        evs = [ev for ev in self.last_real.values() if ev is not None]
        for ln, L in self.lanes.items():
            if L["val"]:
                evs.append(("l", ln, L["val"]))
        for e in self.ENGS:
            self.wait(e, [ev for ev in evs if not (ev[0] == "e" and ev[1] == e)])

    def emit(self):
        nc = self.nc
        for e in self.ENGS:
            for o in self.ops[e]:
                for d in o.deps:
                    if d[0] == "e":
                        tgt = self.ops[d[1]][d[2]]
                        assert tgt.real
                        tgt.marked = True
        cnt = {}
        for e in self.ENGS:
            c = 0
            arr = []
            for o in self.ops[e]:
                if o.marked:
                    c += 1
                arr.append(c)
            cnt[e] = arr
        self.stats = {}

        def run(e):
            def body(eng):
                waited = {}
                nwait = 0
                for o in self.ops[e]:
                    need = {}
                    for d in o.deps:
                        if d[0] == "e":
                            key = ("e", d[1])
                            val = cnt[d[1]][d[2]]
                        else:
                            key = ("l", d[1])
                            val = d[2]
                        if val > need.get(key, 0):
                            need[key] = val
                    for key, val in need.items():
                        if waited.get(key, 0) < val:
                            sem = self.sems[key[1]] if key[0] == "e" else self.lanes[key[1]]["sem"]
                            eng.wait_ge(sem, val)
                            waited[key] = val
                            nwait += 1
                    if o.lane is not None:
                        o.fn(eng, self.lanes[o.lane]["sem"])
                    elif o.fn is not None:
                        ins = o.fn(eng)
                        if o.marked:
                            ins.then_inc(self.sems[e], 1)
                self.stats[e] = (len(self.ops[e]), nwait, cnt[e][-1] if cnt[e] else 0)
            return body

        with nc.Block() as block:
            block.tensor(run("pe"))
            block.scalar(run("act"))
            block.vector(run("dve"))
            block.gpsimd(run("pool"))
            block.sync(run("sp"))


def bc(ap, pos, count):
    dims = [list(d) for d in ap.ap]
    dims.insert(1 + pos, [0, count])
    return bass.AP(ap.tensor, ap.offset, dims)


class Prog:
    def __init__(self, n_layers=DEPTH, stages=("conv", "ret", "mla"), dbg=False, final=True):
        self.n_layers = n_layers
        self.stages = stages
        self.dbg = dbg
        self.final = final
        nc = self.nc = bass.Bass("TRN2", target_bir_lowering=False)
        self.s = Sched(nc)
        dr = nc.dram_tensor
        self.x_d = dr("x", [SEQ, D_MODEL], F32, kind="ExternalInput").ap()
        self.w_in_d = dr("w_in", [DEPTH, D_MODEL, N_IN], F32, kind="ExternalInput").ap()
        self.w_uq_d = dr("w_uq", [DEPTH, 384, 1536], F32, kind="ExternalInput").ap()
        self.w_ukv_d = dr("w_ukv", [DEPTH, 256, 2048], F32, kind="ExternalInput").ap()
        self.w_out_d = dr("w_out", [DEPTH, 2048, D_MODEL], F32, kind="ExternalInput").ap()
        self.par_d = dr("params", [128, NP], F32, kind="ExternalInput").ap()
        self.cst_d = dr("consts", [128, NCST], F32, kind="ExternalInput").ap()
        self.fgb_d = dr("fgb", [128, D_MODEL], F32, kind="ExternalInput").ap()
        self.ropeR_d = dr("ropeR", [2, 128, SEQ], F32, kind="ExternalInput").ap()
        self.ropeM_d = dr("ropeM", [2, 64, SEQ], F32, kind="ExternalInput").ap()
        self.out_d = dr("out", [SEQ, D_MODEL], F32, kind="ExternalOutput").ap()
        if dbg:
            self.dbg_d = dr("dbgY", [2 * DEPTH, 128, 4, SEQ], BF16, kind="ExternalOutput").ap()
        self.dbg_evs = []
        self.interleave = False
        self.nqb = 0
        self.sctr = 0

        self.off = 16384
        sb = self.sb
        self.X = sb("X", [128, NT, D_MODEL], F32)
        self.bX = [Buf("X%d" % t) for t in range(NT)]
        self.y_off = self.off
        self.Y = sb("Y", [128, 4, SEQ], BF16)
        self.bY = [[Buf("Y%d_%d" % (c, tb)) for tb in range(NTB)] for c in range(4)]
        self.HT = sb("HT", [128, NTB, 8, 512], BF16)
        self.bHT = [Buf("HT%d" % tb) for tb in range(NTB)]
        self.ring = [self.off + i * 4096 for i in range(4)]
        self.off += 4 * 4096
        self.bring = [Buf("ring%d" % i) for i in range(4)]
        self.ring_i = 0
        self.par = sb("par", [128, NP], F32)
        self.cst = sb("cst", [128, NCST], F32)
        self.identb = sb("identb", [128, 128], BF16)
        self.onesb = sb("onesb", [128, 128], BF16)
        self.onesf = sb("onesf", [128, 128], F32)
        self.ss = sb("ss", [128, NT], F32)
        self.bconst = Buf("const")
        self.bss = [Buf("ss%d" % i) for i in range(NTB)]
        self.scr0 = self.off
        self.scr_end = 16384 + 212992 - 64
        self.nview = 0
        self.ps = [nc.alloc_psum_tensor("ps%d" % i, [128, 512], F32) for i in range(7)]
        self.bps = [Buf("ps%d" % i) for i in range(7)]
        self.psb = nc.alloc_psum_tensor("psb", [128, 1024], BF16)
        self.bpsb = [Buf("psb0"), Buf("psb1")]
        self.psb_i = 0
        self.ps_rr = {}
        self.witems = []
        self.wi = 0
        self.issued = 0

    def sb(self, name, shape, dt):
        n = int(np.prod(shape[1:])) * (4 if dt == F32 else 2)
        t = self.nc.alloc_sbuf_tensor_at(name, list(shape), dt, offset=self.off)
        self.off += (n + 31) // 32 * 32
        return t

    def scr_reset(self):
        self.off = self.scr0

    def scr(self, name, shape, dt):
        self.nview += 1
        t = self.sb("%s_%d" % (name, self.nview), shape, dt)
        assert self.off <= self.scr_end, (name, self.off, self.scr_end)
        return t

    def view(self, off, name, shape, dt):
        self.nview += 1
        return self.nc.alloc_sbuf_tensor_at("%s_%d" % (name, self.nview), list(shape), dt, offset=off)

    def bank(self, group):
        i = self.ps_rr.get(group, 0)
        self.ps_rr[group] = i + 1
        b = group[i % len(group)]
        return self.ps[b], self.bps[b]

    def bbank(self):
        i = self.psb_i
        self.psb_i += 1
        h = i % 2
        return self.psb[:, h * 512:(h + 1) * 512], self.bpsb[h]

    def wplan(self, specs):
        self.witems.extend(specs)

    def _issue(self, idx):
        if idx >= len(self.witems):
            return
        slot = idx % 4
        off = self.ring[slot]
        parts = self.witems[idx](off)

        def fn(e, sem, parts=parts):
            for dst, src in parts:
                e.dma_start(out=dst, in_=src).then_inc(sem, 16)
        self.s.dma("pool", "w%d" % slot, fn, len(parts), writes=[self.bring[slot]])

    def wpop(self, k=1):
        while self.issued < min(self.wi + 4, len(self.witems)):
            self._issue(self.issued)
            self.issued += 1
        out = []
        for _ in range(k):
            slot = self.wi % 4
            self.wi += 1
            out.append((self.ring[slot], self.bring[slot]))
        return out[0] if k == 1 else out

    def wpop_keep(self):
        while self.issued < min(self.wi + 3, len(self.witems)):
            self._issue(self.issued)
            self.issued += 1
        slot = self.wi % 4
        self.wi += 1
        return self.ring[slot], self.bring[slot]

    def w_in_cols(self, l, col_ranges):
        tot = sum(n for _, n in col_ranges)

        def mk(off):
            t = self.view(off, "wv", [128, 8, tot], BF16)
            wv = self.w_in_d[l].rearrange("(kc p) n -> p kc n", p=128)
            parts = []
            o = 0
            for c0, n in col_ranges:
                parts.append((t[:, :, o:o + n], wv[:, :, c0:c0 + n]))
                o += n
            return parts
        return mk

    def w_out_item(self, l, g, half):
        def mk(off):
            t = self.view(off, "wo", [128, 4, 512], BF16)
            wv = self.w_out_d[l].rearrange("(c p) n -> p c n", p=128)
            return [(t[:], wv[:, g * 4:(g + 1) * 4, half * 512:(half + 1) * 512])]
        return mk

    def w_head_item(self, l, h):
        def mk(off):
            a = self.view(off, "wuq", [128, 3, 192], BF16)
            c = self.view(off + 1536, "wukv", [128, 2, 256], BF16)
            uq = self.w_uq_d[l].rearrange("(kc p) n -> p kc n", p=128)
            ukv = self.w_ukv_d[l].rearrange("(kc p) n -> p kc n", p=128)
            return [(a[:], uq[:, :, h * 192:(h + 1) * 192]), (c[:], ukv[:, :, h * 256:(h + 1) * 256])]
        return mk

    def plan_layer(self, l):
        items = []
        if "conv" in self.stages:
            for c in range(4):
                items.append(self.w_in_cols(l, [(c * 128, 128), (512 + c * 128, 128)]))
            for j in range(2):
                items.append(self.w_in_cols(l, [(OFF_GATE + j * 256, 256)]))
            for half in range(2):
                items.append(self.w_out_item(l, 0, half))
        if "ret" in self.stages:
            for j in range(2):
                items.append(self.w_in_cols(l, [(OFF_GATE + 512 + j * 256, 256)]))
            for h in range(4):
                items.append(self.w_in_cols(l, [(OFF_RET + 1024 + h * 128, 128)]))
                items.append(self.w_in_cols(l, [(OFF_RET + h * 128, 128), (OFF_RET + 512 + h * 128, 128)]))
            for half in range(2):
                items.append(self.w_out_item(l, 1, half))
        if "mla" in self.stages:
            items.append(self.w_in_cols(l, [(OFF_QLAT, 256)]))
            items.append(self.w_in_cols(l, [(OFF_QLAT + 256, 128)]))
            items.append(self.w_in_cols(l, [(OFF_KVLAT, 256)]))
            items.append(self.w_in_cols(l, [(OFF_KROPE - 64, 128)]))
            def gate_items(g):
                return [self.w_in_cols(l, [(OFF_GATE + 1024 + g * 512 + j * 256, 256)]) for j in range(2)]
            items += gate_items(0)
            items += [self.w_head_item(l, h) for h in range(5)]
            items += [self.w_out_item(l, 2, half) for half in range(2)]
            items += gate_items(1)
            items += [self.w_head_item(l, h) for h in range(5, 8)]
            items += [self.w_out_item(l, 3, half) for half in range(2)]
        self.wplan(items)

    def pcol(self, l, base, i, n=1):
        c = l * PL + base + i
        return self.par[:, c:c + n]

    def mm_group(self, out, pairs, reads, writes, first=True, last=True):
        def fn(e):
            ins = None
            n = len(pairs)
            for i, (lt, rh) in enumerate(pairs):
                ins = e.matmul(out, lhsT=lt, rhs=rh, start=(first and i == 0), stop=(last and i == n - 1))
            return ins
        return self.s.op("pe", fn, reads=reads, writes=writes)

    def setup(self):
        s = self.s
        for t in range(NT):
            s.dma("sp", "x%d" % (t % 4),
                  lambda e, sem, t=t: e.dma_start(out=self.X[:, t, :], in_=self.x_d[t * 128:(t + 1) * 128, :]).then_inc(sem, 16),
                  1, writes=[self.bX[t]])
            if t == 0:
                s.dma("sp", "cst", lambda e, sem: (e.dma_start(out=self.par[:], in_=self.par_d[:, :]).then_inc(sem, 16),
                                                   e.dma_start(out=self.cst[:], in_=self.cst_d[:, :]).then_inc(sem, 16)),
                      2, writes=[self.bconst])
        s.op("dve", lambda e: e.tensor_copy(out=self.identb[:], in_=self.cst[:, C_ID:C_ID + 128]),
             reads=[self.bconst], writes=[self.bconst])
        s.op("dve", lambda e: e.memset(self.onesb[:], 1.0), writes=[self.bconst])
        s.op("dve", lambda e: e.memset(self.onesf[:], 1.0), writes=[self.bconst])

    def row_rstd(self, tiles, junk, bjunk, extra_w=()):
        s = self.s
        bss = self.bss[tiles[0] // 4]
        for t in tiles:
            s.op("act", lambda e, t=t: e.activation(out=junk[:], in_=self.X[:, t, :], func=AF.Square,
                                                    accum_out=self.ss[:, t:t + 1]),
                 reads=[self.bX[t]], writes=[bss] + list(extra_w))
        t0, t1 = tiles[0], tiles[-1] + 1
        s.op("dve", lambda e: e.tensor_scalar(out=self.ss[:, t0:t1], in0=self.ss[:, t0:t1], scalar1=1.0 / D_MODEL,
                                              scalar2=EPS, op0=ALU.mult, op1=ALU.add),
             reads=[bss], writes=[bss])
        s.op("act", lambda e: e.activation(out=self.ss[:, t0:t1], in_=self.ss[:, t0:t1], func=AF.Ln),
             reads=[bss], writes=[bss])
        s.op("act", lambda e: e.activation(out=self.ss[:, t0:t1], in_=self.ss[:, t0:t1], func=AF.Exp, scale=-0.5),
             reads=[bss], writes=[bss])

    def stage_norm(self, l):
        s = self.s
        self.nview += 1
        yoff = self.y_off
        xs = [self.nc.alloc_sbuf_tensor_at("xs%d_%d" % (i, self.nview), [128, D_MODEL], F32, offset=yoff + i * 4096) for i in range(2)]
        bxs = [self.bY[0], self.bY[1]]
        junk = self.nc.alloc_sbuf_tensor_at("junk_%d" % self.nview, [128, D_MODEL], BF16, offset=yoff + 8192)
        bjunk = self.bY[2][0]
        idf = self.cst[:, C_ID:C_ID + 128]

        def scale(t):
            k, tb = t % 2, t // 4
            s.op("dve", lambda e: e.tensor_scalar_mul(out=xs[k][:], in0=self.X[:, t, :], scalar1=self.ss[:, t:t + 1]),
                 reads=[self.bX[t], self.bss[tb]], writes=bxs[k])

        def transp(t):
            k, tb, tt = t % 2, t // 4, t % 4
            banks = []
            for hb in range(2):
                pt, bpt = self.bank((3, 4, 5, 6))

                def tr(e, hb=hb, pt=pt):
                    ins = None
                    for c in range(4):
                        cc = hb * 4 + c
                        ins = e.transpose(out=pt[:, c * 128:(c + 1) * 128], in_=xs[k][:, cc * 128:(cc + 1) * 128], identity=idf)
                    return ins
                s.op("pe", tr, reads=bxs[k] + [self.bconst], writes=[bpt])
                banks.append((pt, bpt))
            return banks

        def evac(t, banks):
            tb, tt = t // 4, t % 4
            for hb, (pt, bpt) in enumerate(banks):
                gcol = self.pcol(l, P_NG, hb * 4, 4)
                s.op("dve", lambda e, pt=pt, hb=hb, gcol=gcol: e.tensor_tensor(
                    out=self.HT[:, tb, hb * 4:(hb + 1) * 4, tt * 128:(tt + 1) * 128],
                    in0=pt[:].rearrange("p (c t) -> p c t", c=4), in1=bc(gcol, 1, 128), op=ALU.mult),
                    reads=[bpt, self.bconst], writes=[self.bHT[tb]])

        for tb in range(NTB):
            self.row_rstd(list(range(tb * 4, tb * 4 + 4)), junk, bjunk, extra_w=self.bY[2][0:2])
        scale(0)
        for t in range(NT):
            banks = transp(t)
            if t + 1 < NT:
                scale(t + 1)
            evac(t, banks)

    def inproj_fm(self, wt, bw, col, M, tb, banks=(0, 1, 2, 3)):
        pt, bpt = self.bank(banks)
        pairs = [(wt[:, kc, col:col + M], self.HT[:, tb, kc, :]) for kc in range(8)]
        self.mm_group(pt[:M, :], pairs, reads=[bw, self.bHT[tb]], writes=[bpt])
        return pt, bpt

    def gates(self, l, nitems=2):
        s = self.s
        for j in range(nitems):
            off, bw = self.wpop()
            wt = self.view(off, "wg", [128, 8, 256], BF16)
            for cc in range(2):
                c = j * 2 + cc
                for tb in range(NTB):
                    pt, bpt = self.inproj_fm(wt, bw, cc * 128, 128, tb)
                    s.op("act", lambda e, pt=pt, c=c, tb=tb: e.activation(out=self.Y[:, c, tb * 512:(tb + 1) * 512], in_=pt[:], func=AF.Silu),
                         reads=[bpt], writes=[self.bY[c][tb]])

    def out_proj(self, l, g):
        s = self.s
        for half in range(2):
            off, bw = self.wpop()
            wo = self.view(off, "wo", [128, 4, 512], BF16)
            for t in range(NT):
                pt, bpt = self.bank((0, 1, 2, 3))
                tb = t // 4
                pairs = [(self.Y[:, c, t * 128:(t + 1) * 128], wo[:, c, :]) for c in range(4)]
                self.mm_group(pt[:], pairs, reads=[bw] + [self.bY[c][tb] for c in range(4)], writes=[bpt])
                s.op("dve", lambda e, pt=pt, t=t, half=half: e.tensor_tensor(
                    out=self.X[:, t, half * 512:(half + 1) * 512], in0=pt[:], in1=self.X[:, t, half * 512:(half + 1) * 512], op=ALU.add),
                    reads=[bpt, self.bX[t]], writes=[self.bX[t]])

    def dump(self, name, t, reads):
        if not self.dbg:
            return
        d = self.nc.dram_tensor("dbg_" + name, list(t.shape), t.dtype, kind="ExternalOutput").ap()
        ev = self.s.dma("sp", "dbg", lambda e, sem: e.dma_start(out=d, in_=t[:]).then_inc(sem, 16), 1, reads=reads)
        self.dbg_evs.append(ev)

    def dump_y(self, idx):
        if not self.dbg:
            return
        ev = self.s.dma("sp", "dbg", lambda e, sem: e.dma_start(out=self.dbg_d[idx], in_=self.Y[:]).then_inc(sem, 16), 1,
                        reads=[b for r in self.bY for b in r])
        self.dbg_evs.append(ev)

    def stage_conv(self, l):
        s = self.s
        self.scr_reset()
        GW = SEQ + 30
        glu_off = self.off
        glu = self.scr("glu", [128, 4, GW], BF16)
        bglu = [Buf("glu%d" % c) for c in range(4)]
        diag0 = self.scr("diag", [128, CONV_K, 128], BF16)
        cvo = self.scr("cvo", [128, 4, SEQ], F32)
        bcvo = [[Buf("cvo%d_%d" % (c, tb)) for tb in range(NTB)] for c in range(4)]
        sig = [self.scr("sig%d" % i, [128, 512], F32) for i in range(2)]
        bsig = [Buf("sig0"), Buf("sig1")]
        reg_off = self.off
        cvb = self.scr("cvb", [128, 4, 512], BF16)
        sq = self.scr("sq", [128, 4, 512], BF16)
        diag1 = self.view(reg_off, "diag1", [128, CONV_K, 128], BF16)
        diags = [diag0, diag1]
        bdiags = [Buf("diag0"), Buf("diag1")]
        bcvb = bsq = bdiags[1]
        mean = self.scr("mean", [128, 512], F32)
        msq = self.scr("msq", [128, 512], F32)
        rstd = self.scr("rstd", [128, 512], F32)
        bmean, bmsq, brstd = Buf("mean"), Buf("msq"), Buf("rstd")
        tmp, btmp = sig, bsig

        for c in range(4):
            s.op("pool", lambda e, c=c: e.memset(glu[:, c, 0:15], 0.0), writes=[bglu[c]])
            s.op("pool", lambda e, c=c: e.memset(glu[:, c, GW - 15:GW], 0.0), writes=[bglu[c]])
            off, bw = self.wpop()
            wt = self.view(off, "wc", [128, 8, 256], BF16)
            for tb in range(NTB):
                pa, bpa = self.inproj_fm(wt, bw, 0, 128, tb)
                pb, bpb = self.inproj_fm(wt, bw, 128, 128, tb)
                k = tb % 2
                s.op("act", lambda e, pb=pb, k=k: e.activation(out=sig[k][:], in_=pb[:], func=AF.Sigmoid),
                     reads=[bpb], writes=[bsig[k]])
                s.op("dve", lambda e, pa=pa, k=k, c=c, tb=tb: e.tensor_tensor(
                    out=glu[:, c, 15 + tb * 512:15 + (tb + 1) * 512], in0=pa[:], in1=sig[k][:], op=ALU.mult),
                    reads=[bpa, bsig[k]], writes=[bglu[c]])
        self.gates(l)
        for c in range(4):
            diag, bdiag = diags[c % 2], bdiags[c % 2]
            wc0 = l * PL + P_CW + c
            wb = self.par[:, wc0:wc0 + 1]
            wcols = bass.AP(wb.tensor, wb.offset, [list(wb.ap[0]), [4, CONV_K], [0, 128]])
            s.op("dve", lambda e, diag=diag, wcols=wcols: e.tensor_tensor(out=diag[:], in0=bc(self.identb[:], 0, CONV_K), in1=wcols, op=ALU.mult),
                 reads=[self.bconst], writes=[bdiag])
            for tb in range(NTB):
                pt, bpt = self.bank((0, 1, 2, 3))
                pairs = [(diag[:, k, :], glu[:, c, tb * 512 + k:tb * 512 + k + 512]) for k in range(CONV_K)]
                self.mm_group(pt[:], pairs, reads=[bdiag, bglu[c]], writes=[bpt])
                s.op("act", lambda e, pt=pt, c=c, tb=tb: e.activation(
                    out=cvo[:, c, tb * 512:(tb + 1) * 512], in_=pt[:], func=AF.Identity, bias=self.pcol(l, P_CB, c)),
                    reads=[bpt, self.bconst], writes=[bcvo[c][tb]])
        self.nview += 1
        nv = self.nview
        at = self.nc.alloc_sbuf_tensor_at
        ln_sets = [
            dict(cvb=cvb, sq=sq, mean=mean, msq=msq, rstd=rstd,
                 bcvb=Buf("cvb0"), bsq=Buf("sq0"), bmean=bmean, bmsq=bmsq, brstd=brstd, first=[bdiags[1]]),
            dict(cvb=at("cvb1_%d" % nv, [128, 4, 512], BF16, offset=glu_off), sq=at("sq1_%d" % nv, [128, 4, 512], BF16, offset=glu_off + 4096),
                 mean=at("mean1_%d" % nv, [128, 512], F32, offset=glu_off + 8320), msq=at("msq1_%d" % nv, [128, 512], F32, offset=glu_off + 10368),
                 rstd=at("rstd1_%d" % nv, [128, 512], F32, offset=glu_off + 12416),
                 bcvb=Buf("cvb1"), bsq=Buf("sq1"), bmean=Buf("mean1"), bmsq=Buf("msq1"), brstd=Buf("rstd1"), first=list(bglu)),
        ]
        def ln_stats(tb):
            sl = slice(tb * 512, (tb + 1) * 512)
            L = ln_sets[tb % 2]
            xw = L["first"] if tb < 2 else []
            cvb_, sq_, mean_, msq_, rstd_ = L["cvb"], L["sq"], L["mean"], L["msq"], L["rstd"]
            bcvb_, bsq_, bmean_, bmsq_, brstd_ = L["bcvb"], L["bsq"], L["bmean"], L["bmsq"], L["brstd"]
            s.op("act", lambda e: e.activation(out=cvb_[:], in_=cvo[:, :, sl], func=AF.Copy),
                 reads=[bcvo[c][tb] for c in range(4)], writes=[bcvb_] + xw)
            s.op("act", lambda e: e.activation(out=sq_[:], in_=cvo[:, :, sl], func=AF.Square),
                 reads=[bcvo[c][tb] for c in range(4)], writes=[bsq_] + xw)
            pm, bpm = self.bank((4, 5, 6))
            self.mm_group(pm[:], [(self.onesb[:], cvb_[:, c, :]) for c in range(4)], reads=[bcvb_, self.bconst], writes=[bpm])
            pq, bpq = self.bank((4, 5, 6))
            self.mm_group(pq[:], [(self.onesb[:], sq_[:, c, :]) for c in range(4)], reads=[bsq_, self.bconst], writes=[bpq])
            s.op("dve", lambda e: e.tensor_scalar_mul(out=mean_[:], in0=pm[:], scalar1=1.0 / 512), reads=[bpm], writes=[bmean_] + xw)
            s.op("dve", lambda e: e.tensor_tensor(out=msq_[:], in0=mean_[:], in1=mean_[:], op=ALU.mult), reads=[bmean_], writes=[bmsq_] + xw)
            s.op("dve", lambda e: e.scalar_tensor_tensor(out=rstd_[:], in0=pq[:], scalar=1.0 / 512, in1=msq_[:], op0=ALU.mult, op1=ALU.subtract),
                 reads=[bpq, bmsq_], writes=[brstd_] + xw)
            s.op("dve", lambda e: e.tensor_scalar_add(out=rstd_[:], in0=rstd_[:], scalar1=EPS), reads=[brstd_], writes=[brstd_])
            s.op("act", lambda e: e.activation(out=rstd_[:], in_=rstd_[:], func=AF.Ln), reads=[brstd_], writes=[brstd_])
            s.op("act", lambda e: e.activation(out=rstd_[:], in_=rstd_[:], func=AF.Exp, scale=-0.5), reads=[brstd_], writes=[brstd_])

        def ln_norm(tb):
            sl = slice(tb * 512, (tb + 1) * 512)
            L = ln_sets[tb % 2]
            mean_, rstd_, bmean_, brstd_ = L["mean"], L["rstd"], L["bmean"], L["brstd"]
            def gate(c):
                k = c % 2
                s.op("dve", lambda e: e.tensor_tensor(out=self.Y[:, c, sl], in0=tmp[k][:], in1=self.Y[:, c, sl], op=ALU.mult),
                     reads=[btmp[k], self.bY[c][tb]], writes=[self.bY[c][tb]])
            for c in range(4):
                k = c % 2
                s.op("dve", lambda e, c=c, k=k: e.tensor_tensor(out=tmp[k][:], in0=cvo[:, c, sl], in1=mean_[:], op=ALU.subtract),
                     reads=[bcvo[c][tb], bmean_], writes=[btmp[k]])
                s.op("dve", lambda e, k=k: e.tensor_tensor(out=tmp[k][:], in0=tmp[k][:], in1=rstd_[:], op=ALU.mult),
                     reads=[btmp[k], brstd_], writes=[btmp[k]])
                s.op("act", lambda e, c=c, k=k: e.activation(out=tmp[k][:], in_=tmp[k][:], func=AF.Silu,
                                                             scale=self.pcol(l, P_LG, c), bias=self.pcol(l, P_LB, c)),
                     reads=[btmp[k], self.bconst], writes=[btmp[k]])
                if c > 0:
                    gate(c - 1)
            gate(3)

        ln_stats(0)
        for tb in range(NTB):
            if tb + 1 < NTB:
                ln_stats(tb + 1)
            ln_norm(tb)
        self.dump_y(l * 4 + 0)
        self.s.barrier()
        self.out_proj(l, 0)

    def stage_ret(self, l):
        s = self.s
        self.scr_reset()
        scr = self.scr
        cstv = self.cst
        IB = (0, 1, 2, 3)
        RB = (4, 5, 6)
        tbl = [[scr("tbl", [128, 512], F32) for _ in range(2)] for _ in range(2)]
        btbl = [Buf("tbl0"), Buf("tbl1")]
        qT = [scr("qT", [128, SEQ], BF16) for _ in range(2)]
        kT = [scr("kT", [128, SEQ], BF16) for _ in range(2)]
        ktok = [scr("ktok", [128, NT, 128], BF16) for _ in range(2)]
        vtok = [scr("vtok", [128, NT, 128], BF16) for _ in range(2)]
        bqT = [[Buf("qT%d_%d" % (k, i)) for i in range(4)] for k in range(2)]
        bkT = [[Buf("kT%d_%d" % (k, i)) for i in range(4)] for k in range(2)]
        bktok = [[Buf("ktok%d_%d" % (k, i)) for i in range(4)] for k in range(2)]
        bvtok = [[Buf("vtok%d_%d" % (k, i)) for i in range(4)] for k in range(2)]
        kdec = scr("kdec", [128, NT, 128], BF16)
        bkdec = Buf("kdec")
        qf = scr("qf", [128, SEQ], BF16)
        qb = scr("qb", [128, SEQ], BF16)
        bqf = [Buf("qf%d" % i) for i in range(4)]
        bqb = [Buf("qb%d" % i) for i in range(4)]
        srun = [scr("srun", [128, 128], F32) for _ in range(2)]
        bsrun = [Buf("srun0"), Buf("srun1")]
        Sf = scr("Sf", [128, NT, 128], BF16)
        Sb = scr("Sb", [128, NT, 128], BF16)
        bSf, bSb = Buf("Sf"), Buf("Sb")
        ATb = [scr("ATb", [128, 4, 128], BF16) for _ in range(2)]
        onb = [scr("onb", [128, 4, 128], BF16) for _ in range(2)]
        st = [scr("st", [128, 4, 6], F32) for _ in range(2)]
        mv = [scr("mv", [128, 4, 2], F32) for _ in range(2)]
        rs = [scr("rs", [128, 4], F32) for _ in range(2)]
        bATb = [Buf("ATb0"), Buf("ATb1")]
        bonb = [Buf("onb0"), Buf("onb1")]
        bst = [Buf("st0"), Buf("st1")]
        lg = scr("lg", [128, 8], F32)
        gC = scr("gC", [128, 8], F32)
        KD = scr("KD", [128, 8], F32)
        DT = [scr("DT", [128, 128], F32) for _ in range(2)]
        AFm = scr("AFm", [128, 128], F32)
        ABm = scr("ABm", [128, 128], F32)
        dtmp = scr("dtmp", [128, 128], F32)
        bdec = Buf("dec")
        bhd = [Buf("hd0"), Buf("hd1")]
        t1 = scr("t1", [128, 512], F32)
        t2 = scr("t2", [128, 512], F32)
        bt1, bt2 = Buf("t1"), Buf("t2")
        wBk = scr("wBk", [128, 8, 128], BF16)
        bwBk = Buf("wBk")

        dl = self.par[:, l * PL + P_DL:l * PL + P_DL + 8]
        rd, wr = [self.bconst, bdec], [bdec]
        s.op("act", lambda e: e.activation(out=lg[:], in_=dl, func=AF.Exp, scale=-1.0), reads=rd, writes=wr)
        s.op("dve", lambda e: e.tensor_scalar_add(out=lg[:], in0=lg[:], scalar1=1.0), reads=rd, writes=wr)
        s.op("act", lambda e: e.activation(out=lg[:], in_=lg[:], func=AF.Ln), reads=rd, writes=wr)
        s.op("dve", lambda e: e.tensor_scalar_mul(out=lg[:], in0=lg[:], scalar1=-1.0), reads=rd, writes=wr)
        s.op("act", lambda e: e.activation(out=gC[:], in_=lg[:], func=AF.Exp, scale=128.0), reads=rd, writes=wr)
        for h in range(4):
            lf, lb = lg[:, h:h + 1], lg[:, 4 + h:5 + h]
            s.op("act", lambda e, h=h, lf=lf: e.activation(out=KD[:, h:h + 1], in_=cstv[:, C_KJ:C_KJ + 1], func=AF.Exp, scale=lf), reads=rd, writes=wr)
            s.op("act", lambda e, h=h, lb=lb: e.activation(out=KD[:, 4 + h:5 + h], in_=cstv[:, C_KJ + 1:C_KJ + 2], func=AF.Exp, scale=lb), reads=rd, writes=wr)

        self.gates(l)
        ntab = [0]

        def inproj_steps(h):
            k = h % 2
            st_ = {}
            steps = []

            def svpop():
                off, bw = self.wpop()
                st_["bwv"] = bw
                st_["wv"] = self.view(off, "wv", [128, 8, 128], BF16)
                st_["wBq"] = self.view(off + 2048, "wBq", [128, 8, 128], BF16)

            def svmm():
                bw, wv = st_["bwv"], st_["wv"]
                for tg in range(4):
                    pt, bpt = self.bank(IB)

                    def fv(e, tg=tg, pt=pt):
                        ins = None
                        for i in range(4):
                            for kc in range(8):
                                ins = e.matmul(pt[:, i * 128:(i + 1) * 128], lhsT=self.HT[:, tg, kc, i * 128:(i + 1) * 128], rhs=wv[:, kc, :],
                                               start=(kc == 0), stop=(kc == 7))
                        return ins
                    s.op("pe", fv, reads=[bw, self.bHT[tg]], writes=[bpt])
                    s.op("act", lambda e, tg=tg, pt=pt: e.activation(out=vtok[k][:, tg * 4:(tg + 1) * 4, :],
                                                                    in_=pt[:].rearrange("p (a b) -> p a b", a=4), func=AF.Copy),
                         reads=[bpt], writes=[bvtok[k][tg]])
            steps.append(svpop)

            def sw():
                off, bw = self.wpop_keep()
                st_["bw"] = bw
                st_["wA"] = wA = self.view(off, "wqk", [128, 8, 256], BF16)
                wBq, bwv = st_["wBq"], st_["bwv"]
                s.op("act", lambda e: e.activation(out=wBq[:, :, 0:64], in_=wA[:, :, 64:128], func=AF.Copy, scale=-1.0), reads=[bw], writes=[bwv])
                s.op("act", lambda e: e.activation(out=wBq[:, :, 64:128], in_=wA[:, :, 0:64], func=AF.Copy), reads=[bw], writes=[bwv])
                s.op("act", lambda e: e.activation(out=wBk[:, :, 0:64], in_=wA[:, :, 192:256], func=AF.Copy, scale=-1.0), reads=[bw], writes=[bwBk])
                s.op("act", lambda e: e.activation(out=wBk[:, :, 64:128], in_=wA[:, :, 128:192], func=AF.Copy), reads=[bw], writes=[bwBk])
            steps.append(sw)
            for tb in range(NTB):
                def stb(tb=tb):
                    bw, wA, wBq, bwv = st_["bw"], st_["wA"], st_["wBq"], st_["bwv"]
                    sl = slice(tb * 512, (tb + 1) * 512)
                    kk = ntab[0] % 2
                    ntab[0] += 1
                    s.dma("sp", "tb%d" % kk, lambda e, sem: (
                        e.dma_start(out=tbl[kk][0][:], in_=self.ropeR_d[0][:, sl]).then_inc(sem, 16),
                        e.dma_start(out=tbl[kk][1][:], in_=self.ropeR_d[1][:, sl]).then_inc(sem, 16)), 2, writes=[btbl[kk]])
                    cos, sin = tbl[kk][0], tbl[kk][1]
                    p1, bp1 = self.inproj_fm(wA, bw, 0, 128, tb, banks=IB)
                    p2, bp2 = self.inproj_fm(wBq, bwv, 0, 128, tb, banks=IB)
                    s.op("dve", lambda e: e.tensor_tensor(out=t1[:], in0=p1[:], in1=cos[:], op=ALU.mult), reads=[bp1, btbl[kk]], writes=[bt1])
                    s.op("dve", lambda e: e.tensor_tensor(out=t2[:], in0=p2[:], in1=sin[:], op=ALU.mult), reads=[bp2, btbl[kk]], writes=[bt2])
                    s.op("pool", lambda e: e.tensor_tensor(out=qT[k][:, sl], in0=t1[:], in1=t2[:], op=ALU.add), reads=[bt1, bt2], writes=[bqT[k][tb]])
                    p3, bp3 = self.inproj_fm(wA, bw, 128, 128, tb, banks=IB)
                    p4, bp4 = self.inproj_fm(wBk, bwBk, 0, 128, tb, banks=IB)
                    s.op("dve", lambda e: e.tensor_tensor(out=t1[:], in0=p3[:], in1=cos[:], op=ALU.mult), reads=[bp3, btbl[kk]], writes=[bt1])
                    s.op("dve", lambda e: e.scalar_tensor_tensor(out=t2[:], in0=p4[:], scalar=RET_SCALE, in1=sin[:], op0=ALU.mult, op1=ALU.mult),
                         reads=[bp4, btbl[kk]], writes=[bt2])
                    s.op("dve", lambda e: e.scalar_tensor_tensor(out=kT[k][:, sl], in0=t1[:], scalar=RET_SCALE, in1=t2[:], op0=ALU.mult, op1=ALU.add),
                         reads=[bt1, bt2], writes=[bkT[k][tb]])
                    pb, bpb = self.bbank()

                    def ftr(e):
                        ins = None
                        for i in range(4):
                            n = tb * 4 + i
                            ins = e.transpose(out=pb[:, i * 128:(i + 1) * 128], in_=kT[k][:, n * 128:(n + 1) * 128], identity=self.identb[:])
                        return ins
                    s.op("pe", ftr, reads=[bkT[k][tb], self.bconst], writes=[bpb])
                    s.op("act", lambda e: e.activation(out=ktok[k][:, tb * 4:(tb + 1) * 4, :], in_=pb.rearrange("p (a b) -> p a b", a=4), func=AF.Copy),
                         reads=[bpb], writes=[bktok[k][tb]])
                steps.append(stb)
            steps.append(svmm)
            return steps

        def rest_steps(h):
            k = h % 2
            steps = []
            lf, lb = lg[:, h:h + 1], lg[:, 4 + h:5 + h]

            def sconst():
                rdh, wrh = [self.bconst, bdec, bhd[k]], [bhd[k]]
                s.op("dve", lambda e: e.tensor_scalar_mul(out=dtmp[:], in0=cstv[:, C_P1:C_P1 + 128], scalar1=lf), reads=rdh + [bhd[1 - k]], writes=wrh + [bhd[1 - k]])
                s.op("dve", lambda e: e.scalar_tensor_tensor(out=dtmp[:], in0=cstv[:, C_P2:C_P2 + 128], scalar=lb, in1=dtmp[:], op0=ALU.mult, op1=ALU.add),
                     reads=rdh + [bhd[1 - k]], writes=wrh + [bhd[1 - k]])
                s.op("act", lambda e: e.activation(out=DT[k][:], in_=dtmp[:], func=AF.Exp), reads=rdh + [bhd[1 - k]], writes=wrh)

            def make_chain(direction):
                tg_order = [0, 1, 2, 3] if direction == 0 else [3, 2, 1, 0]
                kvb = {}

                def emit_kv(tg):
                    pt, bpt = self.bank(RB)

                    def fkv(e, tg=tg, pt=pt):
                        ins = None
                        for i in range(4):
                            n = tg * 4 + i
                            ins = e.matmul(pt[:, i * 128:(i + 1) * 128], lhsT=kdec[:, n, :], rhs=vtok[k][:, n, :], start=True, stop=True)
                        return ins
                    s.op("pe", fkv, reads=[bkdec, bvtok[k][tg]], writes=[bpt])
                    kvb[tg] = (pt, bpt)

                def skv_k():
                    kdcol = KD[:, direction * 4 + h:direction * 4 + h + 1]
                    s.op("act", lambda e: e.activation(out=kdec[:], in_=ktok[k][:], func=AF.Copy, scale=kdcol), reads=bktok[k] + [bdec], writes=[bkdec])

                def skv_m():
                    for tg in tg_order[:3]:
                        emit_kv(tg)

                def schain():
                    gcol = gC[:, direction * 4 + h:direction * 4 + h + 1]
                    Sst, bSst = (Sf, bSf) if direction == 0 else (Sb, bSb)
                    order = list(range(0, 15)) if direction == 0 else list(range(15, 0, -1))
                    for step, n in enumerate(order):
                        if n // 4 not in kvb:
                            emit_kv(n // 4)
                        pt, bpt = kvb[n // 4]
                        kvn = pt[:, (n % 4) * 128:(n % 4 + 1) * 128]
                        dst = n + 1 if direction == 0 else n - 1
                        cur, nxt = step % 2, (step + 1) % 2
                        if step == 0:
                            s.op("dve", lambda e, kvn=kvn, nxt=nxt: e.tensor_copy(out=srun[nxt][:], in_=kvn), reads=[bpt], writes=[bsrun[nxt]])
                        else:
                            s.op("dve", lambda e, kvn=kvn, cur=cur, nxt=nxt: e.scalar_tensor_tensor(
                                out=srun[nxt][:], in0=srun[cur][:], scalar=gcol, in1=kvn, op0=ALU.mult, op1=ALU.add),
                                reads=[bpt, bsrun[cur], bdec], writes=[bsrun[nxt]])
                        s.op("act", lambda e, nxt=nxt, dst=dst: e.activation(out=Sst[:, dst, :], in_=srun[nxt][:], func=AF.Copy),
                             reads=[bsrun[nxt]], writes=[bSst])
                return skv_k, skv_m, schain
            make_chain.kvb = {}
            skvk_f, skvm_f, schain_f = make_chain(0)
            skvk_b, skvm_b, schain_b = make_chain(1)

            def sqfb():
                rdh = [self.bconst, bdec, bhd[0], bhd[1]]
                s.op("act", lambda e: e.activation(out=AFm[:], in_=cstv[:, C_I1:C_I1 + 128], func=AF.Exp, scale=lf), reads=rdh, writes=[bhd[0], bhd[1]])
                s.op("act", lambda e: e.activation(out=ABm[:], in_=cstv[:, C_I2:C_I2 + 128], func=AF.Exp, scale=lb), reads=rdh, writes=[bhd[0], bhd[1]])
                for tb in range(NTB):
                    sl = slice(tb * 512, (tb + 1) * 512)
                    s.op("dve", lambda e, sl=sl: e.tensor_tensor(out=qf[:, sl].rearrange("p (a b) -> p a b", a=4),
                                                                 in0=qT[k][:, sl].rearrange("p (a b) -> p a b", a=4),
                                                                 in1=bc(AFm[:], 0, 4), op=ALU.mult),
                         reads=[bqT[k][tb], bhd[0], bhd[1]], writes=[bqf[tb]])
                    s.op("dve", lambda e, sl=sl: e.tensor_tensor(out=qb[:, sl].rearrange("p (a b) -> p a b", a=4),
                                                                 in0=qT[k][:, sl].rearrange("p (a b) -> p a b", a=4),
                                                                 in1=bc(ABm[:], 0, 4), op=ALU.mult),
                         reads=[bqT[k][tb], bhd[0], bhd[1]], writes=[bqb[tb]])

            def make_out(tg):
                kk = tg % 2
                box = {}

                def souts():
                    pa, bpa = self.bank(RB)

                    def fa(e):
                        ins = None
                        for i in range(4):
                            n = tg * 4 + i
                            ins = e.matmul(pa[:, i * 128:(i + 1) * 128], lhsT=kT[k][:, n * 128:(n + 1) * 128], rhs=qT[k][:, n * 128:(n + 1) * 128],
                                           start=True, stop=True)
                        return ins
                    s.op("pe", fa, reads=[bkT[k][tg], bqT[k][tg]], writes=[bpa])
                    s.op("dve", lambda e: e.tensor_tensor(out=ATb[kk][:], in0=pa[:].rearrange("p (a b) -> p a b", a=4),
                                                          in1=bc(DT[k][:], 0, 4), op=ALU.mult),
                         reads=[bpa, bhd[k]], writes=[bATb[kk]])

                def souta():
                    po, bpo = self.bank(RB)
                    box["po"] = (po, bpo)

                    def fo(e):
                        ins = None
                        for i in range(4):
                            n = tg * 4 + i
                            terms = [(ATb[kk][:, i, :], vtok[k][:, n, :])]
                            if n > 0:
                                terms.append((qf[:, n * 128:(n + 1) * 128], Sf[:, n, :]))
                            if n < 15:
                                terms.append((qb[:, n * 128:(n + 1) * 128], Sb[:, n, :]))
                            for j, (lt, rh) in enumerate(terms):
                                ins = e.matmul(po[:, i * 128:(i + 1) * 128], lhsT=lt, rhs=rh, start=(j == 0), stop=(j == len(terms) - 1))
                        return ins
                    s.op("pe", fo, reads=[bATb[kk], bvtok[k][tg], bqf[tg], bqb[tg], bSf, bSb], writes=[bpo])
                    for i in range(4):
                        s.op("dve", lambda e, i=i: e.bn_stats(out=st[kk][:, i, :], in_=po[:, i * 128:(i + 1) * 128]), reads=[bpo], writes=[bst[kk]])
                    for i in range(4):
                        s.op("dve", lambda e, i=i: e.bn_aggr(out=mv[kk][:, i, :], in_=st[kk][:, i, :]), reads=[bst[kk]], writes=[bst[kk]])
                    s.op("dve", lambda e: e.tensor_scalar_add(out=rs[kk][:], in0=mv[kk][:, :, 1], scalar1=EPS), reads=[bst[kk]], writes=[bst[kk]])
                    s.op("act", lambda e: e.activation(out=rs[kk][:], in_=rs[kk][:], func=AF.Ln), reads=[bst[kk]], writes=[bst[kk]])
                    s.op("act", lambda e: e.activation(out=rs[kk][:], in_=rs[kk][:], func=AF.Exp, scale=-0.5), reads=[bst[kk]], writes=[bst[kk]])

                def soutb():
                    po, bpo = box["po"]
                    for i in range(4):
                        s.op("dve", lambda e, i=i: e.tensor_scalar(out=onb[kk][:, i, :], in0=po[:, i * 128:(i + 1) * 128],
                                                                   scalar1=mv[kk][:, i, 0:1], scalar2=rs[kk][:, i:i + 1],
                                                                   op0=ALU.subtract, op1=ALU.mult),
                             reads=[bpo, bst[kk]], writes=[bonb[kk]])
                    pb, bpb = self.bbank()

                    def ftr2(e):
                        ins = None
                        for i in range(4):
                            ins = e.transpose(out=pb[:, i * 128:(i + 1) * 128], in_=onb[kk][:, i, :], identity=self.identb[:])
                        return ins
                    s.op("pe", ftr2, reads=[bonb[kk], self.bconst], writes=[bpb])
                    s.op("dve", lambda e: e.tensor_tensor(out=self.Y[:, h, tg * 512:(tg + 1) * 512], in0=pb,
                                                          in1=self.Y[:, h, tg * 512:(tg + 1) * 512], op=ALU.mult),
                         reads=[bpb, self.bY[h][tg]], writes=[self.bY[h][tg]])
                return souts, souta, soutb
            outs = [make_out(tg) for tg in range(4)]
            return dict(sconst=sconst, skvk_f=skvk_f, skvm_f=skvm_f, schain_f=schain_f, skvk_b=skvk_b, skvm_b=skvm_b, schain_b=schain_b,
                        sqfb=sqfb, outs=outs)

        p0 = inproj_steps(0)
        for i in (0, 1, 6, 2, 3, 4, 5):
            p0[i]()
        for h in range(4):
            nxt = inproj_steps(h + 1) if h + 1 < 4 else [lambda: None] * 7
            r = rest_steps(h)
            r["sconst"]()
            r["sqfb"]()
            r["skvk_f"]()
            r["skvm_f"]()
            nxt[0]()
            nxt[1]()
            r["schain_f"]()
            r["skvk_b"]()
            nxt[2]()
            r["skvm_b"]()
            r["schain_b"]()
            nxt[3]()
            fill = [nxt[4], nxt[5], nxt[6], (lambda: None)]
            r["outs"][0][0]()
            for tg in range(4):
                sc, a, b = r["outs"][tg]
                if tg + 1 < 4:
                    r["outs"][tg + 1][0]()
                a()
                fill[tg]()
                b()
        self.dump_y(l * 4 + 1)
        self.s.barrier()
        self.out_proj(l, 1)

    def stage_mla(self, l):
        s = self.s
        self.scr_reset()
        scr = self.scr
        ALLB = (0, 1, 2, 3, 4, 5, 6)
        qlatn = scr("qlatn", [128, 3, SEQ], BF16)
        kvlatn = scr("kvlatn", [128, 2, SEQ], BF16)
        kr = scr("kr", [128, SEQ], BF16)
        bqlat = [Buf("qlat%d" % i) for i in range(4)]
        bkvlat = [Buf("kvlat%d" % i) for i in range(4)]
        bkr = Buf("kr")
        tblm_off = self.off
        tblm = [[scr("tblm", [128, 512], F32) for _ in range(2)] for _ in range(2)]
        btblm = [Buf("tblm0"), Buf("tblm1")]
        Vaug = [scr("Vaug", [128, NT, 128], BF16) for _ in range(2)]
        bV = [Buf("V0"), Buf("V1")]
        self.ptoff = self.off
        PT = scr("PT", [128, NPT, 512], BF16)
        bPT = [Buf("PT%d" % i) for i in range(NPT)]
        rsum_off = self.off
        rsum = scr("rsum", [128, 512], F32)
        brsum = Buf("rsum")
        rstdL_off = self.off
        rstdL = scr("rstdL", [128, 512], F32)
        brstdL = Buf("rstdL")
        rinv, brinv = rstdL, brstdL
        t2v = self.nc.alloc_sbuf_tensor_at("mt2v_%d" % l, [128, 512], F32, offset=rstdL_off)
        bt2 = brstdL
        hilo = self.nc.alloc_sbuf_tensor_at("hilo_%d" % l, [128, 2, 512], BF16, offset=tblm_off)
        bhilo = btblm[0]
        qn = [scr("qn", [128, SEQ], BF16) for _ in range(2)]
        qr = [scr("qr", [128, SEQ], BF16) for _ in range(2)]
        kn = [scr("kn", [128, SEQ], BF16) for _ in range(2)]
        bqn = [[Buf("qn%d_%d" % (k, i)) for i in range(4)] for k in range(2)]
        bqr = [[Buf("qr%d_%d" % (k, i)) for i in range(4)] for k in range(2)]
        bkn = [Buf("kn0"), Buf("kn1")]
        t1 = scr("mt1", [128, 512], F32)
        bt1 = Buf("mt1")
        wkrot = self.view(rsum_off, "wkrot", [128, 8, 128], BF16)
        bwkrot = brsum
        ntab = [0]

        s.op("pool", lambda e: e.memset(kr[0:64, :], 0.0), writes=[bkr])
        for hs in range(2):
            s.op("pool", lambda e, hs=hs: e.memset(qr[hs][0:64, :], 0.0), writes=bqr[hs])

        def load_tables(tb):
            k = ntab[0] % 2
            ntab[0] += 1
            sl = slice(tb * 512, (tb + 1) * 512)
            s.dma("sp", "tm%d" % k, lambda e, sem, k=k, sl=sl: (
                e.dma_start(out=tblm[k][0][64:128, :], in_=self.ropeM_d[0][:, sl]).then_inc(sem, 16),
                e.dma_start(out=tblm[k][1][64:128, :], in_=self.ropeM_d[1][:, sl]).then_inc(sem, 16)), 2, writes=[btblm[k]])
            return tblm[k][0], tblm[k][1], btblm[k]

        def rope64(p1, bp1, p2, bp2, tb, dst, bdst):
            cos, sin, btb = load_tables(tb)
            sl = slice(tb * 512, (tb + 1) * 512)
            s.op("dve", lambda e: e.tensor_tensor(out=t1[64:128, :], in0=p1[64:128, :], in1=cos[64:128, :], op=ALU.mult), reads=[bp1, btb], writes=[bt1])
            s.op("dve", lambda e: e.tensor_tensor(out=t2v[64:128, :], in0=p2[64:128, :], in1=sin[64:128, :], op=ALU.mult), reads=[bp2, btb], writes=[bt2])
            s.op("dve", lambda e: e.tensor_tensor(out=dst[64:128, sl], in0=t1[64:128, :], in1=t2v[64:128, :], op=ALU.add), reads=[bt1, bt2], writes=[bdst])

        def latent_norm(ws, nch, gbase, dstT, bdst):
            for tb in range(NTB):
                sl = slice(tb * 512, (tb + 1) * 512)
                raws = []
                for c in range(nch):
                    wt, bw, col = ws[c]
                    raws.append(self.inproj_fm(wt, bw, col, 128, tb, banks=(0, 1, 2, 4, 5, 6)))
                for c, (pt, bpt) in enumerate(raws):
                    s.op("act", lambda e, c=c, pt=pt: e.activation(out=PT[:, c, :], in_=pt[:], func=AF.Square), reads=[bpt], writes=[bPT[c]])
                psum, bpsum = self.bank((3,))
                self.mm_group(psum[:], [(self.onesb[:], PT[:, c, :]) for c in range(nch)], reads=[bPT[c] for c in range(nch)] + [self.bconst],
                              writes=[bpsum])
                s.op("dve", lambda e, psum=psum: e.tensor_scalar(out=rstdL[:], in0=psum[:], scalar1=1.0 / (128 * nch), scalar2=EPS,
                                                               op0=ALU.mult, op1=ALU.add), reads=[bpsum], writes=[brstdL])
                s.op("act", lambda e: e.activation(out=rstdL[:], in_=rstdL[:], func=AF.Ln), reads=[brstdL], writes=[brstdL])
                s.op("act", lambda e: e.activation(out=rstdL[:], in_=rstdL[:], func=AF.Exp, scale=-0.5), reads=[brstdL], writes=[brstdL])
                for c, (pt, bpt) in enumerate(raws):
                    s.op("dve", lambda e, c=c, pt=pt, sl=sl: e.scalar_tensor_tensor(out=dstT[:, c, sl], in0=pt[:], scalar=self.pcol(l, gbase, c),
                                                                                    in1=rstdL[:], op0=ALU.mult, op1=ALU.mult),
                         reads=[bpt, brstdL, self.bconst], writes=[bdst[tb]])

        (off1, bw1), (off2, bw2) = self.wpop(2)
        wq1 = self.view(off1, "wq1", [128, 8, 256], BF16)
        wq2 = self.view(off2, "wq2", [128, 8, 128], BF16)
        latent_norm([(wq1, bw1, 0), (wq1, bw1, 128), (wq2, bw2, 0)], 3, P_QG, qlatn, bqlat)
        off, bw = self.wpop()
        wkv = self.view(off, "wkv", [128, 8, 256], BF16)
        latent_norm([(wkv, bw, 0), (wkv, bw, 128)], 2, P_KG, kvlatn, bkvlat)
        off, bw = self.wpop()
        wkr = self.view(off, "wkr", [128, 8, 128], BF16)
        s.op("pool", lambda e: e.tensor_copy(out=wkrot[:, :, 0:64], in_=wkr[:, :, 0:64]), reads=[bw], writes=[bwkrot])
        s.op("pool", lambda e: e.tensor_scalar_mul(out=wkrot[:, :, 64:96], in0=wkr[:, :, 96:128], scalar1=-1.0), reads=[bw], writes=[bwkrot])
        s.op("pool", lambda e: e.tensor_copy(out=wkrot[:, :, 96:128], in_=wkr[:, :, 64:96]), reads=[bw], writes=[bwkrot])
        for tb in range(NTB):
            p1, bp1 = self.inproj_fm(wkr, bw, 0, 128, tb, banks=ALLB)
            p2, bp2 = self.inproj_fm(wkrot, bwkrot, 0, 128, tb, banks=ALLB)
            rope64(p1, bp1, p2, bp2, tb, kr, bkr)

        PREPB = (5, 6) if self.interleave else ALLB

        def prep_steps(h):
            hs = h % 2
            st_ = {}
            steps = []

            def s0():
                off, bw = self.wpop()
                st_["bw"] = bw
                st_["wa"] = wa = self.view(off, "wa", [128, 3, 192], BF16)
                st_["wc"] = self.view(off + 1536, "wc", [128, 2, 256], BF16)
                st_["wr"] = wr = self.view(off + 2560, "wr", [128, 3, 128], BF16)
                s.op("pool", lambda e: e.tensor_copy(out=wr[:, :, 0:64], in_=wa[:, :, 64:128]), reads=[bw], writes=[bw])
                s.op("pool", lambda e: e.tensor_scalar_mul(out=wr[:, :, 64:96], in0=wa[:, :, 160:192], scalar1=-1.0), reads=[bw], writes=[bw])
                s.op("pool", lambda e: e.tensor_copy(out=wr[:, :, 96:128], in_=wa[:, :, 128:160]), reads=[bw], writes=[bw])
            steps.append(s0)
            ksteps, qsteps, vsteps = [], [], []
            for tb in range(NTB):
                sl = slice(tb * 512, (tb + 1) * 512)

                def sqn(tb=tb, sl=sl):
                    bw, wa = st_["bw"], st_["wa"]
                    pt, bpt = self.bank(PREPB)
                    self.mm_group(pt[:], [(wa[:, kc, 0:128], qlatn[:, kc, sl]) for kc in range(3)], reads=[bw, bqlat[tb]], writes=[bpt])
                    s.op("act", lambda e: e.activation(out=qn[hs][:, sl], in_=pt[:], func=AF.Copy), reads=[bpt], writes=[bqn[hs][tb]])

                def sqr(tb=tb, sl=sl):
                    bw, wa, wr = st_["bw"], st_["wa"], st_["wr"]
                    p1, bp1 = self.bank(PREPB)
                    self.mm_group(p1[:], [(wa[:, kc, 64:192], qlatn[:, kc, sl]) for kc in range(3)], reads=[bw, bqlat[tb]], writes=[bp1])
                    p2, bp2 = self.bank(PREPB)
                    self.mm_group(p2[:], [(wr[:, kc, :], qlatn[:, kc, sl]) for kc in range(3)], reads=[bw, bqlat[tb]], writes=[bp2])
                    rope64(p1, bp1, p2, bp2, tb, qr[hs], bqr[hs][tb])

                def skn(tb=tb, sl=sl):
                    bw, wc = st_["bw"], st_["wc"]
                    pt, bpt = self.bank(PREPB)
                    self.mm_group(pt[:], [(wc[:, kc, 0:128], kvlatn[:, kc, sl]) for kc in range(2)], reads=[bw, bkvlat[tb]], writes=[bpt])
                    s.op("act", lambda e: e.activation(out=kn[hs][:, sl], in_=pt[:], func=AF.Copy), reads=[bpt], writes=[bkn[hs]])
                ksteps.append(skn)
                qsteps += [sqn, sqr]
            for tg in range(4):
                def sv(tg=tg):
                    bw, wc = st_["bw"], st_["wc"]
                    pt, bpt = self.bank(PREPB)

                    def fv(e):
                        ins = None
                        for i in range(4):
                            t = tg * 4 + i
                            for kc in range(2):
                                ins = e.matmul(pt[:, i * 128:(i + 1) * 128], lhsT=kvlatn[:, kc, t * 128:(t + 1) * 128], rhs=wc[:, kc, 128:256],
                                               start=(kc == 0), stop=(kc == 1))
                        return ins
                    s.op("pe", fv, reads=[bw, bkvlat[tg]], writes=[bpt])
                    s.op("act", lambda e: e.activation(out=Vaug[hs][:, tg * 4:(tg + 1) * 4, :],
                                                       in_=pt[:].rearrange("p (a b) -> p a b", a=4), func=AF.Copy),
                         reads=[bpt], writes=[bV[hs]])
                vsteps.append(sv)
            return steps + ksteps + vsteps + qsteps

        psbF = self.psb[:].bitcast(F32)
        sbanks = [(self.ps[0], self.bps[0]), (self.ps[1], self.bps[1]), (self.ps[6], self.bps[6]), (psbF, self.bpsb[0])]

        def attention(h, pending):
            hs = h % 2
            hh = h % 4
            fin = {"pe": None, "tail": None}
            for qb in range(4):
                qsl = slice(qb * 512, (qb + 1) * 512)
                sbank = 4 + (self.nqb % 2)
                self.nqb += 1
                pss, bpss = self.ps[sbank], self.bps[sbank]

                def emit_s(kt, qsl=qsl):
                    ps_, bps_ = sbanks[self.sctr % 4]
                    self.sctr += 1

                    def fs(e, kt=kt, ps_=ps_, qsl=qsl):
                        e.matmul(ps_[:], lhsT=kn[hs][:, kt * 128:(kt + 1) * 128], rhs=qn[hs][:, qsl], start=True, stop=False)
                        return e.matmul(ps_[:], lhsT=kr[:, kt * 128:(kt + 1) * 128], rhs=qr[hs][:, qsl], start=False, stop=True)
                    s.op("pe", fs, reads=[bkn[hs], bqn[hs][qb], bkr, bqr[hs][qb]], writes=[bps_])
                    return ps_, bps_
                acc, bacc = self.bank((2, 3))
                sq_ = [emit_s(0), emit_s(1), emit_s(2), emit_s(3)]
                if fin["pe"]:
                    fin["pe"]()
                    fin["pe"] = None
                for kt in range(NT):
                    ps_, bps_ = sq_[kt]
                    pi = (self.nqb * NT + kt) % NPT
                    s.op("act", lambda e, ps_=ps_, pi=pi: e.activation(out=PT[:, pi, :], in_=ps_[:], func=AF.Exp, scale=MLA_SCALE),
                         reads=[bps_], writes=[bPT[pi]])
                    if kt + 4 < NT:
                        sq_.append(emit_s(kt + 4))
                    s.op("pe", lambda e, kt=kt, pi=pi, acc=acc: e.matmul(acc[:], lhsT=Vaug[hs][:, kt, :], rhs=PT[:, pi, :],
                                                                         start=(kt == 0), stop=(kt == NT - 1)),
                         reads=[bPT[pi], bV[hs]], writes=[bacc])
                    if kt % 2 == 0:
                        if kt == 0:
                            s.op("dve", lambda e, pi=pi: e.tensor_copy(out=rsum[:], in_=PT[:, pi, :]), reads=[bPT[pi]], writes=[brsum])
                        else:
                            s.op("dve", lambda e, pi=pi: e.tensor_tensor(out=rsum[:], in0=rsum[:], in1=PT[:, pi, :], op=ALU.add),
                                 reads=[bPT[pi], brsum], writes=[brsum])
                    else:
                        s.op("pe", lambda e, kt=kt, pi=pi, pss=pss: e.matmul(pss[:], lhsT=self.onesb[:], rhs=PT[:, pi, :], start=(kt == 1), stop=False),
                             reads=[bPT[pi], self.bconst], writes=[bpss])
                    if kt == 1 and fin["tail"]:
                        fin["tail"]()
                        fin["tail"] = None
                    if pending:
                        pending.pop(0)()
                s.op("dve", lambda e: e.tensor_copy(out=hilo[:, 0, :], in_=rsum[:]), reads=[brsum], writes=[bhilo])
                s.op("dve", lambda e: e.tensor_tensor(out=hilo[:, 1, :], in0=rsum[:], in1=hilo[:, 0, :], op=ALU.subtract),
                     reads=[brsum, bhilo], writes=[bhilo])

                def fin_pe(pss=pss, bpss=bpss):
                    self.mm_group(pss[:], [(self.onesb[:], hilo[:, 0, :]), (self.onesb[:], hilo[:, 1, :])], reads=[bhilo, self.bconst],
                                  writes=[bpss], first=False, last=True)

                def fin_tail(pss=pss, bpss=bpss, acc=acc, bacc=bacc, qsl=qsl, qb=qb):
                    s.op("act", lambda e: e.activation(out=rinv[:], in_=pss[:], func=AF.Ln), reads=[bpss], writes=[brinv])
                    s.op("act", lambda e: e.activation(out=rinv[:], in_=rinv[:], func=AF.Exp, scale=-1.0), reads=[brinv], writes=[brinv])
                    s.op("dve", lambda e: e.tensor_tensor(out=rinv[:], in0=rinv[:], in1=self.Y[:, hh, qsl], op=ALU.mult),
                         reads=[brinv, self.bY[hh][qb]], writes=[brinv])
                    s.op("dve", lambda e: e.tensor_tensor(out=self.Y[:, hh, qsl], in0=acc[:], in1=rinv[:], op=ALU.mult),
                         reads=[bacc, brinv, self.bY[hh][qb]], writes=[self.bY[hh][qb]])
                fin["pe"], fin["tail"] = fin_pe, fin_tail
            fin["pe"]()
            fin["tail"]()

        self.gates(l)
        for st in prep_steps(0):
            st()
        for h in range(8):
            pending = prep_steps(h + 1) if h + 1 < 8 else []
            attention(h, pending if self.interleave else [])
            for st in pending:
                st()
            if h % 4 == 3:
                g = h // 4
                self.dump_y(l * 4 + 2 + g)
                if g == 1:
                    self.s.barrier()
                self.out_proj(l, 2 + g)
                if g == 0:
                    self.gates(l)
        if l == 0:
            self.dump("qlatn", qlatn, bqlat); self.dump("kvlatn", kvlatn, bkvlat); self.dump("kr", kr, [bkr])
            self.dump("qn1", qn[1], bqn[1]); self.dump("qr1", qr[1], bqr[1]); self.dump("kn1", kn[1], [bkn[1]])
            self.dump("V1", Vaug[1], [bV[1]]); self.dump("PT", PT, bPT)

    def build(self):
        self.setup()
        for l in range(self.n_layers):
            self.plan_layer(l)
        for l in range(self.n_layers):
            self.stage_norm(l)
            if "conv" in self.stages:
                self.stage_conv(l)
            if "ret" in self.stages:
                self.stage_ret(l)
            if "mla" in self.stages:
                self.stage_mla(l)
        self.stage_final()
        self.s.emit()
        return self.nc

    def stage_final(self):
        s = self.s
        self.scr_reset()
        evs = list(self.dbg_evs)
        if not self.final:
            for t in range(NT):
                evs.append(s.dma("sp", "o%d" % (t % 4), lambda e, sem, t=t: e.dma_start(
                    out=self.out_d[t * 128:(t + 1) * 128, :], in_=self.X[:, t, :]).then_inc(sem, 16), 1, reads=[self.bX[t]]))
            s.wait("sp", evs)
            return
        self.nview += 1
        junk = self.nc.alloc_sbuf_tensor_at("junkf_%d" % self.nview, [128, D_MODEL], BF16, offset=self.y_off + 8192)
        bjunk = None
        jw = self.bY[2][0:2]
        fg = self.nc.alloc_sbuf_tensor_at("fg_%d" % self.nview, [128, D_MODEL], F32, offset=self.y_off)
        bfg = Buf("fg")
        s.dma("sp", "cst", lambda e, sem: e.dma_start(out=fg[:], in_=self.fgb_d[:, :]).then_inc(sem, 16), 1, writes=[bfg] + self.bY[0])
        for tb in range(NTB):
            tiles = list(range(tb * 4, tb * 4 + 4))
            self.row_rstd(tiles, junk, bjunk, extra_w=jw)
            for t in tiles:
                s.op("dve", lambda e, t=t: e.scalar_tensor_tensor(out=self.X[:, t, :], in0=self.X[:, t, :], scalar=self.ss[:, t:t + 1],
                                                                  in1=fg[:], op0=ALU.mult, op1=ALU.mult),
                     reads=[self.bX[t], self.bss[tb], bfg], writes=[self.bX[t]])
                evs.append(s.dma("sp", "o%d" % (t % 4), lambda e, sem, t=t: e.dma_start(
                    out=self.out_d[t * 128:(t + 1) * 128, :], in_=self.X[:, t, :]).then_inc(sem, 16), 1, reads=[self.bX[t]]))
        s.wait("sp", evs)


def host_consts():
    f32 = np.float32
    cst = np.zeros((128, NCST), f32)
    i = np.arange(128)
    cst[:, C_ID:C_ID + 128] = np.eye(128)
    cst[:, C_P1:C_P1 + 128] = np.maximum(i[None, :] - i[:, None], 0)
    cst[:, C_P2:C_P2 + 128] = np.maximum(i[:, None] - i[None, :], 0)
    cst[:, C_I1:C_I1 + 128] = (i + 1)[None, :]
    cst[:, C_I2:C_I2 + 128] = (128 - i)[None, :]
    cst[:, C_KJ] = 127 - i
    cst[:, C_KJ + 1] = i
    pos = np.arange(SEQ, dtype=f32)

    def tables(dim, rows):
        inv = (1.0 / (10000.0 ** (np.arange(0, dim, 2, dtype=f32) / f32(dim)))).astype(f32)
        ang = (pos[:, None] * inv[None, :]).astype(f32)
        ang = np.concatenate([ang, ang], axis=-1)
        return np.stack([np.cos(ang).astype(f32).T, np.sin(ang).astype(f32).T])[:, :rows, :].copy()
    return cst, tables(128, 128), tables(64, 64)


def host_params(inp):
    p = np.zeros((128, NP), np.float32)
    for l in range(DEPTH):
        b = l * PL
        p[:, b + P_NG:b + P_NG + 8] = inp["norm_g"][l].reshape(8, 128).T
        p[:, b + P_CW:b + P_CW + 124] = inp["conv_dw_w"][l].reshape(CONV_K, 4, 128).transpose(2, 0, 1).reshape(128, 124)
        p[:, b + P_CB:b + P_CB + 4] = inp["conv_dw_b"][l].reshape(4, 128).T
        p[:, b + P_LG:b + P_LG + 4] = inp["conv_ln_g"][l].reshape(4, 128).T
        p[:, b + P_LB:b + P_LB + 4] = inp["conv_ln_b"][l].reshape(4, 128).T
        p[:, b + P_QG:b + P_QG + 3] = inp["mla_qa_g"][l].reshape(3, 128).T
        p[:, b + P_KG:b + P_KG + 2] = inp["mla_kva_g"][l].reshape(2, 128).T
        p[:, b + P_DL:b + P_DL + 8] = inp["ret_decay_logit"][l].reshape(1, 8)
    p[:, P_FG:P_FG + 8] = inp["final_g"].reshape(8, 128).T
    return p


_CACHE = {}


def run(inputs, n_cores=8, **kw):
    key = tuple(sorted(kw.items()))
    if key not in _CACHE:
        _CACHE[key] = Prog(**kw)
        _CACHE[key].build()
    prog = _CACHE[key]
    inp = {k: np.ascontiguousarray(np.asarray(v, dtype=np.float32)) for k, v in inputs.items()}
    cst, ropeR, ropeM = host_consts()
    par = host_params(inp)
    fgb = np.ascontiguousarray(np.broadcast_to(inp["final_g"][None, :], (128, D_MODEL)))
    common = {"w_in": inp["w_in"], "w_uq": inp["mla_w_uq"], "w_ukv": inp["mla_w_ukv"], "w_out": inp["w_out"],
              "params": par, "consts": cst, "fgb": fgb, "ropeR": ropeR, "ropeM": ropeM}
    in_maps = [dict(common, x=inp["x"][b]) for b in range(n_cores)]
    res = run_bass_kernel_spmd(prog.nc, in_maps, core_ids=list(range(n_cores)))
    return res


def kernel(**inputs):
    res = run(inputs)
    return np.stack([r["out"] for r in res.results], axis=0).astype(np.float32)
```

```python
import numpy as np
import concourse.bass as bass
import concourse.mybir as mybir
from concourse.bass_utils import run_bass_kernel_spmd

F32 = mybir.dt.float32
BF16 = mybir.dt.bfloat16
AF = mybir.ActivationFunctionType
ALU = mybir.AluOpType

D_MODEL = 1024
SEQ = 2048
NT = 16
NTB = 4
DEPTH = 2
CONV_K = 31
OFF_RET = 1024
OFF_QLAT = 2560
OFF_KVLAT = 2944
OFF_KROPE = 3200
OFF_GATE = 3264
N_IN = 5312
EPS = 1e-6
RET_SCALE = 128 ** -0.5
MLA_SCALE = 192 ** -0.5
NPT = 5

P_NG, P_CW, P_CB, P_LG, P_LB, P_QG, P_KG, P_DL = 0, 8, 132, 136, 140, 144, 147, 149
PL = 160
P_FG = 2 * PL
NP = P_FG + 8
C_ID, C_P1, C_P2, C_I1, C_I2, C_KJ = 0, 128, 256, 384, 512, 640
NCST = 642


class Buf:
    __slots__ = ("name", "w", "r")

    def __init__(self, name):
        self.name = name
        self.w = None
        self.r = []


class Op:
    __slots__ = ("eng", "fn", "deps", "marked", "lane", "real")

    def __init__(self, eng, fn, deps):
        self.eng = eng
        self.fn = fn
        self.deps = deps
        self.marked = False
        self.lane = None
        self.real = fn is not None


class Sched:
    ENGS = ("pe", "act", "dve", "pool", "sp")

    def __init__(self, nc):
        self.nc = nc
        self.ops = {e: [] for e in self.ENGS}
        self.lanes = {}
        self.sems = {e: nc.alloc_semaphore("done_" + e) for e in self.ENGS}
        self.last_real = {e: None for e in self.ENGS}

    def _collect(self, eng, reads, writes):
        deps = set()
        for b in reads:
            if b.w is not None:
                deps.add(b.w)
        for b in writes:
            if b.w is not None:
                deps.add(b.w)
            for ev in b.r:
                if ev[0] == "e" and ev[1] == eng:
                    continue
                deps.add(ev)
        if eng == "pe":
            deps = {d for d in deps if not (d[0] == "e" and d[1] == "pe")}
        return deps

    @staticmethod
    def _register(ev, reads, writes):
        for b in reads:
            b.r.append(ev)
        for b in writes:
            b.w = ev
            b.r = []

    def op(self, eng, fn, reads=(), writes=()):
        deps = self._collect(eng, reads, writes)
        o = Op(eng, fn, deps)
        self.ops[eng].append(o)
        ev = ("e", eng, len(self.ops[eng]) - 1)
        self.last_real[eng] = ev
        self._register(ev, reads, writes)
        return ev

    def dma(self, queue, lane, fn, n, reads=(), writes=()):
        if lane not in self.lanes:
            self.lanes[lane] = {"sem": self.nc.alloc_semaphore("ln_" + lane), "val": 0}
        L = self.lanes[lane]
        deps = self._collect(queue, reads, writes)
        if L["val"] > 0:
            deps.add(("l", lane, L["val"]))
        L["val"] += 16 * n
        o = Op(queue, fn, deps)
        o.lane = lane
        o.real = False
        self.ops[queue].append(o)
        ev = ("l", lane, L["val"])
        self._register(ev, reads, writes)
        return ev

    def wait(self, eng, events):
        o = Op(eng, None, set(events))
        self.ops[eng].append(o)

    def barrier(self):
        evs = [ev for ev in self.last_real.values() if ev is not None]
        for ln, L in self.lanes.items():
            if L["val"]:
                evs.append(("l", ln, L["val"]))
        for e in self.ENGS:
            self.wait(e, [ev for ev in evs if not (ev[0] == "e" and ev[1] == e)])

    def emit(self):
        nc = self.nc
        for e in self.ENGS:
            for o in self.ops[e]:
                for d in o.deps:
                    if d[0] == "e":
                        tgt = self.ops[d[1]][d[2]]
                        assert tgt.real
                        tgt.marked = True
        cnt = {}
        for e in self.ENGS:
            c = 0
            arr = []
            for o in self.ops[e]:
                if o.marked:
                    c += 1
                arr.append(c)
            cnt[e] = arr
        self.stats = {}

        def run(e):
            def body(eng):
                waited = {}
                nwait = 0
                for o in self.ops[e]:
                    need = {}
                    for d in o.deps:
                        if d[0] == "e":
                            key = ("e", d[1])
                            val = cnt[d[1]][d[2]]
                        else:
                            key = ("l", d[1])
                            val = d[2]
                        if val > need.get(key, 0):
                            need[key] = val
                    for key, val in need.items():
                        if waited.get(key, 0) < val:
                            sem = self.sems[key[1]] if key[0] == "e" else self.lanes[key[1]]["sem"]
                            eng.wait_ge(sem, val)
                            waited[key] = val
                            nwait += 1
                    if o.lane is not None:
                        o.fn(eng, self.lanes[o.lane]["sem"])
                    elif o.fn is not None:
                        ins = o.fn(eng)
                        if o.marked:
                            ins.then_inc(self.sems[e], 1)
                self.stats[e] = (len(self.ops[e]), nwait, cnt[e][-1] if cnt[e] else 0)
            return body

        with nc.Block() as block:
            block.tensor(run("pe"))
            block.scalar(run("act"))
            block.vector(run("dve"))
            block.gpsimd(run("pool"))
            block.sync(run("sp"))


def bc(ap, pos, count):
    dims = [list(d) for d in ap.ap]
    dims.insert(1 + pos, [0, count])
    return bass.AP(ap.tensor, ap.offset, dims)


class Prog:
    def __init__(self, n_layers=DEPTH, stages=("conv", "ret", "mla"), dbg=False, final=True):
        self.n_layers = n_layers
        self.stages = stages
        self.dbg = dbg
        self.final = final
        nc = self.nc = bass.Bass("TRN2", target_bir_lowering=False)
        self.s = Sched(nc)
        dr = nc.dram_tensor
        self.x_d = dr("x", [SEQ, D_MODEL], F32, kind="ExternalInput").ap()
        self.w_in_d = dr("w_in", [DEPTH, D_MODEL, N_IN], F32, kind="ExternalInput").ap()
        self.w_uq_d = dr("w_uq", [DEPTH, 384, 1536], F32, kind="ExternalInput").ap()
        self.w_ukv_d = dr("w_ukv", [DEPTH, 256, 2048], F32, kind="ExternalInput").ap()
        self.w_out_d = dr("w_out", [DEPTH, 2048, D_MODEL], F32, kind="ExternalInput").ap()
        self.par_d = dr("params", [128, NP], F32, kind="ExternalInput").ap()
        self.cst_d = dr("consts", [128, NCST], F32, kind="ExternalInput").ap()
        self.fgb_d = dr("fgb", [128, D_MODEL], F32, kind="ExternalInput").ap()
        self.ropeR_d = dr("ropeR", [2, 128, SEQ], F32, kind="ExternalInput").ap()
        self.ropeM_d = dr("ropeM", [2, 64, SEQ], F32, kind="ExternalInput").ap()
        self.out_d = dr("out", [SEQ, D_MODEL], F32, kind="ExternalOutput").ap()
        if dbg:
            self.dbg_d = dr("dbgY", [2 * DEPTH, 128, 4, SEQ], BF16, kind="ExternalOutput").ap()
        self.dbg_evs = []
        self.interleave = False
        self.nqb = 0
        self.sctr = 0

        self.off = 16384
        sb = self.sb
        self.X = sb("X", [128, NT, D_MODEL], F32)
        self.bX = [Buf("X%d" % t) for t in range(NT)]
        self.y_off = self.off
        self.Y = sb("Y", [128, 4, SEQ], BF16)
        self.bY = [[Buf("Y%d_%d" % (c, tb)) for tb in range(NTB)] for c in range(4)]
        self.HT = sb("HT", [128, NTB, 8, 512], BF16)
        self.bHT = [Buf("HT%d" % tb) for tb in range(NTB)]
        self.ring = [self.off + i * 4096 for i in range(4)]
        self.off += 4 * 4096
        self.bring = [Buf("ring%d" % i) for i in range(4)]
        self.ring_i = 0
        self.par = sb("par", [128, NP], F32)
        self.cst = sb("cst", [128, NCST], F32)
        self.identb = sb("identb", [128, 128], BF16)
        self.onesb = sb("onesb", [128, 128], BF16)
        self.onesf = sb("onesf", [128, 128], F32)
        self.ss = sb("ss", [128, NT], F32)
        self.bconst = Buf("const")
        self.bss = [Buf("ss%d" % i) for i in range(NTB)]
        self.scr0 = self.off
        self.scr_end = 16384 + 212992 - 64
        self.nview = 0
        self.ps = [nc.alloc_psum_tensor("ps%d" % i, [128, 512], F32) for i in range(7)]
        self.bps = [Buf("ps%d" % i) for i in range(7)]
        self.psb = nc.alloc_psum_tensor("psb", [128, 1024], BF16)
        self.bpsb = [Buf("psb0"), Buf("psb1")]
        self.psb_i = 0
        self.ps_rr = {}
        self.witems = []
        self.wi = 0
        self.issued = 0

    def sb(self, name, shape, dt):
        n = int(np.prod(shape[1:])) * (4 if dt == F32 else 2)
        t = self.nc.alloc_sbuf_tensor_at(name, list(shape), dt, offset=self.off)
        self.off += (n + 31) // 32 * 32
        return t

    def scr_reset(self):
        self.off = self.scr0

    def scr(self, name, shape, dt):
        self.nview += 1
        t = self.sb("%s_%d" % (name, self.nview), shape, dt)
        assert self.off <= self.scr_end, (name, self.off, self.scr_end)
        return t

    def view(self, off, name, shape, dt):
        self.nview += 1
        return self.nc.alloc_sbuf_tensor_at("%s_%d" % (name, self.nview), list(shape), dt, offset=off)

    def bank(self, group):
        i = self.ps_rr.get(group, 0)
        self.ps_rr[group] = i + 1
        b = group[i % len(group)]
        return self.ps[b], self.bps[b]

    def bbank(self):
        i = self.psb_i
        self.psb_i += 1
        h = i % 2
        return self.psb[:, h * 512:(h + 1) * 512], self.bpsb[h]

    def wplan(self, specs):
        self.witems.extend(specs)

    def _issue(self, idx):
        if idx >= len(self.witems):
            return
        slot = idx % 4
        off = self.ring[slot]
        parts = self.witems[idx](off)

        def fn(e, sem, parts=parts):
            for dst, src in parts:
                e.dma_start(out=dst, in_=src).then_inc(sem, 16)
        self.s.dma("pool", "w%d" % slot, fn, len(parts), writes=[self.bring[slot]])

    def wpop(self, k=1):
        while self.issued < min(self.wi + 4, len(self.witems)):
            self._issue(self.issued)
            self.issued += 1
        out = []
        for _ in range(k):
            slot = self.wi % 4
            self.wi += 1
            out.append((self.ring[slot], self.bring[slot]))
        return out[0] if k == 1 else out

    def wpop_keep(self):
        while self.issued < min(self.wi + 3, len(self.witems)):
            self._issue(self.issued)
            self.issued += 1
        slot = self.wi % 4
        self.wi += 1
        return self.ring[slot], self.bring[slot]

    def w_in_cols(self, l, col_ranges):
        tot = sum(n for _, n in col_ranges)

        def mk(off):
            t = self.view(off, "wv", [128, 8, tot], BF16)
            wv = self.w_in_d[l].rearrange("(kc p) n -> p kc n", p=128)
            parts = []
            o = 0
            for c0, n in col_ranges:
                parts.append((t[:, :, o:o + n], wv[:, :, c0:c0 + n]))
                o += n
            return parts
        return mk

    def w_out_item(self, l, g, half):
        def mk(off):
            t = self.view(off, "wo", [128, 4, 512], BF16)
            wv = self.w_out_d[l].rearrange("(c p) n -> p c n", p=128)
            return [(t[:], wv[:, g * 4:(g + 1) * 4, half * 512:(half + 1) * 512])]
        return mk

    def w_head_item(self, l, h):
        def mk(off):
            a = self.view(off, "wuq", [128, 3, 192], BF16)
            c = self.view(off + 1536, "wukv", [128, 2, 256], BF16)
            uq = self.w_uq_d[l].rearrange("(kc p) n -> p kc n", p=128)
            ukv = self.w_ukv_d[l].rearrange("(kc p) n -> p kc n", p=128)
            return [(a[:], uq[:, :, h * 192:(h + 1) * 192]), (c[:], ukv[:, :, h * 256:(h + 1) * 256])]
        return mk

    def plan_layer(self, l):
        items = []
        if "conv" in self.stages:
            for c in range(4):
                items.append(self.w_in_cols(l, [(c * 128, 128), (512 + c * 128, 128)]))
            for j in range(2):
                items.append(self.w_in_cols(l, [(OFF_GATE + j * 256, 256)]))
            for half in range(2):
                items.append(self.w_out_item(l, 0, half))
        if "ret" in self.stages:
            for j in range(2):
                items.append(self.w_in_cols(l, [(OFF_GATE + 512 + j * 256, 256)]))
            for h in range(4):
                items.append(self.w_in_cols(l, [(OFF_RET + 1024 + h * 128, 128)]))
                items.append(self.w_in_cols(l, [(OFF_RET + h * 128, 128), (OFF_RET + 512 + h * 128, 128)]))
            for half in range(2):
                items.append(self.w_out_item(l, 1, half))
        if "mla" in self.stages:
            items.append(self.w_in_cols(l, [(OFF_QLAT, 256)]))
            items.append(self.w_in_cols(l, [(OFF_QLAT + 256, 128)]))
            items.append(self.w_in_cols(l, [(OFF_KVLAT, 256)]))
            items.append(self.w_in_cols(l, [(OFF_KROPE - 64, 128)]))
            def gate_items(g):
                return [self.w_in_cols(l, [(OFF_GATE + 1024 + g * 512 + j * 256, 256)]) for j in range(2)]
            items += gate_items(0)
            items += [self.w_head_item(l, h) for h in range(5)]
            items += [self.w_out_item(l, 2, half) for half in range(2)]
            items += gate_items(1)
            items += [self.w_head_item(l, h) for h in range(5, 8)]
            items += [self.w_out_item(l, 3, half) for half in range(2)]
        self.wplan(items)

    def pcol(self, l, base, i, n=1):
        c = l * PL + base + i
        return self.par[:, c:c + n]

    def mm_group(self, out, pairs, reads, writes, first=True, last=True):
        def fn(e):
            ins = None
            n = len(pairs)
            for i, (lt, rh) in enumerate(pairs):
                ins = e.matmul(out, lhsT=lt, rhs=rh, start=(first and i == 0), stop=(last and i == n - 1))
            return ins
        return self.s.op("pe", fn, reads=reads, writes=writes)

    def setup(self):
        s = self.s
        for t in range(NT):
            s.dma("sp", "x%d" % (t % 4),
                  lambda e, sem, t=t: e.dma_start(out=self.X[:, t, :], in_=self.x_d[t * 128:(t + 1) * 128, :]).then_inc(sem, 16),
                  1, writes=[self.bX[t]])
            if t == 0:
                s.dma("sp", "cst", lambda e, sem: (e.dma_start(out=self.par[:], in_=self.par_d[:, :]).then_inc(sem, 16),
                                                   e.dma_start(out=self.cst[:], in_=self.cst_d[:, :]).then_inc(sem, 16)),
                      2, writes=[self.bconst])
        s.op("dve", lambda e: e.tensor_copy(out=self.identb[:], in_=self.cst[:, C_ID:C_ID + 128]),
             reads=[self.bconst], writes=[self.bconst])
        s.op("dve", lambda e: e.memset(self.onesb[:], 1.0), writes=[self.bconst])
        s.op("dve", lambda e: e.memset(self.onesf[:], 1.0), writes=[self.bconst])

    def row_rstd(self, tiles, junk, bjunk, extra_w=()):
        s = self.s
        bss = self.bss[tiles[0] // 4]
        for t in tiles:
            s.op("act", lambda e, t=t: e.activation(out=junk[:], in_=self.X[:, t, :], func=AF.Square,
                                                    accum_out=self.ss[:, t:t + 1]),
                 reads=[self.bX[t]], writes=[bss] + list(extra_w))
        t0, t1 = tiles[0], tiles[-1] + 1
        s.op("dve", lambda e: e.tensor_scalar(out=self.ss[:, t0:t1], in0=self.ss[:, t0:t1], scalar1=1.0 / D_MODEL,
                                              scalar2=EPS, op0=ALU.mult, op1=ALU.add),
             reads=[bss], writes=[bss])
        s.op("act", lambda e: e.activation(out=self.ss[:, t0:t1], in_=self.ss[:, t0:t1], func=AF.Ln),
             reads=[bss], writes=[bss])
        s.op("act", lambda e: e.activation(out=self.ss[:, t0:t1], in_=self.ss[:, t0:t1], func=AF.Exp, scale=-0.5),
             reads=[bss], writes=[bss])

    def stage_norm(self, l):
        s = self.s
        self.nview += 1
        yoff = self.y_off
        xs = [self.nc.alloc_sbuf_tensor_at("xs%d_%d" % (i, self.nview), [128, D_MODEL], F32, offset=yoff + i * 4096) for i in range(2)]
        bxs = [self.bY[0], self.bY[1]]
        junk = self.nc.alloc_sbuf_tensor_at("junk_%d" % self.nview, [128, D_MODEL], BF16, offset=yoff + 8192)
        bjunk = self.bY[2][0]
        idf = self.cst[:, C_ID:C_ID + 128]

        def scale(t):
            k, tb = t % 2, t // 4
            s.op("dve", lambda e: e.tensor_scalar_mul(out=xs[k][:], in0=self.X[:, t, :], scalar1=self.ss[:, t:t + 1]),
                 reads=[self.bX[t], self.bss[tb]], writes=bxs[k])

        def transp(t):
            k, tb, tt = t % 2, t // 4, t % 4
            banks = []
            for hb in range(2):
                pt, bpt = self.bank((3, 4, 5, 6))

                def tr(e, hb=hb, pt=pt):
                    ins = None
                    for c in range(4):
                        cc = hb * 4 + c
                        ins = e.transpose(out=pt[:, c * 128:(c + 1) * 128], in_=xs[k][:, cc * 128:(cc + 1) * 128], identity=idf)
                    return ins
                s.op("pe", tr, reads=bxs[k] + [self.bconst], writes=[bpt])
                banks.append((pt, bpt))
            return banks

        def evac(t, banks):
            tb, tt = t // 4, t % 4
            for hb, (pt, bpt) in enumerate(banks):
                gcol = self.pcol(l, P_NG, hb * 4, 4)
                s.op("dve", lambda e, pt=pt, hb=hb, gcol=gcol: e.tensor_tensor(
                    out=self.HT[:, tb, hb * 4:(hb + 1) * 4, tt * 128:(tt + 1) * 128],
                    in0=pt[:].rearrange("p (c t) -> p c t", c=4), in1=bc(gcol, 1, 128), op=ALU.mult),
                    reads=[bpt, self.bconst], writes=[self.bHT[tb]])

        for tb in range(NTB):
            self.row_rstd(list(range(tb * 4, tb * 4 + 4)), junk, bjunk, extra_w=self.bY[2][0:2])
        scale(0)
        for t in range(NT):
            banks = transp(t)
            if t + 1 < NT:
                scale(t + 1)
            evac(t, banks)

    def inproj_fm(self, wt, bw, col, M, tb, banks=(0, 1, 2, 3)):
        pt, bpt = self.bank(banks)
        pairs = [(wt[:, kc, col:col + M], self.HT[:, tb, kc, :]) for kc in range(8)]
        self.mm_group(pt[:M, :], pairs, reads=[bw, self.bHT[tb]], writes=[bpt])
        return pt, bpt

    def gates(self, l, nitems=2):
        s = self.s
        for j in range(nitems):
            off, bw = self.wpop()
            wt = self.view(off, "wg", [128, 8, 256], BF16)
            for cc in range(2):
                c = j * 2 + cc
                for tb in range(NTB):
                    pt, bpt = self.inproj_fm(wt, bw, cc * 128, 128, tb)
                    s.op("act", lambda e, pt=pt, c=c, tb=tb: e.activation(out=self.Y[:, c, tb * 512:(tb + 1) * 512], in_=pt[:], func=AF.Silu),
                         reads=[bpt], writes=[self.bY[c][tb]])

    def out_proj(self, l, g):
        s = self.s
        for half in range(2):
            off, bw = self.wpop()
            wo = self.view(off, "wo", [128, 4, 512], BF16)
            for t in range(NT):
                pt, bpt = self.bank((0, 1, 2, 3))
                tb = t // 4
                pairs = [(self.Y[:, c, t * 128:(t + 1) * 128], wo[:, c, :]) for c in range(4)]
                self.mm_group(pt[:], pairs, reads=[bw] + [self.bY[c][tb] for c in range(4)], writes=[bpt])
                s.op("dve", lambda e, pt=pt, t=t, half=half: e.tensor_tensor(
                    out=self.X[:, t, half * 512:(half + 1) * 512], in0=pt[:], in1=self.X[:, t, half * 512:(half + 1) * 512], op=ALU.add),
                    reads=[bpt, self.bX[t]], writes=[self.bX[t]])

    def dump(self, name, t, reads):
        if not self.dbg:
            return
        d = self.nc.dram_tensor("dbg_" + name, list(t.shape), t.dtype, kind="ExternalOutput").ap()
        ev = self.s.dma("sp", "dbg", lambda e, sem: e.dma_start(out=d, in_=t[:]).then_inc(sem, 16), 1, reads=reads)
        self.dbg_evs.append(ev)

    def dump_y(self, idx):
        if not self.dbg:
            return
        ev = self.s.dma("sp", "dbg", lambda e, sem: e.dma_start(out=self.dbg_d[idx], in_=self.Y[:]).then_inc(sem, 16), 1,
                        reads=[b for r in self.bY for b in r])
        self.dbg_evs.append(ev)

    def stage_conv(self, l):
        s = self.s
        self.scr_reset()
        GW = SEQ + 30
        glu_off = self.off
        glu = self.scr("glu", [128, 4, GW], BF16)
        bglu = [Buf("glu%d" % c) for c in range(4)]
        diag0 = self.scr("diag", [128, CONV_K, 128], BF16)
        cvo = self.scr("cvo", [128, 4, SEQ], F32)
        bcvo = [[Buf("cvo%d_%d" % (c, tb)) for tb in range(NTB)] for c in range(4)]
        sig = [self.scr("sig%d" % i, [128, 512], F32) for i in range(2)]
        bsig = [Buf("sig0"), Buf("sig1")]
        reg_off = self.off
        cvb = self.scr("cvb", [128, 4, 512], BF16)
        sq = self.scr("sq", [128, 4, 512], BF16)
        diag1 = self.view(reg_off, "diag1", [128, CONV_K, 128], BF16)
        diags = [diag0, diag1]
        bdiags = [Buf("diag0"), Buf("diag1")]
        bcvb = bsq = bdiags[1]
        mean = self.scr("mean", [128, 512], F32)
        msq = self.scr("msq", [128, 512], F32)
        rstd = self.scr("rstd", [128, 512], F32)
        bmean, bmsq, brstd = Buf("mean"), Buf("msq"), Buf("rstd")
        tmp, btmp = sig, bsig

        for c in range(4):
            s.op("pool", lambda e, c=c: e.memset(glu[:, c, 0:15], 0.0), writes=[bglu[c]])
            s.op("pool", lambda e, c=c: e.memset(glu[:, c, GW - 15:GW], 0.0), writes=[bglu[c]])
            off, bw = self.wpop()
            wt = self.view(off, "wc", [128, 8, 256], BF16)
            for tb in range(NTB):
                pa, bpa = self.inproj_fm(wt, bw, 0, 128, tb)
                pb, bpb = self.inproj_fm(wt, bw, 128, 128, tb)
                k = tb % 2
                s.op("act", lambda e, pb=pb, k=k: e.activation(out=sig[k][:], in_=pb[:], func=AF.Sigmoid),
                     reads=[bpb], writes=[bsig[k]])
                s.op("dve", lambda e, pa=pa, k=k, c=c, tb=tb: e.tensor_tensor(
                    out=glu[:, c, 15 + tb * 512:15 + (tb + 1) * 512], in0=pa[:], in1=sig[k][:], op=ALU.mult),
                    reads=[bpa, bsig[k]], writes=[bglu[c]])
        self.gates(l)
        for c in range(4):
            diag, bdiag = diags[c % 2], bdiags[c % 2]
            wc0 = l * PL + P_CW + c
            wb = self.par[:, wc0:wc0 + 1]
            wcols = bass.AP(wb.tensor, wb.offset, [list(wb.ap[0]), [4, CONV_K], [0, 128]])
            s.op("dve", lambda e, diag=diag, wcols=wcols: e.tensor_tensor(out=diag[:], in0=bc(self.identb[:], 0, CONV_K), in1=wcols, op=ALU.mult),
                 reads=[self.bconst], writes=[bdiag])
            for tb in range(NTB):
                pt, bpt = self.bank((0, 1, 2, 3))
                pairs = [(diag[:, k, :], glu[:, c, tb * 512 + k:tb * 512 + k + 512]) for k in range(CONV_K)]
                self.mm_group(pt[:], pairs, reads=[bdiag, bglu[c]], writes=[bpt])
                s.op("act", lambda e, pt=pt, c=c, tb=tb: e.activation(
                    out=cvo[:, c, tb * 512:(tb + 1) * 512], in_=pt[:], func=AF.Identity, bias=self.pcol(l, P_CB, c)),
                    reads=[bpt, self.bconst], writes=[bcvo[c][tb]])
        self.nview += 1
        nv = self.nview
        at = self.nc.alloc_sbuf_tensor_at
        ln_sets = [
            dict(cvb=cvb, sq=sq, mean=mean, msq=msq, rstd=rstd,
                 bcvb=Buf("cvb0"), bsq=Buf("sq0"), bmean=bmean, bmsq=bmsq, brstd=brstd, first=[bdiags[1]]),
            dict(cvb=at("cvb1_%d" % nv, [128, 4, 512], BF16, offset=glu_off), sq=at("sq1_%d" % nv, [128, 4, 512], BF16, offset=glu_off + 4096),
                 mean=at("mean1_%d" % nv, [128, 512], F32, offset=glu_off + 8320), msq=at("msq1_%d" % nv, [128, 512], F32, offset=glu_off + 10368),
                 rstd=at("rstd1_%d" % nv, [128, 512], F32, offset=glu_off + 12416),
                 bcvb=Buf("cvb1"), bsq=Buf("sq1"), bmean=Buf("mean1"), bmsq=Buf("msq1"), brstd=Buf("rstd1"), first=list(bglu)),
        ]
        def ln_stats(tb):
            sl = slice(tb * 512, (tb + 1) * 512)
            L = ln_sets[tb % 2]
            xw = L["first"] if tb < 2 else []
            cvb_, sq_, mean_, msq_, rstd_ = L["cvb"], L["sq"], L["mean"], L["msq"], L["rstd"]
            bcvb_, bsq_, bmean_, bmsq_, brstd_ = L["bcvb"], L["bsq"], L["bmean"], L["bmsq"], L["brstd"]
            s.op("act", lambda e: e.activation(out=cvb_[:], in_=cvo[:, :, sl], func=AF.Copy),
                 reads=[bcvo[c][tb] for c in range(4)], writes=[bcvb_] + xw)
            s.op("act", lambda e: e.activation(out=sq_[:], in_=cvo[:, :, sl], func=AF.Square),
                 reads=[bcvo[c][tb] for c in range(4)], writes=[bsq_] + xw)
            pm, bpm = self.bank((4, 5, 6))
            self.mm_group(pm[:], [(self.onesb[:], cvb_[:, c, :]) for c in range(4)], reads=[bcvb_, self.bconst], writes=[bpm])
            pq, bpq = self.bank((4, 5, 6))
            self.mm_group(pq[:], [(self.onesb[:], sq_[:, c, :]) for c in range(4)], reads=[bsq_, self.bconst], writes=[bpq])
            s.op("dve", lambda e: e.tensor_scalar_mul(out=mean_[:], in0=pm[:], scalar1=1.0 / 512), reads=[bpm], writes=[bmean_] + xw)
            s.op("dve", lambda e: e.tensor_tensor(out=msq_[:], in0=mean_[:], in1=mean_[:], op=ALU.mult), reads=[bmean_], writes=[bmsq_] + xw)
            s.op("dve", lambda e: e.scalar_tensor_tensor(out=rstd_[:], in0=pq[:], scalar=1.0 / 512, in1=msq_[:], op0=ALU.mult, op1=ALU.subtract),
                 reads=[bpq, bmsq_], writes=[brstd_] + xw)
            s.op("dve", lambda e: e.tensor_scalar_add(out=rstd_[:], in0=rstd_[:], scalar1=EPS), reads=[brstd_], writes=[brstd_])
            s.op("act", lambda e: e.activation(out=rstd_[:], in_=rstd_[:], func=AF.Ln), reads=[brstd_], writes=[brstd_])
            s.op("act", lambda e: e.activation(out=rstd_[:], in_=rstd_[:], func=AF.Exp, scale=-0.5), reads=[brstd_], writes=[brstd_])

        def ln_norm(tb):
            sl = slice(tb * 512, (tb + 1) * 512)
            L = ln_sets[tb % 2]
            mean_, rstd_, bmean_, brstd_ = L["mean"], L["rstd"], L["bmean"], L["brstd"]
            def gate(c):
                k = c % 2
                s.op("dve", lambda e: e.tensor_tensor(out=self.Y[:, c, sl], in0=tmp[k][:], in1=self.Y[:, c, sl], op=ALU.mult),
                     reads=[btmp[k], self.bY[c][tb]], writes=[self.bY[c][tb]])
            for c in range(4):
                k = c % 2
                s.op("dve", lambda e, c=c, k=k: e.tensor_tensor(out=tmp[k][:], in0=cvo[:, c, sl], in1=mean_[:], op=ALU.subtract),
                     reads=[bcvo[c][tb], bmean_], writes=[btmp[k]])
                s.op("dve", lambda e, k=k: e.tensor_tensor(out=tmp[k][:], in0=tmp[k][:], in1=rstd_[:], op=ALU.mult),
                     reads=[btmp[k], brstd_], writes=[btmp[k]])
                s.op("act", lambda e, c=c, k=k: e.activation(out=tmp[k][:], in_=tmp[k][:], func=AF.Silu,
                                                             scale=self.pcol(l, P_LG, c), bias=self.pcol(l, P_LB, c)),
                     reads=[btmp[k], self.bconst], writes=[btmp[k]])
                if c > 0:
                    gate(c - 1)
            gate(3)

        ln_stats(0)
        for tb in range(NTB):
            if tb + 1 < NTB:
                ln_stats(tb + 1)
            ln_norm(tb)
        self.dump_y(l * 4 + 0)
        self.s.barrier()
        self.out_proj(l, 0)

    def stage_ret(self, l):
        s = self.s
        self.scr_reset()
        scr = self.scr
        cstv = self.cst
        IB = (0, 1, 2, 3)
        RB = (4, 5, 6)
        tbl = [[scr("tbl", [128, 512], F32) for _ in range(2)] for _ in range(2)]
        btbl = [Buf("tbl0"), Buf("tbl1")]
        qT = [scr("qT", [128, SEQ], BF16) for _ in range(2)]
        kT = [scr("kT", [128, SEQ], BF16) for _ in range(2)]
        ktok = [scr("ktok", [128, NT, 128], BF16) for _ in range(2)]
        vtok = [scr("vtok", [128, NT, 128], BF16) for _ in range(2)]
        bqT = [[Buf("qT%d_%d" % (k, i)) for i in range(4)] for k in range(2)]
        bkT = [[Buf("kT%d_%d" % (k, i)) for i in range(4)] for k in range(2)]
        bktok = [[Buf("ktok%d_%d" % (k, i)) for i in range(4)] for k in range(2)]
        bvtok = [[Buf("vtok%d_%d" % (k, i)) for i in range(4)] for k in range(2)]
        kdec = scr("kdec", [128, NT, 128], BF16)
        bkdec = Buf("kdec")
        qf = scr("qf", [128, SEQ], BF16)
        qb = scr("qb", [128, SEQ], BF16)
        bqf = [Buf("qf%d" % i) for i in range(4)]
        bqb = [Buf("qb%d" % i) for i in range(4)]
        srun = [scr("srun", [128, 128], F32) for _ in range(2)]
        bsrun = [Buf("srun0"), Buf("srun1")]
        Sf = scr("Sf", [128, NT, 128], BF16)
        Sb = scr("Sb", [128, NT, 128], BF16)
        bSf, bSb = Buf("Sf"), Buf("Sb")
        ATb = [scr("ATb", [128, 4, 128], BF16) for _ in range(2)]
        onb = [scr("onb", [128, 4, 128], BF16) for _ in range(2)]
        st = [scr("st", [128, 4, 6], F32) for _ in range(2)]
        mv = [scr("mv", [128, 4, 2], F32) for _ in range(2)]
        rs = [scr("rs", [128, 4], F32) for _ in range(2)]
        bATb = [Buf("ATb0"), Buf("ATb1")]
        bonb = [Buf("onb0"), Buf("onb1")]
        bst = [Buf("st0"), Buf("st1")]
        lg = scr("lg", [128, 8], F32)
        gC = scr("gC", [128, 8], F32)
        KD = scr("KD", [128, 8], F32)
        DT = [scr("DT", [128, 128], F32) for _ in range(2)]
        AFm = scr("AFm", [128, 128], F32)
        ABm = scr("ABm", [128, 128], F32)
        dtmp = scr("dtmp", [128, 128], F32)
        bdec = Buf("dec")
        bhd = [Buf("hd0"), Buf("hd1")]
        t1 = scr("t1", [128, 512], F32)
        t2 = scr("t2", [128, 512], F32)
        bt1, bt2 = Buf("t1"), Buf("t2")
        wBk = scr("wBk", [128, 8, 128], BF16)
        bwBk = Buf("wBk")

        dl = self.par[:, l * PL + P_DL:l * PL + P_DL + 8]
        rd, wr = [self.bconst, bdec], [bdec]
        s.op("act", lambda e: e.activation(out=lg[:], in_=dl, func=AF.Exp, scale=-1.0), reads=rd, writes=wr)
        s.op("dve", lambda e: e.tensor_scalar_add(out=lg[:], in0=lg[:], scalar1=1.0), reads=rd, writes=wr)
        s.op("act", lambda e: e.activation(out=lg[:], in_=lg[:], func=AF.Ln), reads=rd, writes=wr)
        s.op("dve", lambda e: e.tensor_scalar_mul(out=lg[:], in0=lg[:], scalar1=-1.0), reads=rd, writes=wr)
        s.op("act", lambda e: e.activation(out=gC[:], in_=lg[:], func=AF.Exp, scale=128.0), reads=rd, writes=wr)
        for h in range(4):
            lf, lb = lg[:, h:h + 1], lg[:, 4 + h:5 + h]
            s.op("act", lambda e, h=h, lf=lf: e.activation(out=KD[:, h:h + 1], in_=cstv[:, C_KJ:C_KJ + 1], func=AF.Exp, scale=lf), reads=rd, writes=wr)
            s.op("act", lambda e, h=h, lb=lb: e.activation(out=KD[:, 4 + h:5 + h], in_=cstv[:, C_KJ + 1:C_KJ + 2], func=AF.Exp, scale=lb), reads=rd, writes=wr)

        self.gates(l)
        ntab = [0]

        def inproj_steps(h):
            k = h % 2
            st_ = {}
            steps = []

            def svpop():
                off, bw = self.wpop()
                st_["bwv"] = bw
                st_["wv"] = self.view(off, "wv", [128, 8, 128], BF16)
                st_["wBq"] = self.view(off + 2048, "wBq", [128, 8, 128], BF16)

            def svmm():
                bw, wv = st_["bwv"], st_["wv"]
                for tg in range(4):
                    pt, bpt = self.bank(IB)

                    def fv(e, tg=tg, pt=pt):
                        ins = None
                        for i in range(4):
                            for kc in range(8):
                                ins = e.matmul(pt[:, i * 128:(i + 1) * 128], lhsT=self.HT[:, tg, kc, i * 128:(i + 1) * 128], rhs=wv[:, kc, :],
                                               start=(kc == 0), stop=(kc == 7))
                        return ins
                    s.op("pe", fv, reads=[bw, self.bHT[tg]], writes=[bpt])
                    s.op("act", lambda e, tg=tg, pt=pt: e.activation(out=vtok[k][:, tg * 4:(tg + 1) * 4, :],
                                                                    in_=pt[:].rearrange("p (a b) -> p a b", a=4), func=AF.Copy),
                         reads=[bpt], writes=[bvtok[k][tg]])
            steps.append(svpop)

            def sw():
                off, bw = self.wpop_keep()
                st_["bw"] = bw
                st_["wA"] = wA = self.view(off, "wqk", [128, 8, 256], BF16)
                wBq, bwv = st_["wBq"], st_["bwv"]
                s.op("act", lambda e: e.activation(out=wBq[:, :, 0:64], in_=wA[:, :, 64:128], func=AF.Copy, scale=-1.0), reads=[bw], writes=[bwv])
                s.op("act", lambda e: e.activation(out=wBq[:, :, 64:128], in_=wA[:, :, 0:64], func=AF.Copy), reads=[bw], writes=[bwv])
                s.op("act", lambda e: e.activation(out=wBk[:, :, 0:64], in_=wA[:, :, 192:256], func=AF.Copy, scale=-1.0), reads=[bw], writes=[bwBk])
                s.op("act", lambda e: e.activation(out=wBk[:, :, 64:128], in_=wA[:, :, 128:192], func=AF.Copy), reads=[bw], writes=[bwBk])
            steps.append(sw)
            for tb in range(NTB):
                def stb(tb=tb):
                    bw, wA, wBq, bwv = st_["bw"], st_["wA"], st_["wBq"], st_["bwv"]
                    sl = slice(tb * 512, (tb + 1) * 512)
                    kk = ntab[0] % 2
                    ntab[0] += 1
                    s.dma("sp", "tb%d" % kk, lambda e, sem: (
                        e.dma_start(out=tbl[kk][0][:], in_=self.ropeR_d[0][:, sl]).then_inc(sem, 16),
                        e.dma_start(out=tbl[kk][1][:], in_=self.ropeR_d[1][:, sl]).then_inc(sem, 16)), 2, writes=[btbl[kk]])
                    cos, sin = tbl[kk][0], tbl[kk][1]
                    p1, bp1 = self.inproj_fm(wA, bw, 0, 128, tb, banks=IB)
                    p2, bp2 = self.inproj_fm(wBq, bwv, 0, 128, tb, banks=IB)
                    s.op("dve", lambda e: e.tensor_tensor(out=t1[:], in0=p1[:], in1=cos[:], op=ALU.mult), reads=[bp1, btbl[kk]], writes=[bt1])
                    s.op("dve", lambda e: e.tensor_tensor(out=t2[:], in0=p2[:], in1=sin[:], op=ALU.mult), reads=[bp2, btbl[kk]], writes=[bt2])
                    s.op("pool", lambda e: e.tensor_tensor(out=qT[k][:, sl], in0=t1[:], in1=t2[:], op=ALU.add), reads=[bt1, bt2], writes=[bqT[k][tb]])
                    p3, bp3 = self.inproj_fm(wA, bw, 128, 128, tb, banks=IB)
                    p4, bp4 = self.inproj_fm(wBk, bwBk, 0, 128, tb, banks=IB)
                    s.op("dve", lambda e: e.tensor_tensor(out=t1[:], in0=p3[:], in1=cos[:], op=ALU.mult), reads=[bp3, btbl[kk]], writes=[bt1])
                    s.op("dve", lambda e: e.scalar_tensor_tensor(out=t2[:], in0=p4[:], scalar=RET_SCALE, in1=sin[:], op0=ALU.mult, op1=ALU.mult),
                         reads=[bp4, btbl[kk]], writes=[bt2])
                    s.op("dve", lambda e: e.scalar_tensor_tensor(out=kT[k][:, sl], in0=t1[:], scalar=RET_SCALE, in1=t2[:], op0=ALU.mult, op1=ALU.add),
                         reads=[bt1, bt2], writes=[bkT[k][tb]])
                    pb, bpb = self.bbank()

                    def ftr(e):
                        ins = None
                        for i in range(4):
                            n = tb * 4 + i
                            ins = e.transpose(out=pb[:, i * 128:(i + 1) * 128], in_=kT[k][:, n * 128:(n + 1) * 128], identity=self.identb[:])
                        return ins
                    s.op("pe", ftr, reads=[bkT[k][tb], self.bconst], writes=[bpb])
                    s.op("act", lambda e: e.activation(out=ktok[k][:, tb * 4:(tb + 1) * 4, :], in_=pb.rearrange("p (a b) -> p a b", a=4), func=AF.Copy),
                         reads=[bpb], writes=[bktok[k][tb]])
                steps.append(stb)
            steps.append(svmm)
            return steps

        def rest_steps(h):
            k = h % 2
            steps = []
            lf, lb = lg[:, h:h + 1], lg[:, 4 + h:5 + h]

            def sconst():
                rdh, wrh = [self.bconst, bdec, bhd[k]], [bhd[k]]
                s.op("dve", lambda e: e.tensor_scalar_mul(out=dtmp[:], in0=cstv[:, C_P1:C_P1 + 128], scalar1=lf), reads=rdh + [bhd[1 - k]], writes=wrh + [bhd[1 - k]])
                s.op("dve", lambda e: e.scalar_tensor_tensor(out=dtmp[:], in0=cstv[:, C_P2:C_P2 + 128], scalar=lb, in1=dtmp[:], op0=ALU.mult, op1=ALU.add),
                     reads=rdh + [bhd[1 - k]], writes=wrh + [bhd[1 - k]])
                s.op("act", lambda e: e.activation(out=DT[k][:], in_=dtmp[:], func=AF.Exp), reads=rdh + [bhd[1 - k]], writes=wrh)

            def make_chain(direction):
                tg_order = [0, 1, 2, 3] if direction == 0 else [3, 2, 1, 0]
                kvb = {}

                def emit_kv(tg):
                    pt, bpt = self.bank(RB)

                    def fkv(e, tg=tg, pt=pt):
                        ins = None
                        for i in range(4):
                            n = tg * 4 + i
                            ins = e.matmul(pt[:, i * 128:(i + 1) * 128], lhsT=kdec[:, n, :], rhs=vtok[k][:, n, :], start=True, stop=True)
                        return ins
                    s.op("pe", fkv, reads=[bkdec, bvtok[k][tg]], writes=[bpt])
                    kvb[tg] = (pt, bpt)

                def skv_k():
                    kdcol = KD[:, direction * 4 + h:direction * 4 + h + 1]
                    s.op("act", lambda e: e.activation(out=kdec[:], in_=ktok[k][:], func=AF.Copy, scale=kdcol), reads=bktok[k] + [bdec], writes=[bkdec])

                def skv_m():
                    for tg in tg_order[:3]:
                        emit_kv(tg)

                def schain():
                    gcol = gC[:, direction * 4 + h:direction * 4 + h + 1]
                    Sst, bSst = (Sf, bSf) if direction == 0 else (Sb, bSb)
                    order = list(range(0, 15)) if direction == 0 else list(range(15, 0, -1))
                    for step, n in enumerate(order):
                        if n // 4 not in kvb:
                            emit_kv(n // 4)
                        pt, bpt = kvb[n // 4]
                        kvn = pt[:, (n % 4) * 128:(n % 4 + 1) * 128]
                        dst = n + 1 if direction == 0 else n - 1
                        cur, nxt = step % 2, (step + 1) % 2
                        if step == 0:
                            s.op("dve", lambda e, kvn=kvn, nxt=nxt: e.tensor_copy(out=srun[nxt][:], in_=kvn), reads=[bpt], writes=[bsrun[nxt]])
                        else:
                            s.op("dve", lambda e, kvn=kvn, cur=cur, nxt=nxt: e.scalar_tensor_tensor(
                                out=srun[nxt][:], in0=srun[cur][:], scalar=gcol, in1=kvn, op0=ALU.mult, op1=ALU.add),
                                reads=[bpt, bsrun[cur], bdec], writes=[bsrun[nxt]])
                        s.op("act", lambda e, nxt=nxt, dst=dst: e.activation(out=Sst[:, dst, :], in_=srun[nxt][:], func=AF.Copy),
                             reads=[bsrun[nxt]], writes=[bSst])
                return skv_k, skv_m, schain
            make_chain.kvb = {}
            skvk_f, skvm_f, schain_f = make_chain(0)
            skvk_b, skvm_b, schain_b = make_chain(1)

            def sqfb():
                rdh = [self.bconst, bdec, bhd[0], bhd[1]]
                s.op("act", lambda e: e.activation(out=AFm[:], in_=cstv[:, C_I1:C_I1 + 128], func=AF.Exp, scale=lf), reads=rdh, writes=[bhd[0], bhd[1]])
                s.op("act", lambda e: e.activation(out=ABm[:], in_=cstv[:, C_I2:C_I2 + 128], func=AF.Exp, scale=lb), reads=rdh, writes=[bhd[0], bhd[1]])
                for tb in range(NTB):
                    sl = slice(tb * 512, (tb + 1) * 512)
                    s.op("dve", lambda e, sl=sl: e.tensor_tensor(out=qf[:, sl].rearrange("p (a b) -> p a b", a=4),
                                                                 in0=qT[k][:, sl].rearrange("p (a b) -> p a b", a=4),
                                                                 in1=bc(AFm[:], 0, 4), op=ALU.mult),
                         reads=[bqT[k][tb], bhd[0], bhd[1]], writes=[bqf[tb]])
                    s.op("dve", lambda e, sl=sl: e.tensor_tensor(out=qb[:, sl].rearrange("p (a b) -> p a b", a=4),
                                                                 in0=qT[k][:, sl].rearrange("p (a b) -> p a b", a=4),
                                                                 in1=bc(ABm[:], 0, 4), op=ALU.mult),
                         reads=[bqT[k][tb], bhd[0], bhd[1]], writes=[bqb[tb]])

            def make_out(tg):
                kk = tg % 2
                box = {}

                def souts():
                    pa, bpa = self.bank(RB)

                    def fa(e):
                        ins = None
                        for i in range(4):
                            n = tg * 4 + i
                            ins = e.matmul(pa[:, i * 128:(i + 1) * 128], lhsT=kT[k][:, n * 128:(n + 1) * 128], rhs=qT[k][:, n * 128:(n + 1) * 128],
                                           start=True, stop=True)
                        return ins
                    s.op("pe", fa, reads=[bkT[k][tg], bqT[k][tg]], writes=[bpa])
                    s.op("dve", lambda e: e.tensor_tensor(out=ATb[kk][:], in0=pa[:].rearrange("p (a b) -> p a b", a=4),
                                                          in1=bc(DT[k][:], 0, 4), op=ALU.mult),
                         reads=[bpa, bhd[k]], writes=[bATb[kk]])

                def souta():
                    po, bpo = self.bank(RB)
                    box["po"] = (po, bpo)

                    def fo(e):
                        ins = None
                        for i in range(4):
                            n = tg * 4 + i
                            terms = [(ATb[kk][:, i, :], vtok[k][:, n, :])]
                            if n > 0:
                                terms.append((qf[:, n * 128:(n + 1) * 128], Sf[:, n, :]))
                            if n < 15:
                                terms.append((qb[:, n * 128:(n + 1) * 128], Sb[:, n, :]))
                            for j, (lt, rh) in enumerate(terms):
                                ins = e.matmul(po[:, i * 128:(i + 1) * 128], lhsT=lt, rhs=rh, start=(j == 0), stop=(j == len(terms) - 1))
                        return ins
                    s.op("pe", fo, reads=[bATb[kk], bvtok[k][tg], bqf[tg], bqb[tg], bSf, bSb], writes=[bpo])
                    for i in range(4):
                        s.op("dve", lambda e, i=i: e.bn_stats(out=st[kk][:, i, :], in_=po[:, i * 128:(i + 1) * 128]), reads=[bpo], writes=[bst[kk]])
                    for i in range(4):
                        s.op("dve", lambda e, i=i: e.bn_aggr(out=mv[kk][:, i, :], in_=st[kk][:, i, :]), reads=[bst[kk]], writes=[bst[kk]])
                    s.op("dve", lambda e: e.tensor_scalar_add(out=rs[kk][:], in0=mv[kk][:, :, 1], scalar1=EPS), reads=[bst[kk]], writes=[bst[kk]])
                    s.op("act", lambda e: e.activation(out=rs[kk][:], in_=rs[kk][:], func=AF.Ln), reads=[bst[kk]], writes=[bst[kk]])
                    s.op("act", lambda e: e.activation(out=rs[kk][:], in_=rs[kk][:], func=AF.Exp, scale=-0.5), reads=[bst[kk]], writes=[bst[kk]])

                def soutb():
                    po, bpo = box["po"]
                    for i in range(4):
                        s.op("dve", lambda e, i=i: e.tensor_scalar(out=onb[kk][:, i, :], in0=po[:, i * 128:(i + 1) * 128],
                                                                   scalar1=mv[kk][:, i, 0:1], scalar2=rs[kk][:, i:i + 1],
                                                                   op0=ALU.subtract, op1=ALU.mult),
                             reads=[bpo, bst[kk]], writes=[bonb[kk]])
                    pb, bpb = self.bbank()

                    def ftr2(e):
                        ins = None
                        for i in range(4):
                            ins = e.transpose(out=pb[:, i * 128:(i + 1) * 128], in_=onb[kk][:, i, :], identity=self.identb[:])
                        return ins
                    s.op("pe", ftr2, reads=[bonb[kk], self.bconst], writes=[bpb])
                    s.op("dve", lambda e: e.tensor_tensor(out=self.Y[:, h, tg * 512:(tg + 1) * 512], in0=pb,
                                                          in1=self.Y[:, h, tg * 512:(tg + 1) * 512], op=ALU.mult),
                         reads=[bpb, self.bY[h][tg]], writes=[self.bY[h][tg]])
                return souts, souta, soutb
            outs = [make_out(tg) for tg in range(4)]
            return dict(sconst=sconst, skvk_f=skvk_f, skvm_f=skvm_f, schain_f=schain_f, skvk_b=skvk_b, skvm_b=skvm_b, schain_b=schain_b,
                        sqfb=sqfb, outs=outs)

        p0 = inproj_steps(0)
        for i in (0, 1, 6, 2, 3, 4, 5):
            p0[i]()
        rs_ = [rest_steps(h) for h in range(4)]
        for h in range(4):
            nxt = inproj_steps(h + 1) if h + 1 < 4 else [lambda: None] * 7
            r = rs_[h]
            r["sconst"]()
            r["sqfb"]()
            if h == 0:
                r["skvk_f"]()
            r["skvm_f"]()
            nxt[0]()
            nxt[1]()
            r["schain_f"]()
            r["skvk_b"]()
            nxt[2]()
            r["skvm_b"]()
            r["schain_b"]()
            nxt[3]()
            fill = [nxt[4], nxt[5], nxt[6], (lambda: None)]
            r["outs"][0][0]()
            for tg in range(4):
                sc, a, b = r["outs"][tg]
                if tg + 1 < 4:
                    r["outs"][tg + 1][0]()
                a()
                fill[tg]()
                if tg == 2 and h + 1 < 4:
                    rs_[h + 1]["skvk_f"]()
                b()
        self.dump_y(l * 4 + 1)
        self.s.barrier()
        self.out_proj(l, 1)

    def stage_mla(self, l):
        s = self.s
        self.scr_reset()
        scr = self.scr
        ALLB = (0, 1, 2, 3, 4, 5, 6)
        qlatn = scr("qlatn", [128, 3, SEQ], BF16)
        kvlatn = scr("kvlatn", [128, 2, SEQ], BF16)
        kr = scr("kr", [128, SEQ], BF16)
        bqlat = [Buf("qlat%d" % i) for i in range(4)]
        bkvlat = [Buf("kvlat%d" % i) for i in range(4)]
        bkr = Buf("kr")
        tblm_off = self.off
        tblm = [[scr("tblm", [128, 512], F32) for _ in range(2)] for _ in range(2)]
        btblm = [Buf("tblm0"), Buf("tblm1")]
        Vaug = [scr("Vaug", [128, NT, 128], BF16) for _ in range(2)]
        bV = [Buf("V0"), Buf("V1")]
        self.ptoff = self.off
        PT = scr("PT", [128, NPT, 512], BF16)
        bPT = [Buf("PT%d" % i) for i in range(NPT)]
        rsum_off = self.off
        rsum = scr("rsum", [128, 512], F32)
        brsum = Buf("rsum")
        rstdL_off = self.off
        rstdL = scr("rstdL", [128, 512], F32)
        brstdL = Buf("rstdL")
        rinv, brinv = rstdL, brstdL
        t2v = self.nc.alloc_sbuf_tensor_at("mt2v_%d" % l, [128, 512], F32, offset=rstdL_off)
        bt2 = brstdL
        hilo = self.nc.alloc_sbuf_tensor_at("hilo_%d" % l, [128, 2, 512], BF16, offset=tblm_off)
        bhilo = btblm[0]
        qn = [scr("qn", [128, SEQ], BF16) for _ in range(2)]
        qr = [scr("qr", [128, SEQ], BF16) for _ in range(2)]
        kn = [scr("kn", [128, SEQ], BF16) for _ in range(2)]
        bqn = [[Buf("qn%d_%d" % (k, i)) for i in range(4)] for k in range(2)]
        bqr = [[Buf("qr%d_%d" % (k, i)) for i in range(4)] for k in range(2)]
        bkn = [Buf("kn0"), Buf("kn1")]
        t1 = scr("mt1", [128, 512], F32)
        bt1 = Buf("mt1")
        wkrot = self.view(rsum_off, "wkrot", [128, 8, 128], BF16)
        bwkrot = brsum
        ntab = [0]

        s.op("pool", lambda e: e.memset(kr[0:64, :], 0.0), writes=[bkr])
        for hs in range(2):
            s.op("pool", lambda e, hs=hs: e.memset(qr[hs][0:64, :], 0.0), writes=bqr[hs])

        def load_tables(tb):
            k = ntab[0] % 2
            ntab[0] += 1
            sl = slice(tb * 512, (tb + 1) * 512)
            s.dma("sp", "tm%d" % k, lambda e, sem, k=k, sl=sl: (
                e.dma_start(out=tblm[k][0][64:128, :], in_=self.ropeM_d[0][:, sl]).then_inc(sem, 16),
                e.dma_start(out=tblm[k][1][64:128, :], in_=self.ropeM_d[1][:, sl]).then_inc(sem, 16)), 2, writes=[btblm[k]])
            return tblm[k][0], tblm[k][1], btblm[k]

        def rope64(p1, bp1, p2, bp2, tb, dst, bdst):
            cos, sin, btb = load_tables(tb)
            sl = slice(tb * 512, (tb + 1) * 512)
            s.op("dve", lambda e: e.tensor_tensor(out=t1[64:128, :], in0=p1[64:128, :], in1=cos[64:128, :], op=ALU.mult), reads=[bp1, btb], writes=[bt1])
            s.op("dve", lambda e: e.tensor_tensor(out=t2v[64:128, :], in0=p2[64:128, :], in1=sin[64:128, :], op=ALU.mult), reads=[bp2, btb], writes=[bt2])
            s.op("dve", lambda e: e.tensor_tensor(out=dst[64:128, sl], in0=t1[64:128, :], in1=t2v[64:128, :], op=ALU.add), reads=[bt1, bt2], writes=[bdst])

        def latent_norm(ws, nch, gbase, dstT, bdst):
            for tb in range(NTB):
                sl = slice(tb * 512, (tb + 1) * 512)
                raws = []
                for c in range(nch):
                    wt, bw, col = ws[c]
                    raws.append(self.inproj_fm(wt, bw, col, 128, tb, banks=(0, 1, 2, 4, 5, 6)))
                for c, (pt, bpt) in enumerate(raws):
                    s.op("act", lambda e, c=c, pt=pt: e.activation(out=PT[:, c, :], in_=pt[:], func=AF.Square), reads=[bpt], writes=[bPT[c]])
                psum, bpsum = self.bank((3,))
                self.mm_group(psum[:], [(self.onesb[:], PT[:, c, :]) for c in range(nch)], reads=[bPT[c] for c in range(nch)] + [self.bconst],
                              writes=[bpsum])
                s.op("dve", lambda e, psum=psum: e.tensor_scalar(out=rstdL[:], in0=psum[:], scalar1=1.0 / (128 * nch), scalar2=EPS,
                                                               op0=ALU.mult, op1=ALU.add), reads=[bpsum], writes=[brstdL])
                s.op("act", lambda e: e.activation(out=rstdL[:], in_=rstdL[:], func=AF.Ln), reads=[brstdL], writes=[brstdL])
                s.op("act", lambda e: e.activation(out=rstdL[:], in_=rstdL[:], func=AF.Exp, scale=-0.5), reads=[brstdL], writes=[brstdL])
                for c, (pt, bpt) in enumerate(raws):
                    s.op("dve", lambda e, c=c, pt=pt, sl=sl: e.scalar_tensor_tensor(out=dstT[:, c, sl], in0=pt[:], scalar=self.pcol(l, gbase, c),
                                                                                    in1=rstdL[:], op0=ALU.mult, op1=ALU.mult),
                         reads=[bpt, brstdL, self.bconst], writes=[bdst[tb]])

        (off1, bw1), (off2, bw2) = self.wpop(2)
        wq1 = self.view(off1, "wq1", [128, 8, 256], BF16)
        wq2 = self.view(off2, "wq2", [128, 8, 128], BF16)
        latent_norm([(wq1, bw1, 0), (wq1, bw1, 128), (wq2, bw2, 0)], 3, P_QG, qlatn, bqlat)
        off, bw = self.wpop()
        wkv = self.view(off, "wkv", [128, 8, 256], BF16)
        latent_norm([(wkv, bw, 0), (wkv, bw, 128)], 2, P_KG, kvlatn, bkvlat)
        off, bw = self.wpop()
        wkr = self.view(off, "wkr", [128, 8, 128], BF16)
        s.op("pool", lambda e: e.tensor_copy(out=wkrot[:, :, 0:64], in_=wkr[:, :, 0:64]), reads=[bw], writes=[bwkrot])
        s.op("pool", lambda e: e.tensor_scalar_mul(out=wkrot[:, :, 64:96], in0=wkr[:, :, 96:128], scalar1=-1.0), reads=[bw], writes=[bwkrot])
        s.op("pool", lambda e: e.tensor_copy(out=wkrot[:, :, 96:128], in_=wkr[:, :, 64:96]), reads=[bw], writes=[bwkrot])
        for tb in range(NTB):
            p1, bp1 = self.inproj_fm(wkr, bw, 0, 128, tb, banks=ALLB)
            p2, bp2 = self.inproj_fm(wkrot, bwkrot, 0, 128, tb, banks=ALLB)
            rope64(p1, bp1, p2, bp2, tb, kr, bkr)

        PREPB = (5, 6) if self.interleave else ALLB

        def prep_steps(h):
            hs = h % 2
            st_ = {}
            steps = []

            def s0():
                off, bw = self.wpop()
                st_["bw"] = bw
                st_["wa"] = wa = self.view(off, "wa", [128, 3, 192], BF16)
                st_["wc"] = self.view(off + 1536, "wc", [128, 2, 256], BF16)
                st_["wr"] = wr = self.view(off + 2560, "wr", [128, 3, 128], BF16)
                s.op("pool", lambda e: e.tensor_copy(out=wr[:, :, 0:64], in_=wa[:, :, 64:128]), reads=[bw], writes=[bw])
                s.op("pool", lambda e: e.tensor_scalar_mul(out=wr[:, :, 64:96], in0=wa[:, :, 160:192], scalar1=-1.0), reads=[bw], writes=[bw])
                s.op("pool", lambda e: e.tensor_copy(out=wr[:, :, 96:128], in_=wa[:, :, 128:160]), reads=[bw], writes=[bw])
            steps.append(s0)
            ksteps, qsteps, vsteps = [], [], []
            for tb in range(NTB):
                sl = slice(tb * 512, (tb + 1) * 512)

                def sqn(tb=tb, sl=sl):
                    bw, wa = st_["bw"], st_["wa"]
                    pt, bpt = self.bank(PREPB)
                    self.mm_group(pt[:], [(wa[:, kc, 0:128], qlatn[:, kc, sl]) for kc in range(3)], reads=[bw, bqlat[tb]], writes=[bpt])
                    s.op("act", lambda e: e.activation(out=qn[hs][:, sl], in_=pt[:], func=AF.Copy), reads=[bpt], writes=[bqn[hs][tb]])

                def sqr(tb=tb, sl=sl):
                    bw, wa, wr = st_["bw"], st_["wa"], st_["wr"]
                    p1, bp1 = self.bank(PREPB)
                    self.mm_group(p1[:], [(wa[:, kc, 64:192], qlatn[:, kc, sl]) for kc in range(3)], reads=[bw, bqlat[tb]], writes=[bp1])
                    p2, bp2 = self.bank(PREPB)
                    self.mm_group(p2[:], [(wr[:, kc, :], qlatn[:, kc, sl]) for kc in range(3)], reads=[bw, bqlat[tb]], writes=[bp2])
                    rope64(p1, bp1, p2, bp2, tb, qr[hs], bqr[hs][tb])

                def skn(tb=tb, sl=sl):
                    bw, wc = st_["bw"], st_["wc"]
                    pt, bpt = self.bank(PREPB)
                    self.mm_group(pt[:], [(wc[:, kc, 0:128], kvlatn[:, kc, sl]) for kc in range(2)], reads=[bw, bkvlat[tb]], writes=[bpt])
                    s.op("act", lambda e: e.activation(out=kn[hs][:, sl], in_=pt[:], func=AF.Copy), reads=[bpt], writes=[bkn[hs]])
                ksteps.append(skn)
                qsteps += [sqn, sqr]
            for tg in range(4):
                def sv(tg=tg):
                    bw, wc = st_["bw"], st_["wc"]
                    pt, bpt = self.bank(PREPB)

                    def fv(e):
                        ins = None
                        for i in range(4):
                            t = tg * 4 + i
                            for kc in range(2):
                                ins = e.matmul(pt[:, i * 128:(i + 1) * 128], lhsT=kvlatn[:, kc, t * 128:(t + 1) * 128], rhs=wc[:, kc, 128:256],
                                               start=(kc == 0), stop=(kc == 1))
                        return ins
                    s.op("pe", fv, reads=[bw, bkvlat[tg]], writes=[bpt])
                    s.op("act", lambda e: e.activation(out=Vaug[hs][:, tg * 4:(tg + 1) * 4, :],
                                                       in_=pt[:].rearrange("p (a b) -> p a b", a=4), func=AF.Copy),
                         reads=[bpt], writes=[bV[hs]])
                vsteps.append(sv)
            return steps + ksteps + vsteps + qsteps

        psbF = self.psb[:].bitcast(F32)
        sbanks = [(self.ps[0], self.bps[0]), (self.ps[1], self.bps[1]), (self.ps[6], self.bps[6]), (psbF, self.bpsb[0])]

        def attention(h, pending):
            hs = h % 2
            hh = h % 4
            fin = {"pe": None, "tail": None}
            for qb in range(4):
                qsl = slice(qb * 512, (qb + 1) * 512)
                sbank = 4 + (self.nqb % 2)
                self.nqb += 1
                pss, bpss = self.ps[sbank], self.bps[sbank]

                def emit_s(kt, qsl=qsl):
                    ps_, bps_ = sbanks[self.sctr % 4]
                    self.sctr += 1

                    def fs(e, kt=kt, ps_=ps_, qsl=qsl):
                        e.matmul(ps_[:], lhsT=kn[hs][:, kt * 128:(kt + 1) * 128], rhs=qn[hs][:, qsl], start=True, stop=False)
                        return e.matmul(ps_[:], lhsT=kr[:, kt * 128:(kt + 1) * 128], rhs=qr[hs][:, qsl], start=False, stop=True)
                    s.op("pe", fs, reads=[bkn[hs], bqn[hs][qb], bkr, bqr[hs][qb]], writes=[bps_])
                    return ps_, bps_
                acc, bacc = self.bank((2, 3))
                sq_ = [emit_s(0), emit_s(1), emit_s(2), emit_s(3)]
                if fin["pe"]:
                    fin["pe"]()
                    fin["pe"] = None
                for kt in range(NT):
                    ps_, bps_ = sq_[kt]
                    pi = (self.nqb * NT + kt) % NPT
                    s.op("act", lambda e, ps_=ps_, pi=pi: e.activation(out=PT[:, pi, :], in_=ps_[:], func=AF.Exp, scale=MLA_SCALE),
                         reads=[bps_], writes=[bPT[pi]])
                    if kt + 4 < NT:
                        sq_.append(emit_s(kt + 4))
                    s.op("pe", lambda e, kt=kt, pi=pi, acc=acc: e.matmul(acc[:], lhsT=Vaug[hs][:, kt, :], rhs=PT[:, pi, :],
                                                                         start=(kt == 0), stop=(kt == NT - 1)),
                         reads=[bPT[pi], bV[hs]], writes=[bacc])
                    if kt % 2 == 0:
                        if kt == 0:
                            s.op("dve", lambda e, pi=pi: e.tensor_copy(out=rsum[:], in_=PT[:, pi, :]), reads=[bPT[pi]], writes=[brsum])
                        else:
                            s.op("dve", lambda e, pi=pi: e.tensor_tensor(out=rsum[:], in0=rsum[:], in1=PT[:, pi, :], op=ALU.add),
                                 reads=[bPT[pi], brsum], writes=[brsum])
                    else:
                        s.op("pe", lambda e, kt=kt, pi=pi, pss=pss: e.matmul(pss[:], lhsT=self.onesb[:], rhs=PT[:, pi, :], start=(kt == 1), stop=False),
                             reads=[bPT[pi], self.bconst], writes=[bpss])
                    if kt == 1 and fin["tail"]:
                        fin["tail"]()
                        fin["tail"] = None
                    if pending:
                        pending.pop(0)()
                s.op("dve", lambda e: e.tensor_copy(out=hilo[:, 0, :], in_=rsum[:]), reads=[brsum], writes=[bhilo])
                s.op("dve", lambda e: e.tensor_tensor(out=hilo[:, 1, :], in0=rsum[:], in1=hilo[:, 0, :], op=ALU.subtract),
                     reads=[brsum, bhilo], writes=[bhilo])

                def fin_pe(pss=pss, bpss=bpss):
                    self.mm_group(pss[:], [(self.onesb[:], hilo[:, 0, :]), (self.onesb[:], hilo[:, 1, :])], reads=[bhilo, self.bconst],
                                  writes=[bpss], first=False, last=True)

                def fin_tail(pss=pss, bpss=bpss, acc=acc, bacc=bacc, qsl=qsl, qb=qb):
                    s.op("act", lambda e: e.activation(out=rinv[:], in_=pss[:], func=AF.Ln), reads=[bpss], writes=[brinv])
                    s.op("act", lambda e: e.activation(out=rinv[:], in_=rinv[:], func=AF.Exp, scale=-1.0), reads=[brinv], writes=[brinv])
                    s.op("dve", lambda e: e.tensor_tensor(out=rinv[:], in0=rinv[:], in1=self.Y[:, hh, qsl], op=ALU.mult),
                         reads=[brinv, self.bY[hh][qb]], writes=[brinv])
                    s.op("dve", lambda e: e.tensor_tensor(out=self.Y[:, hh, qsl], in0=acc[:], in1=rinv[:], op=ALU.mult),
                         reads=[bacc, brinv, self.bY[hh][qb]], writes=[self.bY[hh][qb]])
                fin["pe"], fin["tail"] = fin_pe, fin_tail
            fin["pe"]()
            fin["tail"]()

        self.gates(l)
        for st in prep_steps(0):
            st()
        for h in range(8):
            pending = prep_steps(h + 1) if h + 1 < 8 else []
            attention(h, pending if self.interleave else [])
            for st in pending:
                st()
            if h % 4 == 3:
                g = h // 4
                self.dump_y(l * 4 + 2 + g)
                if g == 1:
                    self.s.barrier()
                self.out_proj(l, 2 + g)
                if g == 0:
                    self.gates(l)
        if l == 0:
            self.dump("qlatn", qlatn, bqlat); self.dump("kvlatn", kvlatn, bkvlat); self.dump("kr", kr, [bkr])
            self.dump("qn1", qn[1], bqn[1]); self.dump("qr1", qr[1], bqr[1]); self.dump("kn1", kn[1], [bkn[1]])
            self.dump("V1", Vaug[1], [bV[1]]); self.dump("PT", PT, bPT)

    def build(self):
        self.setup()
        for l in range(self.n_layers):
            self.plan_layer(l)
        for l in range(self.n_layers):
            self.stage_norm(l)
            if "conv" in self.stages:
                self.stage_conv(l)
            if "ret" in self.stages:
                self.stage_ret(l)
            if "mla" in self.stages:
                self.stage_mla(l)
        self.stage_final()
        self.s.emit()
        return self.nc

    def stage_final(self):
        s = self.s
        self.scr_reset()
        evs = list(self.dbg_evs)
        if not self.final:
            for t in range(NT):
                evs.append(s.dma("sp", "o%d" % (t % 4), lambda e, sem, t=t: e.dma_start(
                    out=self.out_d[t * 128:(t + 1) * 128, :], in_=self.X[:, t, :]).then_inc(sem, 16), 1, reads=[self.bX[t]]))
            s.wait("sp", evs)
            return
        self.nview += 1
        junk = self.nc.alloc_sbuf_tensor_at("junkf_%d" % self.nview, [128, D_MODEL], BF16, offset=self.y_off + 8192)
        bjunk = None
        jw = self.bY[2][0:2]
        fg = self.nc.alloc_sbuf_tensor_at("fg_%d" % self.nview, [128, D_MODEL], F32, offset=self.y_off)
        bfg = Buf("fg")
        s.dma("sp", "cst", lambda e, sem: e.dma_start(out=fg[:], in_=self.fgb_d[:, :]).then_inc(sem, 16), 1, writes=[bfg] + self.bY[0])
        for tb in range(NTB):
            tiles = list(range(tb * 4, tb * 4 + 4))
            self.row_rstd(tiles, junk, bjunk, extra_w=jw)
            for t in tiles:
                s.op("dve", lambda e, t=t: e.scalar_tensor_tensor(out=self.X[:, t, :], in0=self.X[:, t, :], scalar=self.ss[:, t:t + 1],
                                                                  in1=fg[:], op0=ALU.mult, op1=ALU.mult),
                     reads=[self.bX[t], self.bss[tb], bfg], writes=[self.bX[t]])
                evs.append(s.dma("sp", "o%d" % (t % 4), lambda e, sem, t=t: e.dma_start(
                    out=self.out_d[t * 128:(t + 1) * 128, :], in_=self.X[:, t, :]).then_inc(sem, 16), 1, reads=[self.bX[t]]))
        s.wait("sp", evs)


def host_consts():
    f32 = np.float32
    cst = np.zeros((128, NCST), f32)
    i = np.arange(128)
    cst[:, C_ID:C_ID + 128] = np.eye(128)
    cst[:, C_P1:C_P1 + 128] = np.maximum(i[None, :] - i[:, None], 0)
    cst[:, C_P2:C_P2 + 128] = np.maximum(i[:, None] - i[None, :], 0)
    cst[:, C_I1:C_I1 + 128] = (i + 1)[None, :]
    cst[:, C_I2:C_I2 + 128] = (128 - i)[None, :]
    cst[:, C_KJ] = 127 - i
    cst[:, C_KJ + 1] = i
    pos = np.arange(SEQ, dtype=f32)

    def tables(dim, rows):
        inv = (1.0 / (10000.0 ** (np.arange(0, dim, 2, dtype=f32) / f32(dim)))).astype(f32)
        ang = (pos[:, None] * inv[None, :]).astype(f32)
        ang = np.concatenate([ang, ang], axis=-1)
        return np.stack([np.cos(ang).astype(f32).T, np.sin(ang).astype(f32).T])[:, :rows, :].copy()
    return cst, tables(128, 128), tables(64, 64)


def host_params(inp):
    p = np.zeros((128, NP), np.float32)
    for l in range(DEPTH):
        b = l * PL
        p[:, b + P_NG:b + P_NG + 8] = inp["norm_g"][l].reshape(8, 128).T
        p[:, b + P_CW:b + P_CW + 124] = inp["conv_dw_w"][l].reshape(CONV_K, 4, 128).transpose(2, 0, 1).reshape(128, 124)
        p[:, b + P_CB:b + P_CB + 4] = inp["conv_dw_b"][l].reshape(4, 128).T
        p[:, b + P_LG:b + P_LG + 4] = inp["conv_ln_g"][l].reshape(4, 128).T
        p[:, b + P_LB:b + P_LB + 4] = inp["conv_ln_b"][l].reshape(4, 128).T
        p[:, b + P_QG:b + P_QG + 3] = inp["mla_qa_g"][l].reshape(3, 128).T
        p[:, b + P_KG:b + P_KG + 2] = inp["mla_kva_g"][l].reshape(2, 128).T
        p[:, b + P_DL:b + P_DL + 8] = inp["ret_decay_logit"][l].reshape(1, 8)
    p[:, P_FG:P_FG + 8] = inp["final_g"].reshape(8, 128).T
    return p


_CACHE = {}


def run(inputs, n_cores=8, **kw):
    key = tuple(sorted(kw.items()))
    if key not in _CACHE:
        _CACHE[key] = Prog(**kw)
        _CACHE[key].build()
    prog = _CACHE[key]
    inp = {k: np.ascontiguousarray(np.asarray(v, dtype=np.float32)) for k, v in inputs.items()}
    cst, ropeR, ropeM = host_consts()
    par = host_params(inp)
    fgb = np.ascontiguousarray(np.broadcast_to(inp["final_g"][None, :], (128, D_MODEL)))
    common = {"w_in": inp["w_in"], "w_uq": inp["mla_w_uq"], "w_ukv": inp["mla_w_ukv"], "w_out": inp["w_out"],
              "params": par, "consts": cst, "fgb": fgb, "ropeR": ropeR, "ropeM": ropeM}
    in_maps = [dict(common, x=inp["x"][b]) for b in range(n_cores)]
    res = run_bass_kernel_spmd(prog.nc, in_maps, core_ids=list(range(n_cores)))
    return res


def kernel(**inputs):
    res = run(inputs)
    return np.stack([r["out"] for r in res.results], axis=0).astype(np.float32)
```
